# Optimizing a Trainium2 kernel written in Bass

```python
import math
import jax, jax.numpy as jnp
from jax import lax
import numpy as np

D_MODEL = 2048
BATCH = 4
SEQ = 2048
DEPTH = 1

MIX_WIDTH = D_MODEL
GLA_WIDTH = MIX_WIDTH // 2
NSA_WIDTH = MIX_WIDTH - GLA_WIDTH
GLA_HEADS = 4
GLA_DK = (GLA_WIDTH // 2) // GLA_HEADS
GLA_DV = GLA_WIDTH // GLA_HEADS
GLA_GATE_RANK = 16
GLA_GATE_NORM = 16.0
GLA_CHUNK = 64
NSA_HEADS = 8
NSA_HD = NSA_WIDTH // NSA_HEADS
NSA_KV_GROUPS = 2
NSA_HPG = NSA_HEADS // NSA_KV_GROUPS
CMP_BLOCK = 32
CMP_STRIDE = 16
CMP_HIDDEN = 2 * NSA_HD
SLC_BLOCK = 64
SLC_TOPK = 16
SLC_QCHUNK = 64
WINDOW = 512
WIN_QBLOCK = 128
ROPE_THETA = 500000.0
ROPE_DIM = NSA_HD // 4
FFN_HIDDEN = -(-8 * D_MODEL // (3 * 256)) * 256
DEEPNORM_ALPHA = (2.0 * DEPTH) ** 0.25
DEEPNORM_BETA = (8.0 * DEPTH) ** -0.25
LN_EPS = 1e-5
FORCED_SCORE = 1e4
INVALID_SCORE = -1e4

IN_SIZES = (
    GLA_HEADS * GLA_DK,
    GLA_HEADS * GLA_DK,
    GLA_HEADS * GLA_DV,
    GLA_HEADS * GLA_DV,
    GLA_GATE_RANK,
    NSA_HEADS * NSA_HD,
    NSA_KV_GROUPS * NSA_HD,
    NSA_KV_GROUPS * NSA_HD,
    NSA_KV_GROUPS * NSA_HD,
    NSA_KV_GROUPS * NSA_HD,
    NSA_KV_GROUPS * NSA_HD,
    NSA_KV_GROUPS * NSA_HD,
    NSA_HEADS * 3,
)
IN_WIDTH = sum(IN_SIZES)

kernel_name = "hymba_gla_nsa_deepnorm_layer"


def layer_norm(x, g, b):
    xf = x.astype(jnp.float32)
    mu = jnp.mean(xf, axis=-1, keepdims=True)
    var = jnp.mean(jnp.square(xf - mu), axis=-1, keepdims=True)
    y = (xf - mu) * lax.rsqrt(var + LN_EPS) * g.astype(jnp.float32) + b.astype(jnp.float32)
    return y.astype(x.dtype)


def masked_softmax(s, mask):
    s = jnp.where(mask, s.astype(jnp.float32), -jnp.inf)
    m = jnp.max(s, axis=-1, keepdims=True)
    m = jnp.where(jnp.isfinite(m), m, 0.0)
    e = jnp.where(mask, jnp.exp(s - m), 0.0)
    den = jnp.sum(e, axis=-1, keepdims=True)
    return e / jnp.where(den > 0, den, 1.0)


def rope_tables(S):
    pos = jnp.arange(S, dtype=jnp.float32)
    inv = jnp.power(ROPE_THETA, -jnp.arange(0, ROPE_DIM, 2, dtype=jnp.float32) / ROPE_DIM)
    ang = pos[:, None] * inv[None, :]
    return jnp.cos(ang), jnp.sin(ang)


def apply_partial_rope(x, cos, sin):
    half = ROPE_DIM // 2
    c = cos[None, :, None, :].astype(x.dtype)
    s = sin[None, :, None, :].astype(x.dtype)
    x1 = x[..., :half]
    x2 = x[..., half:ROPE_DIM]
    return jnp.concatenate([x1 * c - x2 * s, x2 * c + x1 * s, x[..., ROPE_DIM:]], axis=-1)


def gla_mixer(q, k, v, gk, g_out, norm_w):
    B, S, H, dk = q.shape
    dv = v.shape[-1]
    C = GLA_CHUNK
    nC = S // C

    def chunks(t):
        return t.astype(jnp.float32).reshape(B, nC, C, H, t.shape[-1]).transpose(0, 3, 1, 2, 4)

    qc = chunks(q) * (dk ** -0.5)
    kc = chunks(k)
    vc = chunks(v)
    bc = jnp.cumsum(chunks(gk), axis=3)
    b_last = bc[:, :, :, -1:, :]
    q_dec = qc * jnp.exp(bc)
    k_intra = kc * jnp.exp(-bc)
    k_state = kc * jnp.exp(b_last - bc)
    decay = jnp.exp(b_last[:, :, :, 0, :])

    causal = jnp.tril(jnp.ones((C, C), dtype=bool))
    attn = jnp.where(causal, jnp.einsum('bhnid,bhnjd->bhnij', q_dec, k_intra), 0.0)
    o_intra = jnp.einsum('bhnij,bhnje->bhnie', attn, vc)

    def step(state, inp):
        qd, ks, vv, dec = inp
        o = jnp.einsum('bhid,bhde->bhie', qd, state)
        state = dec[..., None] * state + jnp.einsum('bhjd,bhje->bhde', ks, vv)
        return state, o

    xs = (jnp.moveaxis(q_dec, 2, 0), jnp.moveaxis(k_state, 2, 0),
          jnp.moveaxis(vc, 2, 0), jnp.moveaxis(decay, 2, 0))
    state0 = jnp.zeros((B, H, dk, dv), jnp.float32)
    _, o_inter = lax.scan(step, state0, xs)
    o = o_intra + jnp.moveaxis(o_inter, 0, 2)
    o = o.transpose(0, 2, 3, 1, 4).reshape(B, S, H, dv)
    o = o * lax.rsqrt(jnp.mean(jnp.square(o), axis=-1, keepdims=True) + LN_EPS) * norm_w.astype(jnp.float32)
    o = o.reshape(B, S, H * dv) * jax.nn.silu(g_out.astype(jnp.float32))
    return o.astype(q.dtype)


def compress_blocks(kv, pos_emb, w1, w2):
    B, S, G, hd = kv.shape
    n_cmp = (S - CMP_BLOCK) // CMP_STRIDE + 1
    idx = np.arange(n_cmp)[:, None] * CMP_STRIDE + np.arange(CMP_BLOCK)[None, :]
    blocks = kv[:, idx] + pos_emb[None, None, :, None, :]
    flat = blocks.transpose(0, 1, 3, 2, 4).reshape(B, n_cmp, G, CMP_BLOCK * hd)
    return jax.nn.gelu(flat @ w1) @ w2


def nsa_mixer(q, kc, vc, ks, vs, kw, vw, gate_logits, cmp_k_pos, cmp_k_w1, cmp_k_w2,
              cmp_v_pos, cmp_v_w1, cmp_v_w2, cos, sin):
    B, S, H, hd = q.shape
    G, R = NSA_KV_GROUPS, NSA_HPG
    scale = hd ** -0.5
    t = jnp.arange(S)

    k_cmp = compress_blocks(kc, cmp_k_pos, cmp_k_w1, cmp_k_w2)
    v_cmp = compress_blocks(vc, cmp_v_pos, cmp_v_w1, cmp_v_w2)
    n_cmp = k_cmp.shape[1]
    cmp_end = jnp.arange(n_cmp) * CMP_STRIDE + CMP_BLOCK - 1
    qg = q.reshape(B, S, G, R, hd)
    s_cmp = jnp.einsum('bsgrd,bngd->bsgrn', qg, k_cmp) * scale
    cmp_mask = (cmp_end[None, :] <= t[:, None])[None, :, None, None, :]
    p_cmp = masked_softmax(s_cmp, cmp_mask)
    o_cmp = jnp.einsum('bsgrn,bngd->bsgrd', p_cmp, v_cmp.astype(jnp.float32))

    nb = S // SLC_BLOCK
    k_sel = min(SLC_TOPK, nb)
    c_start = np.arange(n_cmp) * CMP_STRIDE
    b_start = np.arange(nb) * SLC_BLOCK
    overlap = ((c_start[:, None] < b_start[None, :] + SLC_BLOCK) &
               (b_start[None, :] < c_start[:, None] + CMP_BLOCK)).astype(np.float32)
    imp = jnp.einsum('bsgrn,nj->bsgj', p_cmp, jnp.asarray(overlap))
    blk = jnp.arange(nb)
    cur = t // SLC_BLOCK
    valid = blk[None, :] <= cur[:, None]
    forced = (blk[None, :] == 0) | (blk[None, :] == cur[:, None]) | (blk[None, :] == cur[:, None] - 1)
    imp = jnp.where(valid[None, :, None, :], imp, INVALID_SCORE)
    imp = jnp.where(forced[None, :, None, :], FORCED_SCORE, imp)
    _, sel_idx = lax.top_k(imp, k_sel)

    q_r = apply_partial_rope(q, cos, sin).reshape(B, S, G, R, hd)
    k_s = apply_partial_rope(ks, cos, sin)
    k_blocks = k_s.reshape(B, nb, SLC_BLOCK, G, hd).transpose(0, 3, 1, 2, 4)
    v_blocks = vs.reshape(B, nb, SLC_BLOCK, G, hd).transpose(0, 3, 1, 2, 4)
    nQ = S // SLC_QCHUNK
    bi = jnp.arange(B)[:, None, None, None]
    gi = jnp.arange(G)[None, None, :, None]

    def sel_chunk(args):
        qc_, idx, tq = args
        Tq = qc_.shape[1]
        kg = k_blocks[bi, gi, idx]
        vg = v_blocks[bi, gi, idx]
        kpos = idx[..., None] * SLC_BLOCK + jnp.arange(SLC_BLOCK)
        mask = (kpos <= tq[None, :, None, None, None])[:, :, :, None].reshape(B, Tq, G, 1, k_sel * SLC_BLOCK)
        s = jnp.einsum('bqgrd,bqgjcd->bqgrjc', qc_, kg) * scale
        p = masked_softmax(s.reshape(B, Tq, G, R, k_sel * SLC_BLOCK), mask)
        p = p.reshape(B, Tq, G, R, k_sel, SLC_BLOCK)
        return jnp.einsum('bqgrjc,bqgjcd->bqgrd', p, vg.astype(jnp.float32))

    q_chunks = jnp.moveaxis(q_r.reshape(B, nQ, SLC_QCHUNK, G, R, hd), 1, 0)
    i_chunks = jnp.moveaxis(sel_idx.reshape(B, nQ, SLC_QCHUNK, G, k_sel), 1, 0)
    t_chunks = t.reshape(nQ, SLC_QCHUNK)
    o_slc = lax.map(sel_chunk, (q_chunks, i_chunks, t_chunks))
    o_slc = jnp.moveaxis(o_slc, 0, 1).reshape(B, S, G, R, hd)

    k_w = apply_partial_rope(kw, cos, sin)
    nWQ = S // WIN_QBLOCK
    n_pad_blk = WINDOW // WIN_QBLOCK
    nkb = n_pad_blk + 1

    def band(a):
        a = jnp.pad(a, ((0, 0), (WINDOW, 0), (0, 0), (0, 0))).reshape(B, nWQ + n_pad_blk, WIN_QBLOCK, G, hd)
        return jnp.concatenate([a[:, j:j + nWQ] for j in range(nkb)], axis=2)

    kwin = band(k_w)
    vwin = band(vw)
    qw = q_r.reshape(B, nWQ, WIN_QBLOCK, G, R, hd)
    qpos = t.reshape(nWQ, WIN_QBLOCK)
    kpos = jnp.arange(nWQ)[:, None] * WIN_QBLOCK + jnp.arange(nkb * WIN_QBLOCK)[None, :] - WINDOW
    diff = qpos[:, :, None] - kpos[:, None, :]
    win_mask = (kpos[:, None, :] >= 0) & (diff >= 0) & (diff < WINDOW)
    s_win = jnp.einsum('bnqgrd,bnkgd->bnqgrk', qw, kwin) * scale
    p_win = masked_softmax(s_win, win_mask[None, :, :, None, None, :])
    o_win = jnp.einsum('bnqgrk,bnkgd->bnqgrd', p_win, vwin.astype(jnp.float32)).reshape(B, S, G, R, hd)

    g = jax.nn.sigmoid(gate_logits.astype(jnp.float32)).reshape(B, S, G, R, 3)
    o = g[..., 0:1] * o_cmp + g[..., 1:2] * o_slc + g[..., 2:3] * o_win
    return o.reshape(B, S, H * hd).astype(q.dtype)


def setup_inputs(seed: int = 0) -> dict:
    key = jax.random.key(seed)
    ks = jax.random.split(key, 20)
    L = DEPTH

    def nrm(k, shape, scale):
        return jax.random.normal(k, shape, jnp.float32) * scale

    return {
        "x": nrm(ks[0], (BATCH, SEQ, D_MODEL), 1.0),
        "w_in": nrm(ks[1], (L, D_MODEL, IN_WIDTH), D_MODEL ** -0.5),
        "gla_gate_w2": nrm(ks[2], (L, GLA_GATE_RANK, GLA_HEADS * GLA_DK), GLA_GATE_RANK ** -0.5),
        "gla_gate_b2": nrm(ks[3], (L, GLA_HEADS * GLA_DK), 0.1),
        "gla_norm_w": 1.0 + nrm(ks[4], (L, GLA_DV), 0.02),
        "cmp_k_pos": nrm(ks[5], (L, CMP_BLOCK, NSA_HD), 0.1),
        "cmp_k_w1": nrm(ks[6], (L, CMP_BLOCK * NSA_HD, CMP_HIDDEN), (CMP_BLOCK * NSA_HD) ** -0.5),
        "cmp_k_w2": nrm(ks[7], (L, CMP_HIDDEN, NSA_HD), CMP_HIDDEN ** -0.5),
        "cmp_v_pos": nrm(ks[8], (L, CMP_BLOCK, NSA_HD), 0.1),
        "cmp_v_w1": nrm(ks[9], (L, CMP_BLOCK * NSA_HD, CMP_HIDDEN), (CMP_BLOCK * NSA_HD) ** -0.5),
        "cmp_v_w2": nrm(ks[10], (L, CMP_HIDDEN, NSA_HD), CMP_HIDDEN ** -0.5),
        "w_out": nrm(ks[11], (L, MIX_WIDTH, D_MODEL), MIX_WIDTH ** -0.5 * DEEPNORM_BETA),
        "ln1_g": 1.0 + nrm(ks[12], (L, D_MODEL), 0.02),
        "ln1_b": nrm(ks[13], (L, D_MODEL), 0.02),
        "ffn_w1": nrm(ks[14], (L, D_MODEL, FFN_HIDDEN), D_MODEL ** -0.5),
        "ffn_w3": nrm(ks[15], (L, D_MODEL, FFN_HIDDEN), D_MODEL ** -0.5),
        "ffn_w2": nrm(ks[16], (L, FFN_HIDDEN, D_MODEL), FFN_HIDDEN ** -0.5 * DEEPNORM_BETA),
        "ln2_g": 1.0 + nrm(ks[17], (L, D_MODEL), 0.02),
        "ln2_b": nrm(ks[18], (L, D_MODEL), 0.02),
    }


def reference(x, w_in, gla_gate_w2, gla_gate_b2, gla_norm_w, cmp_k_pos, cmp_k_w1, cmp_k_w2,
              cmp_v_pos, cmp_v_w1, cmp_v_w2, w_out, ln1_g, ln1_b, ffn_w1, ffn_w3, ffn_w2,
              ln2_g, ln2_b):
    B, S, _ = x.shape
    cos, sin = rope_tables(S)
    split_points = []
    acc = 0
    for size in IN_SIZES[:-1]:
        acc += size
        split_points.append(acc)

    for l in range(DEPTH):
        u = jnp.einsum('bsd,de->bse', x, w_in[l])
        (g_q, g_k, g_v, g_o, g_lr, n_q, n_kc, n_vc, n_ks, n_vs, n_kw, n_vw, n_gate) = jnp.split(u, split_points, axis=-1)

        gk = jax.nn.log_sigmoid((g_lr @ gla_gate_w2[l] + gla_gate_b2[l]).astype(jnp.float32)) / GLA_GATE_NORM
        o_gla = gla_mixer(g_q.reshape(B, S, GLA_HEADS, GLA_DK), g_k.reshape(B, S, GLA_HEADS, GLA_DK),
                          g_v.reshape(B, S, GLA_HEADS, GLA_DV), gk.reshape(B, S, GLA_HEADS, GLA_DK),
                          g_o, gla_norm_w[l])

        kvs = lambda a: a.reshape(B, S, NSA_KV_GROUPS, NSA_HD)
        o_nsa = nsa_mixer(n_q.reshape(B, S, NSA_HEADS, NSA_HD), kvs(n_kc), kvs(n_vc), kvs(n_ks), kvs(n_vs),
                          kvs(n_kw), kvs(n_vw), n_gate.reshape(B, S, NSA_HEADS, 3),
                          cmp_k_pos[l], cmp_k_w1[l], cmp_k_w2[l], cmp_v_pos[l], cmp_v_w1[l], cmp_v_w2[l],
                          cos, sin)

        mix = jnp.einsum('bse,ed->bsd', jnp.concatenate([o_gla, o_nsa], axis=-1), w_out[l])
        h = layer_norm(DEEPNORM_ALPHA * x + mix, ln1_g[l], ln1_b[l])
        ffn = (jax.nn.silu(h @ ffn_w1[l]) * (h @ ffn_w3[l])) @ ffn_w2[l]
        x = layer_norm(DEEPNORM_ALPHA * h + ffn, ln2_g[l], ln2_b[l])
    return x
```

```python
import math
from contextlib import ExitStack

import numpy as np
import concourse.bass as bass
import concourse.mybir as mybir
from concourse.bass_utils import run_bass_kernel_spmd

F32 = mybir.dt.float32
BF16 = mybir.dt.bfloat16
AF = mybir.ActivationFunctionType
ALU = mybir.AluOpType
AX = mybir.AxisListType

D = 2048
SEQ = 2048
NB = 2048
NO = 1024
FF = 5632
NFC = FF // 128
ALPHA = 2.0 ** 0.25
EPS = 1e-5
NEG = -30000.0
EXTRA_DBG = []
SKIP = set()
GLA_H = 4
GLA_T = 16
NSA_BR = {'slc', 'win'}
SCALE = 128.0 ** -0.5

O_GQ, O_GK, O_GV, O_GO, O_GLR = 0, 512, 1024, 2048, 3072
O_NQ = 3088
O_KC, O_VC, O_KS, O_VS, O_KW, O_VW = 4112, 4368, 4624, 4880, 5136, 5392
O_GATE = 5648


class Buf:
    __slots__ = ("name", "w", "r", "excl")

    def __init__(self, name="", excl=False):
        self.name = name
        self.w = None
        self.r = {}
        self.excl = excl


class Sched:
    ENGS = ("pe", "act", "dve", "pool", "sp")
    NDS = 10

    def __init__(self, nc, es):
        self.nc = nc
        self.q = {e: [] for e in self.ENGS}
        self.cnt = {e: 0 for e in self.ENGS}
        self.waited = {e: {} for e in self.ENGS}
        self.dma_state = {qn: {"rr": 0, "n": [0] * self.NDS} for qn in ("sp", "pool", "act")}
        self.sems = {}
        for e in self.ENGS:
            self.sems["e_" + e] = es.enter_context(nc.semaphore("e_" + e))
        for qn in ("sp", "pool"):
            for k in range(self.NDS):
                sk = "d_%s_%d" % (qn, k)
                self.sems[sk] = es.enter_context(nc.semaphore(sk))

    def _collect(self, eng, reads, writes):
        deps = {}

        def add(tok, kind):
            if tok is None:
                return
            sk, val, e2 = tok
            if e2 == eng and eng == "pe":
                return
            if deps.get(sk, 0) < val:
                deps[sk] = val

        for b in reads:
            add(b.w, "raw")
            if b.excl:
                for sk, (val, e2) in b.r.items():
                    if e2 != eng:
                        add((sk, val, e2), "rar")
        for b in writes:
            add(b.w, "waw")
            for sk, (val, e2) in b.r.items():
                add((sk, val, e2), "war")
        waits = []
        wd = self.waited[eng]
        for sk, val in deps.items():
            if wd.get(sk, 0) >= val:
                continue
            wd[sk] = val
            waits.append((sk, val))
        return waits

    def _commit(self, tok, reads, writes):
        sk, val, e = tok
        for b in reads:
            old = b.r.get(sk)
            if old is None or old[0] < val:
                b.r[sk] = (val, e)
        for b in writes:
            b.w = tok
            b.r = {}

    LIMIT = None
    total = 0
    lines = []

    def _skip(self):
        import sys
        Sched.total += 1
        f = sys._getframe(2)
        Sched.lines.append(f.f_lineno)
        return Sched.LIMIT is not None and Sched.total > Sched.LIMIT

    def op(self, eng, fn, reads=(), writes=()):
        if self._skip():
            return None
        waits = self._collect(eng, reads, writes)
        self.cnt[eng] += 1
        tok = ("e_" + eng, self.cnt[eng], eng)
        self.q[eng].append((waits, fn, ("e_" + eng, 1)))
        self._commit(tok, reads, writes)
        return tok

    def dma(self, qn, fn, reads=(), writes=()):
        if self._skip():
            return None
        st = self.dma_state[qn]
        k = st["rr"]
        st["rr"] = (k + 1) % self.NDS
        sk = "d_%s_%d" % (qn, k)
        waits = self._collect(qn, reads, writes)
        prev = st["n"][k] * 16
        wd = self.waited[qn]
        if prev > 0 and wd.get(sk, 0) < prev:
            wd[sk] = prev
            waits.append((sk, prev))
        st["n"][k] += 1
        tok = (sk, st["n"][k] * 16, "dma_" + qn)
        self.q[qn].append((waits, fn, (sk, 16)))
        self._commit(tok, reads, writes)
        return tok

    def barrier(self):
        toks = []
        for e in self.ENGS:
            if self.cnt[e] > 0:
                toks.append(("e_" + e, self.cnt[e]))
        for qn, st in self.dma_state.items():
            for k, n in enumerate(st["n"]):
                if n > 0:
                    toks.append(("d_%s_%d" % (qn, k), n * 16))
        for e in self.ENGS:
            waits = []
            for sk, val in toks:
                if sk == "e_" + e:
                    continue
                if self.waited[e].get(sk, 0) < val:
                    self.waited[e][sk] = val
                    waits.append((sk, val))
            self.q[e].append((waits, None, None))

    def emit(self):
        nc = self.nc
        sems = self.sems
        with nc.Block() as block:
            def run(engname):
                def body(engine):
                    for waits, fn, inc in self.q[engname]:
                        for sk, val in waits:
                            engine.wait_ge(sems[sk], val)
                        if fn is not None:
                            fn(engine).then_inc(sems[inc[0]], inc[1])
                return body

            block.tensor(run("pe"))
            block.scalar(run("act"))
            block.vector(run("dve"))
            block.gpsimd(run("pool"))
            block.sync(run("sp"))
        self.q = {e: [] for e in self.ENGS}


def build_nc(dbg=()):
    nc = bass.Bass("TRN2", target_bir_lowering=False)

    def din(name, shape, dt=F32):
        return nc.dram_tensor(name, list(shape), dt, kind="ExternalInput").ap()

    xT_d = din("xT", [128, 16, NB])
    xo_d = din("xo", [8, 128, D])
    wg_d = din("wg", [4, 128, 16 * 768])
    wglr_d = din("wglr", [128, 16 * 16])
    w2aug_d = din("w2aug", [17, 512])
    normw_d = din("normw", [128, 256])
    wn_d = din("wn", [16, 128, 16 * 128])
    wv_d = din("wv", [2, 128, 16 * 256])
    wgate_d = din("wgate", [128, 16 * 24])
    w1c_d = din("w1c", [2, 128, 32 * 256])
    w2c_d = din("w2c", [2, 128, 2 * 128])
    posc_d = din("posc", [2, 128, 32])
    wout_d = din("wout", [4, 128, 16 * 512])
    w1t_d = din("w1t", [NFC, 128, 16 * 128])
    w3t_d = din("w3t", [NFC, 128, 16 * 128])
    w2t_d = din("w2t", [4, 11, 128, 4 * 512])
    ln_d = din("ln", [4, 128, D])
    ident_d = din("ident", [128, 128])
    umat_d = din("umat", [128, 128])
    lmat_d = din("lmat", [128, 128])
    cind_d = din("cind", [128, 2])
    pm_d = din("pm", [32, 32])
    cos_d = din("cosT", [32, NB])
    sin_d = din("sinT", [32, NB])
    wedge_d = din("wedge", [128, 128])
    wdiag_d = din("wdiag", [128, 128])
    wctx_d = din("wctx", [128, 128])
    cmpb_d = din("cmpb", [128, NO])
    ovl_d = din("ovl", [128, 32])
    tka_d = din("tka", [128, 8 * 32])
    tkb_d = din("tkb", [128, 8 * 32])
    tkv_d = din("tkv", [128, 8 * 32])
    ekt_d = din("ekt", [128, 16 * 128])
    out_d = nc.dram_tensor("out", [8, 128, D], F32, kind="ExternalOutput").ap()
    dbg_d = {}
    for name, shape in dbg:
        dbg_d[name] = nc.dram_tensor("dbg_" + name, list(shape), F32, kind="ExternalOutput").ap()

    with ExitStack() as es:
        S = Sched(nc, es)

        uid = [0]

        def sb(stack, name, shape, dt):
            uid[0] += 1
            return stack.enter_context(nc.sbuf_tensor("%s_u%d" % (name, uid[0]), list(shape), dt))

        psf = [(es.enter_context(nc.psum_tensor("psf%d" % i, [128, 512], F32)), Buf("psf%d" % i, True)) for i in range(6)]
        psb = [(es.enter_context(nc.psum_tensor("psb%d" % i, [128, 1024], BF16)), Buf("psb%d" % i, True)) for i in range(2)]
        rr = {"f": 0, "b": 0}

        def ps_f(fixed=None, lo=0):
            if fixed is not None:
                return psf[fixed]
            r = psf[lo + rr["f"] % (6 - lo)]
            rr["f"] += 1
            return r

        def ps_b():
            r = psb[rr["b"] % 2]
            rr["b"] += 1
            return r

        def const(name, src, shape, dt):
            t = sb(es, "c_" + name, shape, dt)
            b = Buf(name)
            q = "pool" if dt == BF16 else "sp"
            S.dma(q, lambda e: e.dma_start(out=t[:], in_=src), writes=[b])
            return t, b

        id16, b_id16 = const("id16", ident_d, [128, 128], BF16)
        id32, b_id32 = const("id32", ident_d, [128, 128], F32)
        umat, b_umat = const("umat", umat_d, [128, 128], F32)
        lmat, b_lmat = const("lmat", lmat_d, [128, 128], F32)
        cind, b_cind = const("cind", cind_d, [128, 2], F32)
        normw, b_normw = const("normw", normw_d, [128, 256], F32)
        mixT = sb(es, "mixT", [128, 16, NO], BF16)
        b_mixT = [[Buf("mixT%d_%d" % (c, t)) for t in range(8)] for c in range(16)]

        def dump(name, ap, bufs):
            if name in dbg_d:
                S.dma("sp", lambda e: e.dma_start(out=dbg_d[name], in_=ap), reads=bufs)

        with ExitStack() as pm:
            xT16 = sb(pm, "xT16", [128, 16, NB], BF16)
            b_xT = [Buf("xT%d" % k) for k in range(16)]
            for kc in range(16):
                S.dma("pool", lambda e, kc=kc: e.dma_start(out=xT16[:, kc, :], in_=xT_d[:, kc, :]), writes=[b_xT[kc]])

            nsa_dd = dict(wn=wn_d, wv=wv_d, wgate=wgate_d, w1c=w1c_d, w2c=w2c_d, posc=posc_d, pm=pm_d, cos=cos_d, sin=sin_d,
                          wedge=wedge_d, wdiag=wdiag_d, wctx=wctx_d, cmpb=cmpb_d, ovl=ovl_d, tka=tka_d, tkb=tkb_d, tkv=tkv_d, ekt=ekt_d)
            nsa_consts = NSA_consts(nc, S, pm, sb, nsa_dd)
            with ExitStack() as pg:
                wg16 = [sb(pg, "wg16_%d" % i, [128, 16, 768], BF16) for i in range(2)]
                b_wg = [Buf("wg0"), Buf("wg1")]
                wglr16 = sb(pg, "wglr16", [128, 16, 16], BF16)
                b_wglr = Buf("wglr")
                w2aug = sb(pg, "w2aug_s", [17, 512], F32)
                b_w2aug = Buf("w2aug")
                glrT = sb(pg, "glrT", [17, NB], F32)
                b_glrT = Buf("glrT")
                S.dma("pool", lambda e: e.dma_start(out=wglr16[:].rearrange("p a b -> p (a b)"), in_=wglr_d), writes=[b_wglr])
                S.dma("sp", lambda e: e.dma_start(out=w2aug[:], in_=w2aug_d), writes=[b_w2aug])

                def load_wg(h):
                    S.dma("pool", lambda e: e.dma_start(out=wg16[h % 2][:].rearrange("p a b -> p (a b)"), in_=wg_d[h]),
                          writes=[b_wg[h % 2]])

                load_wg(0)
                load_wg(1)
                S.op("dve", lambda e: e.memset(glrT[:], 1.0), writes=[b_glrT])
                for tb in range(4):
                    pt, pb = ps_f()
                    for kc in range(16):
                        S.op("pe", lambda e, kc=kc, tb=tb, pt=pt: e.matmul(
                            pt[0:16, :], lhsT=wglr16[:, kc, :], rhs=xT16[:, kc, tb * 512:(tb + 1) * 512],
                            start=(kc == 0), stop=(kc == 15)), reads=[b_wglr, b_xT[kc]], writes=[pb])
                    S.op("act", lambda e, tb=tb, pt=pt: e.copy(out=glrT[0:16, tb * 512:(tb + 1) * 512], in_=pt[0:16, :]),
                         reads=[pb], writes=[b_glrT])

                NBUF = 2
                def mk(name, shape, dt):
                    return [(sb(pg, "%s%d" % (name, i), shape, dt), Buf("%s%d" % (name, i))) for i in range(NBUF)]
                sp_b = mk("sp", [128, 128], F32)
                ez_b = mk("ez", [128, 128], F32)
                e1_b = mk("e1", [128, 128], F32)
                e2_b = mk("e2", [128, 128], F32)
                e3_b = mk("e3", [128, 128], F32)
                dec_b = mk("dec", [128, 2], F32)
                qd_b = mk("qd", [128, 128], BF16)
                ki_b = mk("ki", [128, 128], BF16)
                ks_b = mk("kst", [128, 128], BF16)
                v16_b = mk("v16", [128, 256], BF16)
                qdp_b = mk("qdp", [128, 2, 128], BF16)
                kiT_b = mk("kiT", [128, 128], BF16)
                at_b = mk("at", [128, 128], BF16)
                ssq_b = mk("ssq", [128, 2], F32)
                junk_b = mk("junk", [128, 256], F32)
                y_b = mk("y", [128, 256], F32)
                sg_b = mk("sg", [128, 256], F32)
                gO_b = mk("gO", [128, 256], F32)
                yg_b = mk("yg", [128, 256], BF16)
                st32 = sb(pg, "st32", [128, 256], F32)
                b_st32 = Buf("st32")
                st16 = mk("st16", [128, 256], BF16)
                for i in range(NBUF):
                    S.op("dve", lambda e, i=i: e.memset(qdp_b[i][0][:], 0.0), writes=[qdp_b[i][1]])

                free_banks = list(range(6))

                def palloc():
                    assert free_banks, "GLA: out of PSUM banks"
                    return free_banks.pop(0)

                def pfree(i):
                    free_banks.append(i)

                tile_ctr = [0]

                def make_proj(h, t):
                    own = t >= 8
                    wt, wb = wg16[h % 2], b_wg[h % 2]
                    i2 = tile_ctr[0] % NBUF
                    tile_ctr[0] += 1
                    ia, ib = palloc(), palloc()
                    pA, bA = psf[ia]
                    pB, bB = psf[ib]
                    tok = slice(t * 128, (t + 1) * 128)
                    c0 = 0 if own else 128
                    thunks = []
                    thunks.append(lambda: S.op("pe", lambda e: e.matmul(
                        pB[:, 256:384], lhsT=glrT[0:17, tok], rhs=w2aug[0:17, h * 128:(h + 1) * 128], start=True, stop=True),
                        reads=[b_glrT, b_w2aug], writes=[bB]))
                    for kc in range(16):
                        thunks.append(lambda kc=kc: S.op("pe", lambda e: e.matmul(
                            pA[:, c0:512], lhsT=xT16[:, kc, tok], rhs=wt[:, kc, c0:512], start=(kc == 0), stop=(kc == 15)),
                            reads=[b_xT[kc], wb], writes=[bA]))
                    if own:
                        for kc in range(16):
                            thunks.append(lambda kc=kc: S.op("pe", lambda e: e.matmul(
                                pB[:, 0:256], lhsT=xT16[:, kc, tok], rhs=wt[:, kc, 512:768], start=(kc == 0), stop=(kc == 15)),
                                reads=[b_xT[kc], wb], writes=[bB]))
                    ez, bez = ez_b[i2]
                    spt, bsp = sp_b[i2]
                    sg, bsg = sg_b[i2]

                    done = {"sp": False}

                    def post_sp():
                        if done["sp"]:
                            return
                        done["sp"] = True
                        S.op("act", lambda e: e.activation(out=ez[:], in_=pB[:, 256:384], func=AF.Exp, scale=-1.0), reads=[bB], writes=[bez])
                        S.op("act", lambda e: e.activation(out=spt[:], in_=ez[:], func=AF.Ln, bias=1.0), reads=[bez], writes=[bsp])

                    def post():
                        post_sp()
                        if own:
                            gO, bgO = gO_b[i2]
                            S.op("act", lambda e: e.activation(out=sg[:], in_=pB[:, 0:256], func=AF.Exp, scale=-1.0), reads=[bB], writes=[bsg])
                            S.op("act", lambda e: e.copy(out=gO[:], in_=pB[:, 0:256]), reads=[bB], writes=[bgO])
                            S.op("dve", lambda e: e.tensor_scalar(out=sg[:], in0=sg[:], scalar1=1.0, scalar2=None, op0=ALU.add), reads=[bsg], writes=[bsg])
                            S.op("dve", lambda e: e.reciprocal(out=sg[:], in_=sg[:]), reads=[bsg], writes=[bsg])
                            S.op("dve", lambda e: e.tensor_tensor(out=sg[:], in0=sg[:], in1=gO[:], op=ALU.mult), reads=[bsg, bgO], writes=[bsg])
                        pfree(ib)

                    return dict(h=h, t=t, own=own, i2=i2, ia=ia, pA=pA, bA=bA, thunks=thunks, post=post, post_sp=post_sp,
                                spt=spt, bsp=bsp, sg=sg, bsg=bsg)

                def fill(nxt, n):
                    if nxt is None:
                        return
                    for _ in range(n):
                        if nxt["thunks"]:
                            nxt["thunks"].pop(0)()

                def run_tile(cur, nxt, state):
                    h, t, own, i2 = cur["h"], cur["t"], cur["own"], cur["i2"]
                    pA, bA, spt, bsp = cur["pA"], cur["bA"], cur["spt"], cur["bsp"]
                    nfill = (len(nxt["thunks"]) + 3) // 4 if nxt is not None else 0
                    iu = palloc()
                    pU, bU = psf[iu]
                    if own:
                        S.op("pe", lambda e: e.matmul(pU[:, 0:128], lhsT=umat[:], rhs=spt[:], start=True, stop=True),
                             reads=[b_umat, bsp], writes=[bU])
                    S.op("pe", lambda e: e.matmul(pU[:, 128:256], lhsT=lmat[:], rhs=spt[:], start=True, stop=True),
                         reads=[b_lmat, bsp], writes=[bU])
                    S.op("pe", lambda e: e.matmul(pU[:, 256:258], lhsT=spt[:], rhs=cind[:], start=True, stop=True),
                         reads=[b_cind, bsp], writes=[bU])
                    e3, be3 = e3_b[i2]
                    dec, bdec = dec_b[i2]
                    if own:
                        e1, be1 = e1_b[i2]
                        e2, be2 = e2_b[i2]
                        S.op("act", lambda e: e.activation(out=e1[:], in_=pU[:, 0:128], func=AF.Exp, scale=-1.0 / 16), reads=[bU], writes=[be1])
                        S.op("act", lambda e: e.activation(out=e2[:], in_=pU[:, 0:128], func=AF.Exp, scale=1.0 / 16), reads=[bU], writes=[be2])
                        qd, bqd = qd_b[i2]
                        ki, bki = ki_b[i2]
                        S.op("dve", lambda e: e.scalar_tensor_tensor(out=qd[:], in0=pA[:, 0:128], scalar=SCALE, in1=e1[:], op0=ALU.mult, op1=ALU.mult),
                             reads=[bA, be1], writes=[bqd])
                        S.op("dve", lambda e: e.tensor_tensor(out=ki[:], in0=pA[:, 128:256], in1=e2[:], op=ALU.mult), reads=[bA, be2], writes=[bki])
                    S.op("act", lambda e: e.activation(out=e3[:], in_=pU[:, 128:256], func=AF.Exp, scale=-1.0 / 16), reads=[bU], writes=[be3])
                    S.op("act", lambda e: e.activation(out=dec[:], in_=pU[:, 256:258], func=AF.Exp, scale=-1.0 / 16), reads=[bU], writes=[bdec])
                    pfree(iu)
                    kst, bks = ks_b[i2]
                    v16, bv16 = v16_b[i2]
                    S.op("dve", lambda e: e.tensor_tensor(out=kst[:], in0=pA[:, 128:256], in1=e3[:], op=ALU.mult), reads=[bA, be3], writes=[bks])
                    S.op("dve", lambda e: e.tensor_copy(out=v16[:], in_=pA[:, 256:512]), reads=[bA], writes=[bv16])
                    pfree(cur["ia"])
                    fill(nxt, nfill + 6 if own else nfill)
                    if nxt is not None:
                        nxt["post_sp"]()
                    if state.get("deferred") is not None:
                        state["deferred"]()
                        state["deferred"] = None
                    if own:
                        pT, bT = ps_b()
                        S.op("pe", lambda e: e.transpose(pT[:, 0:128], qd[:], id16[:]), reads=[bqd, b_id16], writes=[bT])
                        S.op("pe", lambda e: e.transpose(pT[:, 128:256], ki[:], id16[:]), reads=[bki, b_id16], writes=[bT])
                        qdp, bqdp = qdp_b[i2]
                        kiT, bkiT = kiT_b[i2]
                        S.op("dve", lambda e: e.tensor_copy(out=qdp[:, 0, 0:64], in_=pT[:, 0:64]), reads=[bT], writes=[bqdp])
                        S.op("dve", lambda e: e.tensor_copy(out=qdp[:, 1, 64:128], in_=pT[:, 64:128]), reads=[bT], writes=[bqdp])
                        S.op("dve", lambda e: e.tensor_copy(out=kiT[:], in_=pT[:, 128:256]), reads=[bT], writes=[bkiT])
                        fill(nxt, nfill)
                        iat = palloc()
                        pAt, bAt = psf[iat]
                        for c in range(2):
                            S.op("pe", lambda e, c=c: e.matmul(pAt[:, c * 64:(c + 1) * 64], lhsT=kiT[:], rhs=qdp[:, c, c * 64:(c + 1) * 64],
                                                               start=True, stop=True), reads=[bkiT, bqdp], writes=[bAt])
                        at, bat = at_b[i2]
                        S.op("dve", lambda e: e.tensor_tensor(out=at[:], in0=pAt[:, 0:128], in1=umat[:], op=ALU.mult), reads=[bAt, b_umat], writes=[bat])
                        pfree(iat)
                        fill(nxt, nfill)
                        io = palloc()
                        pO, bO = psf[io]
                        S.op("pe", lambda e: e.matmul(pO[:, 0:256], lhsT=at[:], rhs=v16[:], start=True, stop=False), reads=[bat, bv16], writes=[bO])
                    for c in range(2):
                        if own:
                            s16, bs16 = st16[state["v"] % 2]
                            S.op("pe", lambda e, c=c, s16=s16: e.matmul(pO[:, 0:256], lhsT=qdp[:, c, :], rhs=s16[:], start=False, stop=(c == 1)),
                                 reads=[bqdp, bs16], writes=[bO])
                        if t == 15 and c == 1:
                            break
                        isb = palloc()
                        pS, bS = psf[isb]
                        S.op("pe", lambda e, c=c, pS=pS: e.matmul(pS[:, 0:256], lhsT=kst[c * 64:(c + 1) * 64, :], rhs=v16[c * 64:(c + 1) * 64, :],
                                                                  start=True, stop=True), reads=[bks, bv16], writes=[bS])
                        S.op("dve", lambda e, c=c, pS=pS: e.scalar_tensor_tensor(out=st32[:], in0=st32[:], scalar=dec[:, c:c + 1], in1=pS[:, 0:256],
                                                                               op0=ALU.mult, op1=ALU.add), reads=[b_st32, bdec, bS], writes=[b_st32])
                        pfree(isb)
                        state["v"] += 1
                        s16n, bs16n = st16[state["v"] % 2]
                        S.op("act", lambda e, s16n=s16n: e.copy(out=s16n[:], in_=st32[:]), reads=[b_st32], writes=[bs16n])
                        if c == 0:
                            fill(nxt, nfill if not own else nfill)
                    fill(nxt, 1000)
                    if nxt is not None:
                        nxt["post"]()
                    if own:
                        ssq, bssq = ssq_b[i2]
                        junk, bjunk = junk_b[i2]
                        S.op("dve", lambda e: e.memset(ssq[:], 0.0), writes=[bssq])
                        S.op("act", lambda e: e.activation(out=junk[:], in_=pO[:, 0:256], func=AF.Square, accum_out=ssq[:, 0:1]),
                             reads=[bO, bssq], writes=[bjunk, bssq])
                        S.op("act", lambda e: e.activation(out=ssq[:, 1:2], in_=ssq[:, 0:1], func=AF.Ln, scale=1.0 / 256, bias=EPS), reads=[bssq], writes=[bssq])
                        S.op("act", lambda e: e.activation(out=ssq[:, 1:2], in_=ssq[:, 1:2], func=AF.Exp, scale=-0.5), reads=[bssq], writes=[bssq])
                        y, by = y_b[i2]
                        S.op("dve", lambda e: e.scalar_tensor_tensor(out=y[:], in0=pO[:, 0:256], scalar=ssq[:, 1:2], in1=normw[:], op0=ALU.mult, op1=ALU.mult),
                             reads=[bO, bssq, b_normw], writes=[by])
                        pfree(io)
                        sg, bsg = cur["sg"], cur["bsg"]
                        yg, byg = yg_b[i2]
                        S.op("dve", lambda e: e.tensor_tensor(out=yg[:], in0=y[:], in1=sg[:], op=ALU.mult), reads=[by, bsg], writes=[byg])
                        def s6_pe():
                            pT2, bT2 = ps_b()
                            for ec in range(2):
                                S.op("pe", lambda e, ec=ec: e.transpose(pT2[:, ec * 128:(ec + 1) * 128], yg[:, ec * 128:(ec + 1) * 128], id16[:]),
                                     reads=[byg, b_id16], writes=[bT2])
                            qt = t - 8
                            S.op("act", lambda e: e.copy(out=mixT[:, 2 * h:2 * h + 2, qt * 128:(qt + 1) * 128],
                                                         in_=pT2[:, 0:256].rearrange("p (a b) -> p a b", a=2)),
                                 reads=[bT2], writes=[b_mixT[2 * h][qt], b_mixT[2 * h + 1][qt]])
                        state["deferred"] = s6_pe

                tiles = [(h, t) for h in range(GLA_H) for t in range(GLA_T)]
                cur = make_proj(*tiles[0])
                fill(cur, 1000)
                cur["post"]()
                state = {"v": 0}
                for idx, (h, t) in enumerate(tiles):
                    if t == 0:
                        S.op("dve", lambda e: e.memset(st32[:], 0.0), writes=[b_st32])
                        S.op("dve", lambda e: e.memset(st16[0][0][:], 0.0), writes=[st16[0][1]])
                        state["v"] = 0
                    nxt = make_proj(*tiles[idx + 1]) if idx + 1 < len(tiles) else None
                    run_tile(cur, nxt, state)
                    cur = nxt
                    if t == GLA_T - 1 and h + 2 < 4:
                        load_wg(h + 2)
                if state.get("deferred") is not None:
                    state["deferred"]()
                    state["deferred"] = None
                S.barrier()
                S.emit()

            with ExitStack() as pn:
                NSA_phase(nc, S, pn, sb, ps_f, ps_b, xT16, b_xT, mixT, b_mixT, id16, b_id16, id32, b_id32, dump,
                          nsa_dd, nsa_consts)
            if True:
                S.barrier()
                S.emit()

        if "mixT" in dbg_d:
            with ExitStack() as pd:
                m32 = sb(pd, "m32", [128, 16, NO], F32)
                bm = Buf("m32")
                S.op("dve", lambda e: e.tensor_copy(out=m32[:], in_=mixT[:]), reads=[b for r in b_mixT for b in r], writes=[bm])
                S.dma("sp", lambda e: e.dma_start(out=dbg_d["mixT"], in_=m32[:]), reads=[bm])
                S.barrier()
                S.emit()

        with ExitStack() as pf:
          if "ffn" not in SKIP:
            FFN_phase(nc, S, pf, sb, ps_f, ps_b, mixT, b_mixT, id16, b_id16, xo_d, wout_d, w1t_d, w3t_d, w2t_d, ln_d, out_d, dump)
            S.barrier()
            S.emit()
    return nc


def FFN_phase(nc, S, pf, sb, ps_f, ps_b, mixT, b_mixT, id16, b_id16, xo_d, wout_d, w1t_d, w3t_d, w2t_d, ln_d, out_d, dump):
    r1 = sb(pf, "r1", [128, 4, D], F32)
    b_r1 = [[Buf("r1_%d_%d" % (t, c)) for c in range(4)] for t in range(4)]
    hT = sb(pf, "hT", [128, 16, 512], BF16)
    b_hT = [[Buf("hT%d_%d" % (k, t)) for t in range(4)] for k in range(16)]
    gT = sb(pf, "gT", [128, NFC, 512], BF16)
    b_gT = [Buf("gT%d" % j) for j in range(NFC)]
    lng = sb(pf, "lng", [128, D], F32)
    lnb = sb(pf, "lnb", [128, D], F32)
    b_lng, b_lnb = Buf("lng"), Buf("lnb")
    NW = 3
    wo16 = [(sb(pf, "wo16_%d" % i, [128, 16, 256], BF16), Buf("wo%d" % i)) for i in range(2)]
    w1b = [(sb(pf, "w1b_%d" % i, [128, 16, 128], BF16), Buf("w1b%d" % i)) for i in range(NW)]
    w3b = [(sb(pf, "w3b_%d" % i, [128, 16, 128], BF16), Buf("w3b%d" % i)) for i in range(NW)]
    w2b = [(sb(pf, "w2b_%d" % i, [128, 4, 512], BF16), Buf("w2b%d" % i)) for i in range(NW)]
    st = sb(pf, "lnst", [128, 4, 10], F32)
    b_st = [Buf("lnst%d" % t) for t in range(4)]
    sa = [(sb(pf, "sa%d" % i, [128, 512], F32), Buf("sa%d" % i)) for i in range(2)]

    def gview(c0):
        return gT[:, c0:c0 + 4, :].rearrange("p a b -> p (a b)"), b_gT[c0:c0 + 4]

    def layer_norm_all(after_tile=None):
        for tt in range(4):
            rb, src, s, bs = b_r1[tt], r1[:, tt, :], st[:, tt, :], b_st[tt]
            junk, bj = gview(4 * tt)
            S.op("dve", lambda e, s=s: e.memset(s[:, 0:2], 0.0), writes=[bs])
            S.op("act", lambda e, s=s, src=src, junk=junk: e.activation(out=junk, in_=src, func=AF.Identity, accum_out=s[:, 0:1]),
                 reads=rb + [bs], writes=bj + [bs])
            S.op("act", lambda e, s=s, src=src, junk=junk: e.activation(out=junk, in_=src, func=AF.Square, accum_out=s[:, 1:2]),
                 reads=rb + [bs], writes=bj + [bs])
        for tt in range(4):
            s, bs = st[:, tt, :], b_st[tt]
            S.op("dve", lambda e, s=s: e.tensor_scalar(out=s[:, 2:4], in0=s[:, 0:2], scalar1=1.0 / D, scalar2=None, op0=ALU.mult), reads=[bs], writes=[bs])
            S.op("dve", lambda e, s=s: e.tensor_tensor(out=s[:, 4:5], in0=s[:, 2:3], in1=s[:, 2:3], op=ALU.mult), reads=[bs], writes=[bs])
            S.op("dve", lambda e, s=s: e.tensor_tensor(out=s[:, 5:6], in0=s[:, 3:4], in1=s[:, 4:5], op=ALU.subtract), reads=[bs], writes=[bs])
            S.op("dve", lambda e, s=s: e.tensor_scalar(out=s[:, 5:6], in0=s[:, 5:6], scalar1=EPS, scalar2=None, op0=ALU.add), reads=[bs], writes=[bs])
        for tt in range(4):
            s, bs = st[:, tt, :], b_st[tt]
            S.op("act", lambda e, s=s: e.activation(out=s[:, 6:7], in_=s[:, 5:6], func=AF.Sqrt), reads=[bs], writes=[bs])
        for tt in range(4):
            s, bs = st[:, tt, :], b_st[tt]
            S.op("dve", lambda e, s=s: e.reciprocal(out=s[:, 7:8], in_=s[:, 6:7]), reads=[bs], writes=[bs])
            S.op("dve", lambda e, s=s: e.tensor_scalar(out=s[:, 8:9], in0=s[:, 2:3], scalar1=s[:, 7:8], scalar2=-1.0, op0=ALU.mult, op1=ALU.mult),
                 reads=[bs], writes=[bs])
        for tt in range(4):
            rb, src, s, bs = b_r1[tt], r1[:, tt, :], st[:, tt, :], b_st[tt]
            S.op("act", lambda e, s=s, src=src: e.activation(out=src, in_=src, func=AF.Identity, scale=s[:, 7:8], bias=s[:, 8:9]),
                 reads=rb + [bs], writes=rb)
        for tt in range(4):
            rb, src = b_r1[tt], r1[:, tt, :]
            S.op("dve", lambda e, src=src: e.tensor_tensor(out=src, in0=src, in1=lng[:], op=ALU.mult), reads=rb + [b_lng], writes=rb)
            S.op("dve", lambda e, src=src: e.tensor_tensor(out=src, in0=src, in1=lnb[:], op=ALU.add), reads=rb + [b_lnb], writes=rb)
            if after_tile is not None:
                after_tile(tt)

    for th in range(2):
        def load_wo(i, th=th):
            cb, hc = i // 2, i % 2
            wo, bwo = wo16[i % 2]
            src = wout_d[cb].rearrange("p (k c) -> p k c", k=16)[:, :, hc * 256:(hc + 1) * 256]
            S.dma("pool", lambda e: e.dma_start(out=wo[:], in_=src), writes=[bwo])
        load_wo(0)
        load_wo(1)
        for tt in range(4):
            S.dma("sp", lambda e, tt=tt, th=th: e.dma_start(out=r1[:, tt, :], in_=xo_d[th * 4 + tt]), writes=b_r1[tt])
        S.dma("sp", lambda e: e.dma_start(out=lng[:], in_=ln_d[0]), writes=[b_lng])
        S.dma("sp", lambda e: e.dma_start(out=lnb[:], in_=ln_d[1]), writes=[b_lnb])
        for i in range(8):
            cb, hc = i // 2, i % 2
            wo, bwo = wo16[i % 2]
            for tt in range(4):
                qt = th * 4 + tt
                pt, pb = ps_f()
                for kc in range(16):
                    S.op("pe", lambda e, kc=kc, pt=pt, qt=qt, wo=wo: e.matmul(
                        pt[:, 0:256], lhsT=mixT[:, kc, qt * 128:(qt + 1) * 128], rhs=wo[:, kc, :], start=(kc == 0), stop=(kc == 15)),
                        reads=[b_mixT[kc][qt], bwo], writes=[pb])
                dst = r1[:, tt, i * 256:(i + 1) * 256]
                S.op("dve", lambda e, dst=dst, pt=pt: e.scalar_tensor_tensor(out=dst, in0=dst, scalar=ALPHA, in1=pt[:, 0:256], op0=ALU.mult, op1=ALU.add),
                     reads=[pb, b_r1[tt][cb]], writes=[b_r1[tt][cb]])
            if i + 2 < 8:
                load_wo(i + 2)
        def ln1_tail(tt):
            h16, bh = gview(16 + 4 * (tt % 2))
            S.op("act", lambda e: e.copy(out=h16, in_=r1[:, tt, :]), reads=b_r1[tt], writes=bh)
            for k4 in range(4):
                pT, bT = ps_b()
                for k in range(4):
                    kc = k4 * 4 + k
                    S.op("pe", lambda e, k=k, kc=kc, pT=pT: e.transpose(pT[:, k * 128:(k + 1) * 128], h16[:, kc * 128:(kc + 1) * 128], id16[:]),
                         reads=bh + [b_id16], writes=[bT])
                eng = "act" if k4 % 2 == 0 else "dve"
                dstT = hT[:, k4 * 4:(k4 + 1) * 4, tt * 128:(tt + 1) * 128]
                srcT = pT[:, 0:512].rearrange("p (a b) -> p a b", a=4)
                wr = [b_hT[k4 * 4 + k][tt] for k in range(4)]
                if eng == "act":
                    S.op("act", lambda e, dstT=dstT, srcT=srcT: e.copy(out=dstT, in_=srcT), reads=[bT], writes=wr)
                else:
                    S.op("dve", lambda e, dstT=dstT, srcT=srcT: e.tensor_copy(out=dstT, in_=srcT), reads=[bT], writes=wr)
        layer_norm_all(after_tile=ln1_tail)
        if th == 0:
            dump("h", r1[:], [b for r in b_r1 for b in r])
        S.dma("sp", lambda e: e.dma_start(out=lng[:], in_=ln_d[2]), writes=[b_lng])
        S.dma("sp", lambda e: e.dma_start(out=lnb[:], in_=ln_d[3]), writes=[b_lnb])
        hT_all = [b for r in b_hT for b in r]

        def load_w13(j):
            w1, bw1 = w1b[j % NW]
            w3, bw3 = w3b[j % NW]
            S.dma("pool", lambda e: e.dma_start(out=w1[:].rearrange("p a b -> p (a b)"), in_=w1t_d[j]), writes=[bw1])
            S.dma("pool", lambda e: e.dma_start(out=w3[:].rearrange("p a b -> p (a b)"), in_=w3t_d[j]), writes=[bw3])

        for j in range(min(NW - 1, NFC)):
            load_w13(j)
        for j in range(NFC):
            if j + NW - 1 < NFC:
                load_w13(j + NW - 1)
            w1, bw1 = w1b[j % NW]
            w3, bw3 = w3b[j % NW]
            pa, pab = ps_f()
            pb_, pbb = ps_f()
            for kc in range(16):
                S.op("pe", lambda e, kc=kc, pa=pa, w1=w1: e.matmul(pa[:], lhsT=w1[:, kc, :], rhs=hT[:, kc, :], start=(kc == 0), stop=(kc == 15)),
                     reads=[bw1] + b_hT[kc], writes=[pab])
            for kc in range(16):
                S.op("pe", lambda e, kc=kc, pb_=pb_, w3=w3: e.matmul(pb_[:], lhsT=w3[:, kc, :], rhs=hT[:, kc, :], start=(kc == 0), stop=(kc == 15)),
                     reads=[bw3] + b_hT[kc], writes=[pbb])
            s, bs = sa[j % 2]
            S.op("act", lambda e, s=s, pa=pa: e.activation(out=s[:], in_=pa[:], func=AF.Silu), reads=[pab], writes=[bs])
            S.op("dve", lambda e, s=s, pb_=pb_, j=j: e.tensor_tensor(out=gT[:, j, :], in0=s[:], in1=pb_[:], op=ALU.mult),
                 reads=[bs, pbb], writes=[b_gT[j]])
        def load_w2(cb, fg):
            i = (cb * 11 + fg) % NW
            w2, bw2 = w2b[i]
            S.dma("pool", lambda e: e.dma_start(out=w2[:].rearrange("p a b -> p (a b)"), in_=w2t_d[cb, fg]), writes=[bw2])

        seq = [(cb, fg) for cb in range(4) for fg in range(11)]
        for i in range(NW - 1):
            load_w2(*seq[i])
        for idx, (cb, fg) in enumerate(seq):
            if idx + NW - 1 < len(seq):
                load_w2(*seq[idx + NW - 1])
            if fg == 0:
                acc = [ps_f() for _ in range(4)]
            w2, bw2 = w2b[idx % NW]
            for tt in range(4):
                pt, pb = acc[tt]
                for f in range(4):
                    fc = fg * 4 + f
                    S.op("pe", lambda e, pt=pt, fc=fc, tt=tt, w2=w2, f=f: e.matmul(
                        pt[:], lhsT=gT[:, fc, tt * 128:(tt + 1) * 128], rhs=w2[:, f, :], start=(fc == 0), stop=(fc == NFC - 1)),
                        reads=[b_gT[fc], bw2], writes=[pb])
            if fg == 10:
                for tt in range(4):
                    pt, pb = acc[tt]
                    dst = r1[:, tt, cb * 512:(cb + 1) * 512]
                    S.op("dve", lambda e, dst=dst, pt=pt: e.scalar_tensor_tensor(out=dst, in0=dst, scalar=ALPHA, in1=pt[:], op0=ALU.mult, op1=ALU.add),
                         reads=[pb, b_r1[tt][cb]], writes=[b_r1[tt][cb]])
        def ln2_tail(tt, th=th):
            S.dma("sp", lambda e: e.dma_start(out=out_d[th * 4 + tt], in_=r1[:, tt, :]), reads=b_r1[tt])
        layer_norm_all(after_tile=ln2_tail)


def NSA_consts(nc, S, pn, sb, dd):
    def cst(name, src, shape, dt):
        t = sb(pn, "n_" + name, shape, dt)
        b = Buf(name)
        q = "pool" if dt == BF16 else "sp"
        if len(shape) == 3:
            S.dma(q, lambda e: e.dma_start(out=t[:].rearrange("p a b -> p (a b)"), in_=src), writes=[b])
        else:
            S.dma(q, lambda e: e.dma_start(out=t[:], in_=src), writes=[b])
        return t, b

    pm32, b_pm = cst("pm", dd["pm"], [32, 32], F32)
    cosT, b_cos = cst("cos", dd["cos"], [32, NB], F32)
    sinT, b_sin = cst("sin", dd["sin"], [32, NB], F32)
    wedge, b_wedge = cst("wedge", dd["wedge"], [128, 128], BF16)
    wdiag, b_wdiag = cst("wdiag", dd["wdiag"], [128, 128], BF16)
    wctx, b_wctx = cst("wctx", dd["wctx"], [128, 128], BF16)
    cmpb, b_cmpb = cst("cmpb", dd["cmpb"], [128, NO], BF16)
    tka, b_tka = cst("tka", dd["tka"], [128, 8, 32], F32)
    tkb, b_tkb = cst("tkb", dd["tkb"], [128, 8, 32], F32)
    tkv, b_tkv = cst("tkv", dd["tkv"], [128, 8, 32], F32)
    ekt, b_ekt = cst("ekt", dd["ekt"], [128, 16, 128], BF16)
    wgate16, b_wgate = cst("wgate", dd["wgate"], [128, 16, 24], BF16)
    vcaug = [sb(pn, "vcaug%d" % g, [128, 161], BF16) for g in range(2)]
    b_vcaug = [Buf("vcaug0"), Buf("vcaug1")]
    kcmpT = [sb(pn, "kcmpT%d" % g, [128, 128], BF16) for g in range(2)]
    b_kcmpT = [Buf("kcmpT0"), Buf("kcmpT1")]
    sig = sb(pn, "sig", [128, 8, 24], F32)
    b_sig = Buf("sig")
    for g in range(2):
        S.op("dve", lambda e, g=g: e.memset(vcaug[g][:], 0.0), writes=[b_vcaug[g]])
        S.op("dve", lambda e, g=g: e.memset(vcaug[g][:, 128:129], 1.0), writes=[b_vcaug[g]])
        S.dma("pool", lambda e, g=g: e.dma_start(out=vcaug[g][:, 129:161], in_=dd["ovl"]), writes=[b_vcaug[g]])
        S.op("dve", lambda e, g=g: e.memset(kcmpT[g][:], 0.0), writes=[b_kcmpT[g]])

    return dict(pm32=pm32, b_pm=b_pm, cosT=cosT, b_cos=b_cos, sinT=sinT, b_sin=b_sin, wedge=wedge, b_wedge=b_wedge, wdiag=wdiag, b_wdiag=b_wdiag, wctx=wctx, b_wctx=b_wctx, cmpb=cmpb, b_cmpb=b_cmpb, tka=tka, b_tka=b_tka, tkb=tkb, b_tkb=b_tkb, tkv=tkv, b_tkv=b_tkv, ekt=ekt, b_ekt=b_ekt, wgate16=wgate16, b_wgate=b_wgate, vcaug=vcaug, b_vcaug=b_vcaug, kcmpT=kcmpT, b_kcmpT=b_kcmpT, sig=sig, b_sig=b_sig)


def NSA_phase(nc, S, pn, sb, ps_f, ps_b, xT16, b_xT, mixT, b_mixT, id16, b_id16, id32, b_id32, dump, dd, NC_):
    (pm32, b_pm, cosT, b_cos, sinT, b_sin, wedge, b_wedge, wdiag, b_wdiag, wctx, b_wctx, cmpb, b_cmpb, tka, b_tka, tkb, b_tkb, tkv, b_tkv, ekt, b_ekt, wgate16, b_wgate, vcaug, b_vcaug, kcmpT, b_kcmpT, sig, b_sig) = [NC_[k] for k in ('pm32', 'b_pm', 'cosT', 'b_cos', 'sinT', 'b_sin', 'wedge', 'b_wedge', 'wdiag', 'b_wdiag', 'wctx', 'b_wctx', 'cmpb', 'b_cmpb', 'tka', 'b_tka', 'tkb', 'b_tkb', 'tkv', 'b_tkv', 'ekt', 'b_ekt', 'wgate16', 'b_wgate', 'vcaug', 'b_vcaug', 'kcmpT', 'b_kcmpT', 'sig', 'b_sig')]
    def load_blk(ring, bi, src):
        t, b = ring[bi[0] % len(ring)]
        bi[0] += 1
        S.dma("pool", lambda e: e.dma_start(out=t[:].rearrange("p a b -> p (a b)"), in_=src), writes=[b])
        return t, b

    def proj_fm(wt, wb, tbs, evac):
        for tb in tbs:
            pt, pb = ps_f()
            for kc in range(16):
                S.op("pe", lambda e, kc=kc, pt=pt, tb=tb: e.matmul(pt[:], lhsT=wt[:, kc, :], rhs=xT16[:, kc, tb * 512:(tb + 1) * 512],
                                                                   start=(kc == 0), stop=(kc == 15)), reads=[wb, b_xT[kc]], writes=[pb])
            evac(tb, pt, pb)

    for qt in range(8):
        pt, pb = ps_f()
        for kc in range(16):
            S.op("pe", lambda e, kc=kc, pt=pt, qt=qt: e.matmul(pt[:, 0:24], lhsT=xT16[:, kc, NO + qt * 128:NO + (qt + 1) * 128], rhs=wgate16[:, kc, :],
                                                               start=(kc == 0), stop=(kc == 15)), reads=[b_wgate, b_xT[kc]], writes=[pb])
        S.op("act", lambda e, pt=pt, qt=qt: e.activation(out=sig[:, qt, :], in_=pt[:, 0:24], func=AF.Sigmoid), reads=[pb], writes=[b_sig])

    with ExitStack() as p1:
        kvT = [[sb(p1, "kvT%d_%d" % (ty, g), [128, NB], BF16) for g in range(2)] for ty in range(2)]
        b_kvT = [[Buf("kvT%d_%d" % (ty, g)) for g in range(2)] for ty in range(2)]
        w1c16 = [sb(p1, "w1c16_%d" % ty, [128, 32, 256], BF16) for ty in range(2)]
        b_w1c = [Buf("w1c0"), Buf("w1c1")]
        w2c16 = [sb(p1, "w2c16_%d" % ty, [128, 2, 128], BF16) for ty in range(2)]
        b_w2c = [Buf("w2c0"), Buf("w2c1")]
        posc16 = [sb(p1, "posc16_%d" % ty, [128, 32], BF16) for ty in range(2)]
        b_posc = [Buf("posc0"), Buf("posc1")]
        ring = [(sb(p1, "wnb1_%d" % i, [128, 16, 128], BF16), Buf("wnb1_%d" % i)) for i in range(2)]
        bi = [0]
        pbias = sb(p1, "pbias", [128, 4], F32)
        b_pbias = Buf("pbias")
        xh = sb(p1, "xh", [128, 127], F32); b_xh = Buf("xh")
        x2 = sb(p1, "x2", [128, 127], F32); b_x2 = Buf("x2")
        sg = sb(p1, "sgc", [128, 127], F32); b_sg = Buf("sgc")
        hid = sb(p1, "hid", [128, 2, 128], BF16); b_hid = Buf("hid")
        items1 = [(g, ty) for g in range(2) for ty in range(2)]
        loaded = [load_blk(ring, bi, dd["wn"][8 * g + 6 + ty]) for g, ty in items1[:2]]
        for ty in range(2):
            S.dma("pool", lambda e, ty=ty: e.dma_start(out=w1c16[ty][:].rearrange("p a b -> p (a b)"), in_=dd["w1c"][ty]), writes=[b_w1c[ty]])
            S.dma("pool", lambda e, ty=ty: e.dma_start(out=w2c16[ty][:].rearrange("p a b -> p (a b)"), in_=dd["w2c"][ty]), writes=[b_w2c[ty]])
            S.dma("pool", lambda e, ty=ty: e.dma_start(out=posc16[ty][:], in_=dd["posc"][ty]), writes=[b_posc[ty]])
        for i1, (g, ty) in enumerate(items1):
            wt, wb = loaded[i1]
            dst, bd = kvT[ty][g], b_kvT[ty][g]
            proj_fm(wt, wb, range(4), lambda tb, pt, pb, dst=dst, bd=bd: S.op(
                "act", lambda e: e.copy(out=dst[:, tb * 512:(tb + 1) * 512], in_=pt[:]), reads=[pb], writes=[bd]))
            if i1 + 2 < len(items1):
                g2, ty2 = items1[i1 + 2]
                loaded.append(load_blk(ring, bi, dd["wn"][8 * g2 + 6 + ty2]))
        pt, pb = ps_f()
        for ty in range(2):
            for cc in range(2):
                for l in range(32):
                    S.op("pe", lambda e, ty=ty, cc=cc, l=l, pt=pt: e.matmul(
                        pt[:, ty * 2 + cc:ty * 2 + cc + 1], lhsT=w1c16[ty][:, l, cc * 128:(cc + 1) * 128], rhs=posc16[ty][:, l:l + 1],
                        start=(l == 0), stop=(l == 31)), reads=[b_w1c[ty], b_posc[ty]], writes=[pb])
        S.op("act", lambda e, pt=pt: e.copy(out=pbias[:], in_=pt[:, 0:4]), reads=[pb], writes=[b_pbias])
        S.op("dve", lambda e: e.memset(hid[:], 0.0), writes=[b_hid])
        for g in range(2):
            for ty in range(2):
                src, bs = kvT[ty][g], b_kvT[ty][g]
                for cc in range(2):
                    pt, pb = ps_f()
                    for l in range(32):
                        S.op("pe", lambda e, ty=ty, cc=cc, l=l, pt=pt, src=src: e.matmul(
                            pt[:, 0:127], lhsT=w1c16[ty][:, l, cc * 128:(cc + 1) * 128], rhs=src[:, l:l + 16 * 126 + 1:16],
                            start=(l == 0), stop=(l == 31)), reads=[b_w1c[ty], bs], writes=[pb])
                    S.op("act", lambda e, pt=pt, ty=ty, cc=cc: e.activation(out=xh[:], in_=pt[:, 0:127], func=AF.Identity,
                                                                          bias=pbias[:, ty * 2 + cc:ty * 2 + cc + 1]),
                         reads=[pb, b_pbias], writes=[b_xh])
                    S.op("dve", lambda e: e.tensor_tensor(out=x2[:], in0=xh[:], in1=xh[:], op=ALU.mult), reads=[b_xh], writes=[b_x2])
                    S.op("dve", lambda e: e.tensor_scalar(out=x2[:], in0=x2[:], scalar1=0.044715, scalar2=1.0, op0=ALU.mult, op1=ALU.add),
                         reads=[b_x2], writes=[b_x2])
                    S.op("dve", lambda e: e.tensor_tensor(out=x2[:], in0=x2[:], in1=xh[:], op=ALU.mult), reads=[b_x2, b_xh], writes=[b_x2])
                    S.op("act", lambda e: e.activation(out=sg[:], in_=x2[:], func=AF.Sigmoid, scale=1.5957691216057308),
                         reads=[b_x2], writes=[b_sg])
                    S.op("dve", lambda e, cc=cc: e.tensor_tensor(out=hid[:, cc, 0:127], in0=xh[:], in1=sg[:], op=ALU.mult),
                         reads=[b_xh, b_sg], writes=[b_hid])
                pt, pb = ps_f()
                if ty == 0:
                    for cc in range(2):
                        S.op("pe", lambda e, cc=cc, pt=pt: e.matmul(pt[:, 0:127], lhsT=w2c16[0][:, cc, :], rhs=hid[:, cc, 0:127],
                                                                    start=(cc == 0), stop=(cc == 1)), reads=[b_w2c[0], b_hid], writes=[pb])
                    S.op("act", lambda e, pt=pt, g=g: e.copy(out=kcmpT[g][:, 0:127], in_=pt[:, 0:127]), reads=[pb], writes=[b_kcmpT[g]])
                else:
                    for cc in range(2):
                        S.op("pe", lambda e, cc=cc, pt=pt: e.matmul(pt[0:127, 0:128], lhsT=hid[:, cc, 0:127], rhs=w2c16[1][:, cc, :],
                                                                    start=(cc == 0), stop=(cc == 1)), reads=[b_w2c[1], b_hid], writes=[pb])
                    S.op("act", lambda e, pt=pt, g=g: e.copy(out=vcaug[g][0:127, 0:128], in_=pt[0:127, 0:128]), reads=[pb], writes=[b_vcaug[g]])
        S.barrier()
        S.emit()

    qraw = sb(pn, "qraw", [128, 4, NO], BF16); b_qraw = [Buf("qraw%d" % r) for r in range(4)]
    qrope = sb(pn, "qrope", [128, 4, NO], BF16); b_qrope = [Buf("qrope%d" % r) for r in range(4)]
    ksT = sb(pn, "ksT", [128, NB], BF16); b_ksT = Buf("ksT")
    kwT = sb(pn, "kwT", [128, NB], BF16); b_kwT = Buf("kwT")
    vsaug = sb(pn, "vsaug", [128, 16, 129], BF16); b_vs = Buf("vsaug")
    vwaug = sb(pn, "vwaug", [128, 16, 129], BF16); b_vw = Buf("vwaug")
    S.op("dve", lambda e: e.memset(vsaug[:, :, 128:129], 1.0), writes=[b_vs])
    S.op("dve", lambda e: e.memset(vwaug[:, :, 128:129], 1.0), writes=[b_vw])
    for g in range(2):
        with ExitStack() as pa:
            ring = [(sb(pa, "wnb2_%d" % i, [128, 16, 128], BF16), Buf("wnb2_%d" % i)) for i in range(3)]
            bi = [0]
            wv16 = sb(pa, "wv16", [128, 16, 256], BF16); b_wv = Buf("wv16")
            q32s = [(sb(pa, "q32_%d" % i, [32, 512], F32), Buf("q32_%d" % i)) for i in range(2)]
            t1 = sb(pa, "t1", [32, 512], F32); b_t1 = Buf("t1")
            t2 = sb(pa, "t2", [32, 512], F32); b_t2 = Buf("t2")
            rope_pending = []
            rope_ctr = [0]
            S.dma("pool", lambda e, g=g: e.dma_start(out=wv16[:].rearrange("p a b -> p (a b)"), in_=dd["wv"][g]), writes=[b_wv])

            def rope_evac(tb, pt, pb, dst, bd, raw=None):
                cs = slice(tb * 512, (tb + 1) * 512)
                q32, b_q32 = q32s[rope_ctr[0] % 2]
                rope_ctr[0] += 1
                while rope_pending:
                    rope_pending.pop(0)()
                if raw is not None:
                    rdst, rb = raw
                    S.op("act", lambda e: e.copy(out=rdst, in_=pt[:]), reads=[pb], writes=[rb])
                S.op("act", lambda e: e.copy(out=q32[:], in_=pt[0:32, :]), reads=[pb], writes=[b_q32])
                S.op("act", lambda e: e.copy(out=dst[32:64, :], in_=pt[32:64, :]), reads=[pb], writes=[bd])
                S.op("act", lambda e: e.copy(out=dst[64:128, :], in_=pt[64:128, :]), reads=[pb], writes=[bd])
                def part2():
                    pw, pwb = ps_f()
                    S.op("pe", lambda e: e.matmul(pw[0:32, :], lhsT=pm32[:], rhs=q32[:], start=True, stop=True), reads=[b_pm, b_q32], writes=[pwb])
                    S.op("dve", lambda e: e.tensor_tensor(out=t1[:], in0=q32[:], in1=cosT[:, cs], op=ALU.mult), reads=[b_q32, b_cos], writes=[b_t1])
                    S.op("dve", lambda e: e.tensor_tensor(out=t2[:], in0=pw[0:32, :], in1=sinT[:, cs], op=ALU.mult), reads=[pwb, b_sin], writes=[b_t2])
                    S.op("dve", lambda e: e.tensor_tensor(out=dst[0:32, :], in0=t1[:], in1=t2[:], op=ALU.add), reads=[b_t1, b_t2], writes=[bd])
                rope_pending.append(part2)

            for r in range(4):
                wt, wb = load_blk(ring, bi, dd["wn"][8 * g + r])
                proj_fm(wt, wb, (2, 3), lambda tb, pt, pb, r=r: rope_evac(
                    tb, pt, pb, qrope[:, r, (tb - 2) * 512:(tb - 1) * 512], b_qrope[r],
                    raw=(qraw[:, r, (tb - 2) * 512:(tb - 1) * 512], b_qraw[r])))
            wt, wb = load_blk(ring, bi, dd["wn"][8 * g + 4])
            proj_fm(wt, wb, range(4), lambda tb, pt, pb: rope_evac(tb, pt, pb, ksT[:, tb * 512:(tb + 1) * 512], b_ksT))
            wt, wb = load_blk(ring, bi, dd["wn"][8 * g + 5])
            proj_fm(wt, wb, range(4), lambda tb, pt, pb: rope_evac(tb, pt, pb, kwT[:, tb * 512:(tb + 1) * 512], b_kwT))
            for t in range(16):
                if t == 1:
                    while rope_pending:
                        rope_pending.pop(0)()
                pt, pb = ps_f()
                for kc in range(16):
                    S.op("pe", lambda e, kc=kc, pt=pt, t=t: e.matmul(pt[:, 0:256], lhsT=xT16[:, kc, t * 128:(t + 1) * 128], rhs=wv16[:, kc, :],
                                                                     start=(kc == 0), stop=(kc == 15)), reads=[b_wv, b_xT[kc]], writes=[pb])
                S.op("act", lambda e, pt=pt, t=t: e.copy(out=vsaug[:, t, 0:128], in_=pt[:, 0:128]), reads=[pb], writes=[b_vs])
                S.op("act", lambda e, pt=pt, t=t: e.copy(out=vwaug[:, t, 0:128], in_=pt[:, 128:256]), reads=[pb], writes=[b_vw])
            S.barrier()
            S.emit()
        with ExitStack() as pb_:
            PT = [(sb(pb_, "PT%d" % i, [128, 512], BF16), Buf("PT%d" % i)) for i in range(3)]
            pti = [0]
            PTw = [(sb(pb_, "PTw%d" % i, [128, 5, 128], BF16), Buf("PTw%d" % i)) for i in range(2)]
            ptwi = [0]
            selbT = sb(pb_, "selbT", [128, NO], BF16); b_selbT = [Buf("selbT%d" % q) for q in range(8)]
            S.op("dve", lambda e: e.memset(selbT[:], 0.0), writes=b_selbT)
            onsa = sb(pb_, "onsa", [128, 8, 4, 128], F32); b_onsa = [[Buf("onsa%d_%d" % (q, r)) for r in range(4)] for q in range(8)]
            imp = sb(pb_, "imp", [128, 8, 32], F32); b_imp = [Buf("imp%d" % q) for q in range(8)]
            cmp3s = [(sb(pb_, "cmp3_%d" % i, [128, 32, 32], F32), Buf("cmp3_%d" % i)) for i in range(2)]
            rk = sb(pb_, "rk", [128, 8, 32], F32); b_rk = [Buf("rk%d" % q) for q in range(8)]
            sm = [(sb(pb_, "sm%d" % i, [128, 4], F32), Buf("sm%d" % i)) for i in range(4)]
            smi = [0]
            o16 = [(sb(pb_, "o16_%d" % i, [128, 4, 128], BF16), Buf("o16_%d" % i)) for i in range(2)]
            wstg = [(sb(pb_, "wstg%d" % i, [128, 129], F32), Buf("wstg%d" % i)) for i in range(12)]
            wsi = [0]
            zz = sb(pb_, "zz", [128, 258], BF16); b_zz = Buf("zz")
            S.op("dve", lambda e: e.memset(zz[:], 0.0), writes=[b_zz])

            def finalize(pacc, pab, col0, qt, r, branch, first):
                s, bs = sm[smi[0] % 4]
                smi[0] += 1
                S.op("dve", lambda e: e.reciprocal(out=s[:, 1:2], in_=pacc[:, col0 + 128:col0 + 129]), reads=[pab], writes=[bs])
                gcol = (4 * g + r) * 3 + branch
                S.op("dve", lambda e: e.tensor_tensor(out=s[:, 2:3], in0=s[:, 1:2], in1=sig[:, qt, gcol:gcol + 1], op=ALU.mult),
                     reads=[bs, b_sig], writes=[bs])
                dst = onsa[:, qt, r, :]
                if first:
                    S.op("act", lambda e: e.activation(out=dst, in_=pacc[:, col0:col0 + 128], func=AF.Copy, scale=s[:, 2:3]),
                         reads=[pab, bs], writes=[b_onsa[qt][r]])
                else:
                    S.op("dve", lambda e: e.scalar_tensor_tensor(out=dst, in0=pacc[:, col0:col0 + 128], scalar=s[:, 2:3], in1=dst,
                                                                 op0=ALU.mult, op1=ALU.add), reads=[pab, bs, b_onsa[qt][r]], writes=[b_onsa[qt][r]])
                return s, bs

            def cmp_scores(qb, r):
                qs = slice(qb * 512, (qb + 1) * 512)
                pt, pb = ps_f()
                S.op("pe", lambda e: e.matmul(pt[:], lhsT=kcmpT[g][:], rhs=qraw[:, r, qs], start=True, stop=False),
                     reads=[b_kcmpT[g], b_qraw[r]], writes=[pb])
                S.op("pe", lambda e: e.matmul(pt[:], lhsT=id16[:], rhs=cmpb[:, qs], start=False, stop=True),
                     reads=[b_id16, b_cmpb], writes=[pb])
                P, bP = PT[pti[0] % 3]
                pti[0] += 1
                S.op("act", lambda e: e.activation(out=P[:], in_=pt[:], func=AF.Exp, scale=SCALE), reads=[pb], writes=[bP])
                return P, bP

            def cmp_pv(qb, r, P, bP):
                for j in range(4):
                    qt = qb * 4 + j
                    po, pob = ps_f()
                    S.op("pe", lambda e, po=po, j=j: e.matmul(po[:, 0:161], lhsT=P[:, j * 128:(j + 1) * 128], rhs=vcaug[g][:],
                                                              start=True, stop=True), reads=[bP, b_vcaug[g]], writes=[pob])
                    s, bs = finalize(po, pob, 0, qt, r, 0, True)
                    if r == 0:
                        S.op("dve", lambda e, po=po, s=s, qt=qt: e.tensor_scalar(out=imp[:, qt, :], in0=po[:, 129:161], scalar1=s[:, 1:2],
                                                                                scalar2=None, op0=ALU.mult), reads=[pob, bs], writes=[b_imp[qt]])
                    else:
                        S.op("dve", lambda e, po=po, s=s, qt=qt: e.scalar_tensor_tensor(
                            out=imp[:, qt, :], in0=po[:, 129:161], scalar=s[:, 1:2], in1=imp[:, qt, :], op0=ALU.mult, op1=ALU.add),
                            reads=[pob, bs, b_imp[qt]], writes=[b_imp[qt]])

            def topk_dve(qts):
                for qt in qts:
                    iv = imp[:, qt, :]
                    rq = rk[:, qt, :]
                    cmp3, b_cmp3 = cmp3s[qt % 2]
                    S.op("dve", lambda e, iv=iv, qt=qt: e.tensor_tensor(out=iv, in0=iv, in1=tka[:, qt, :], op=ALU.mult), reads=[b_imp[qt], b_tka], writes=[b_imp[qt]])
                    S.op("dve", lambda e, iv=iv, qt=qt: e.tensor_tensor(out=iv, in0=iv, in1=tkb[:, qt, :], op=ALU.add), reads=[b_imp[qt], b_tkb], writes=[b_imp[qt]])
                    S.op("dve", lambda e, iv=iv, cmp3=cmp3: e.tensor_tensor(out=cmp3[:], in0=iv.unsqueeze(1).to_broadcast([128, 32, 32]),
                                                                 in1=iv.unsqueeze(2).to_broadcast([128, 32, 32]), op=ALU.is_gt),
                         reads=[b_imp[qt]], writes=[b_cmp3])
                    S.op("dve", lambda e, rq=rq, cmp3=cmp3: e.tensor_reduce(out=rq, in_=cmp3[:], axis=AX.X, op=ALU.add), reads=[b_cmp3], writes=[b_rk[qt]])
                    S.op("dve", lambda e, rq=rq: e.tensor_scalar(out=rq, in0=rq, scalar1=15.5, scalar2=None, op0=ALU.is_lt), reads=[b_rk[qt]], writes=[b_rk[qt]])
                    S.op("dve", lambda e, rq=rq, qt=qt: e.tensor_tensor(out=rq, in0=rq, in1=tkv[:, qt, :], op=ALU.mult), reads=[b_rk[qt], b_tkv], writes=[b_rk[qt]])
                    S.op("dve", lambda e, rq=rq: e.tensor_scalar(out=rq, in0=rq, scalar1=-NEG, scalar2=NEG, op0=ALU.mult, op1=ALU.add),
                         reads=[b_rk[qt]], writes=[b_rk[qt]])


            def topk_transposes(qts):
                for qt in qts:
                    pt, pb = ps_f()
                    S.op("pe", lambda e, pt=pt, qt=qt: e.transpose(pt[0:32, 0:128], rk[:, qt, :], id32[:]), reads=[b_rk[qt], b_id32], writes=[pb])
                    S.op("act", lambda e, pt=pt, qt=qt: e.copy(out=selbT[0:32, qt * 128:(qt + 1) * 128], in_=pt[0:32, 0:128]), reads=[pb], writes=[b_selbT[qt]])

            def win_scores(qt, r):
                pw1, pw1b = ps_f()
                pw2, pw2b = ps_f()
                for m in range(5):
                    kt = 4 + qt + m
                    tgt, tb_ = (pw1, pw1b) if m < 4 else (pw2, pw2b)
                    cs = slice((m % 4) * 128, (m % 4 + 1) * 128)
                    extra = []
                    if m == 0:
                        extra.append((wedge, b_wedge))
                    if m == 4:
                        extra.append((wdiag, b_wdiag))
                    if kt < 8:
                        extra.append((wctx, b_wctx))
                    S.op("pe", lambda e, tgt=tgt, cs=cs, kt=kt, ne=len(extra): e.matmul(
                        tgt[:, cs], lhsT=kwT[:, kt * 128:(kt + 1) * 128], rhs=qrope[:, r, qt * 128:(qt + 1) * 128], start=True, stop=(ne == 0)),
                        reads=[b_kwT, b_qrope[r]], writes=[tb_])
                    for xi, (xt, xb) in enumerate(extra):
                        S.op("pe", lambda e, tgt=tgt, cs=cs, xt=xt, xi=xi, ne=len(extra): e.matmul(
                            tgt[:, cs], lhsT=id16[:], rhs=xt[:], start=False, stop=(xi == ne - 1)), reads=[b_id16, xb], writes=[tb_])
                Pw, bPw = PTw[ptwi[0] % 2]
                ptwi[0] += 1
                S.op("act", lambda e: e.activation(out=Pw[:, 0:4, :], in_=pw1[:].rearrange("p (a b) -> p a b", a=4), func=AF.Exp, scale=SCALE),
                     reads=[pw1b], writes=[bPw])
                S.op("act", lambda e: e.activation(out=Pw[:, 4, :], in_=pw2[:, 0:128], func=AF.Exp, scale=SCALE),
                     reads=[pw2b], writes=[bPw])
                return Pw, bPw

            def win_pv(qt, r, Pw, bPw):
                po, pob = ps_f()
                for m in range(5):
                    kt = 4 + qt + m
                    S.op("pe", lambda e, m=m, kt=kt: e.matmul(po[:, 0:129], lhsT=Pw[:, m, :], rhs=vwaug[:, kt, :],
                                                              start=(m == 0), stop=(m == 4)), reads=[bPw, b_vw], writes=[pob])
                stg, bstg = wstg[wsi[0] % len(wstg)]
                wsi[0] += 1
                S.op("act", lambda e: e.copy(out=stg[:], in_=po[:, 0:129]), reads=[pob], writes=[bstg])
                finalize(stg, bstg, 0, qt, r, 2, False)

            cmp_list = [(qb, r) for qb in range(2) for r in range(4)]
            win_list = [(4 * qb + j, r) for qb in range(2) for r in range(4) for j in range(4)] if 'win' in NSA_BR else []
            wst = {"i": 0, "prev": None}

            def win_step():
                if wst["i"] >= len(win_list):
                    return
                qt, r = win_list[wst["i"]]
                wst["i"] += 1
                cur = (qt, r) + win_scores(qt, r)
                if wst["prev"] is not None:
                    win_pv(*wst["prev"])
                wst["prev"] = cur

            cprev = None
            for r in range(4):
                cur = (0, r) + cmp_scores(0, r)
                if cprev is not None:
                    cmp_pv(*cprev)
                cprev = cur
            cmp_pv(*cprev)
            topk_dve(range(4))
            cprev = None
            for r in range(4):
                cur = (1, r) + cmp_scores(1, r)
                if cprev is not None:
                    cmp_pv(*cprev)
                cprev = cur
                for _ in range(4):
                    win_step()
            cmp_pv(*cprev)
            if g == 0:
                dump("imp", imp[:], b_imp)
            while wst["i"] < len(win_list):
                win_step()
            if wst["prev"] is not None:
                win_pv(*wst["prev"])
            topk_transposes(range(4))
            topk_dve(range(4, 8))

            def slc_scores(qb, r, kt):
                qs = slice(qb * 512, (qb + 1) * 512)
                m = kt - (8 + 4 * qb)
                pt, pb = ps_f(lo=4)
                S.op("pe", lambda e: e.matmul(pt[:], lhsT=ksT[:, kt * 128:(kt + 1) * 128], rhs=qrope[:, r, qs],
                                              start=True, stop=False), reads=[b_ksT, b_qrope[r]], writes=[pb])
                if 0 <= m <= 3:
                    S.op("pe", lambda e: e.matmul(pt[:, m * 128:(m + 1) * 128], lhsT=id16[:], rhs=wdiag[:], start=False, stop=False),
                         reads=[b_id16, b_wdiag], writes=[pb])
                S.op("pe", lambda e: e.matmul(pt[:], lhsT=ekt[:, kt, :], rhs=selbT[:, qs], start=False, stop=True),
                     reads=[b_ekt] + b_selbT[qb * 4:qb * 4 + 4], writes=[pb])
                P, bP = PT[pti[0] % 3]
                pti[0] += 1
                S.op("act", lambda e: e.activation(out=P[:], in_=pt[:], func=AF.Exp, scale=SCALE), reads=[pb], writes=[bP])
                return P, bP

            def slc_pv(qb, kt, acc, P, bP):
                for j in range(4):
                    last = 8 + 4 * qb + j
                    if kt > last:
                        continue
                    pa_, pab = acc[j // 2]
                    c0 = (j % 2) * 129
                    S.op("pe", lambda e, pa_=pa_, c0=c0, j=j, last=last: e.matmul(
                        pa_[:, c0:c0 + 129], lhsT=P[:, j * 128:(j + 1) * 128], rhs=vsaug[:, kt, :], start=False, stop=(kt == last and j % 2 == 1)),
                        reads=[bP, b_vs], writes=[pab])

            it_ = 0
            for qb in (range(2) if 'slc' in NSA_BR else ()):
                nkt = 8 + 4 * qb + 4
                if qb == 1:
                    topk_transposes(range(4, 8))
                for r in range(4):
                    acc = [ps_f(fixed=2 * (it_ % 2)), ps_f(fixed=2 * (it_ % 2) + 1)]
                    it_ += 1
                    for pa_, pab in acc:
                        S.op("pe", lambda e, pa_=pa_: e.matmul(pa_[:, 0:258], lhsT=zz[:, 0:128], rhs=zz[:, 0:258], start=True, stop=False),
                             reads=[b_zz], writes=[pab])
                    prev = None
                    for kt in range(nkt):
                        cur = (kt,) + slc_scores(qb, r, kt)
                        if prev is not None:
                            slc_pv(qb, prev[0], acc, prev[1], prev[2])
                        prev = cur
                    slc_pv(qb, prev[0], acc, prev[1], prev[2])
                    for j in range(4):
                        pa_, pab = acc[j // 2]
                        finalize(pa_, pab, (j % 2) * 129, qb * 4 + j, r, 1, False)
            if g == 0:
                dump("onsa", onsa[:], [b for rr_ in b_onsa for b in rr_])
            for qt in range(8):
                o, bo = o16[qt % 2]
                S.op("act", lambda e, o=o, qt=qt: e.copy(out=o[:], in_=onsa[:, qt, :, :]), reads=b_onsa[qt], writes=[bo])
                pT, bT = ps_b()
                for r in range(4):
                    S.op("pe", lambda e, pT=pT, o=o, r=r: e.transpose(pT[:, r * 128:(r + 1) * 128], o[:, r, :], id16[:]), reads=[bo, b_id16], writes=[bT])
                S.op("dve", lambda e, pT=pT, qt=qt: e.tensor_copy(out=mixT[:, 8 + 4 * g:12 + 4 * g, qt * 128:(qt + 1) * 128],
                                                                  in_=pT[:, 0:512].rearrange("p (a b) -> p a b", a=4)),
                     reads=[bT], writes=[b_mixT[8 + 4 * g + r][qt] for r in range(4)])
            S.barrier()
            S.emit()


def _kc_layout(w):
    C = w.shape[1]
    return np.ascontiguousarray(w.reshape(16, 128, C).transpose(1, 0, 2)).reshape(128, 16 * C)


def prep_shared(inp):
    w_in = inp["w_in"][0]
    sh = {}
    wg = []
    for h in range(4):
        cols = np.concatenate([w_in[:, O_GQ + h * 128:O_GQ + (h + 1) * 128], w_in[:, O_GK + h * 128:O_GK + (h + 1) * 128],
                               w_in[:, O_GV + h * 256:O_GV + (h + 1) * 256], w_in[:, O_GO + h * 256:O_GO + (h + 1) * 256]], axis=1)
        wg.append(_kc_layout(cols))
    sh["wg"] = np.stack(wg)
    sh["wglr"] = _kc_layout(w_in[:, O_GLR:O_GLR + 16])
    sh["w2aug"] = np.concatenate([inp["gla_gate_w2"][0], inp["gla_gate_b2"][0][None, :]], axis=0).astype(np.float32)
    sh["normw"] = np.ascontiguousarray(np.broadcast_to(inp["gla_norm_w"][0][None, :], (128, 256))).astype(np.float32)
    blocks = []
    for g in range(2):
        for r in range(4):
            hh = 4 * g + r
            blocks.append(w_in[:, O_NQ + hh * 128:O_NQ + (hh + 1) * 128])
        blocks.append(w_in[:, O_KS + g * 128:O_KS + (g + 1) * 128])
        blocks.append(w_in[:, O_KW + g * 128:O_KW + (g + 1) * 128])
        blocks.append(w_in[:, O_KC + g * 128:O_KC + (g + 1) * 128])
        blocks.append(w_in[:, O_VC + g * 128:O_VC + (g + 1) * 128])
    sh["wn"] = np.stack([_kc_layout(b) for b in blocks])
    sh["wv"] = np.stack([_kc_layout(np.concatenate([w_in[:, O_VS + g * 128:O_VS + (g + 1) * 128],
                                                     w_in[:, O_VW + g * 128:O_VW + (g + 1) * 128]], axis=1)) for g in range(2)])
    sh["wgate"] = _kc_layout(w_in[:, O_GATE:O_GATE + 24])
    w1c = []
    for nm in ("cmp_k_w1", "cmp_v_w1"):
        w1 = inp[nm][0]
        w1c.append(np.ascontiguousarray(w1.reshape(32, 128, 256).transpose(1, 0, 2)).reshape(128, 32 * 256))
    sh["w1c"] = np.stack(w1c)
    w2c = []
    for nm in ("cmp_k_w2", "cmp_v_w2"):
        w2 = inp[nm][0]
        w2c.append(np.ascontiguousarray(w2.reshape(2, 128, 128).transpose(1, 0, 2)).reshape(128, 256))
    sh["w2c"] = np.stack(w2c)
    sh["posc"] = np.stack([np.ascontiguousarray(inp["cmp_k_pos"][0].T), np.ascontiguousarray(inp["cmp_v_pos"][0].T)])
    w_out = inp["w_out"][0]
    sh["wout"] = np.stack([_kc_layout(w_out[:, cb * 512:(cb + 1) * 512]) for cb in range(4)])
    w1 = inp["ffn_w1"][0]
    w3 = inp["ffn_w3"][0]
    sh["w1t"] = np.ascontiguousarray(w1.reshape(16, 128, NFC, 128).transpose(2, 1, 0, 3)).reshape(NFC, 128, 16 * 128)
    sh["w3t"] = np.ascontiguousarray(w3.reshape(16, 128, NFC, 128).transpose(2, 1, 0, 3)).reshape(NFC, 128, 16 * 128)
    w2 = inp["ffn_w2"][0]
    sh["w2t"] = np.ascontiguousarray(w2.reshape(11, 4, 128, 4, 512).transpose(3, 0, 2, 1, 4)).reshape(4, 11, 128, 4 * 512)
    ln = np.stack([inp["ln1_g"][0], inp["ln1_b"][0], inp["ln2_g"][0], inp["ln2_b"][0]])
    sh["ln"] = np.ascontiguousarray(np.broadcast_to(ln[:, None, :], (4, 128, D))).astype(np.float32)
    sh["ident"] = np.eye(128, dtype=np.float32)
    j = np.arange(128)[:, None]
    i = np.arange(128)[None, :]
    same = (j // 64) == (i // 64)
    sh["umat"] = ((j <= i) & same).astype(np.float32)
    sh["lmat"] = ((j > i) & same).astype(np.float32)
    sh["cind"] = np.stack([(np.arange(128) < 64), (np.arange(128) >= 64)], axis=1).astype(np.float32)
    pm = np.zeros((32, 32), np.float32)
    for m in range(32):
        pm[(m + 16) % 32, m] = 1.0
    sh["pm"] = pm
    sh["wedge"] = np.where(j > i, 0.0, NEG).astype(np.float32)
    sh["wdiag"] = np.where(j <= i, 0.0, NEG).astype(np.float32)
    n = np.arange(128)[:, None]
    blk = np.arange(32)[None, :]
    ovl = ((16 * n < 64 * blk + 64) & (64 * blk < 16 * n + 32)).astype(np.float32)
    ovl[127] = 0.0
    sh["ovl"] = ovl
    ekt = np.zeros((128, 16, 128), np.float32)
    for kt in range(16):
        for jj in range(128):
            ekt[2 * kt + jj // 64, kt, jj] = 1.0
    sh["ekt"] = ekt.reshape(128, 16 * 128)
    return sh


def prep_core(inp, b, half):
    x = inp["x"]
    own = x[b, half * NO:(half + 1) * NO]
    ctx = x[b, 0:NO] if half == 1 else np.zeros((NO, D), np.float32)
    xbuf = np.concatenate([ctx, own], axis=0)
    pc = {}
    pc["xT"] = np.ascontiguousarray(xbuf.T.reshape(16, 128, NB).transpose(1, 0, 2))
    pc["xo"] = np.ascontiguousarray(own.reshape(8, 128, D))
    off = 0 if half == 1 else -NO
    pos = (np.arange(NB) + off).astype(np.float32)
    inv = np.power(np.float32(500000.0), -np.arange(0, 32, 2, dtype=np.float32) / np.float32(32))
    ang = pos[None, :] * inv[:, None]
    c, s = np.cos(ang), np.sin(ang)
    pc["cosT"] = np.concatenate([c, c], axis=0).astype(np.float32)
    pc["sinT"] = np.concatenate([-s, s], axis=0).astype(np.float32)
    pc["wctx"] = np.full((128, 128), NEG if half == 0 else 0.0, np.float32)
    n = np.arange(128)[:, None]
    q = np.arange(NO)[None, :]
    t_true = half * NO + q
    n_true = n + (0 if half == 1 else -64)
    valid = (n_true >= 0) & (n < 127) & (16 * n_true + 31 <= t_true)
    pc["cmpb"] = np.where(valid, 0.0, NEG).astype(np.float32)
    pc["cmpb"][127, :] = -780.0
    qq = np.arange(NO)
    t_true = half * NO + qq
    cur = t_true // 64
    jb = np.arange(32)[None, :]
    jt = jb + (0 if half == 1 else -16)
    val = (jt >= 0) & (jt <= cur[:, None])
    forced = val & ((jt == 0) | (jt == cur[:, None]) | (jt == cur[:, None] - 1))
    A = (val & ~forced).astype(np.float32)
    Bt = forced.astype(np.float32) * 1e4 + (1.0 - val.astype(np.float32)) * (-1e4)
    def tl(a):
        return np.ascontiguousarray(a.reshape(8, 128, 32).transpose(1, 0, 2)).reshape(128, 8 * 32).astype(np.float32)
    pc["tka"], pc["tkb"], pc["tkv"] = tl(A), tl(Bt), tl(val.astype(np.float32))
    return pc


_NC_CACHE = {}


def kernel(**inputs):
    inp = {k: np.asarray(v) for k, v in inputs.items()}
    sh = prep_shared(inp)
    in_maps = []
    for c in range(8):
        m = dict(sh)
        m.update(prep_core(inp, c // 2, c % 2))
        in_maps.append(m)
    if "nc" not in _NC_CACHE:
        _NC_CACHE["nc"] = build_nc()
    nc = _NC_CACHE["nc"]
    res = run_bass_kernel_spmd(nc, in_maps, core_ids=list(range(8)))
    out = np.zeros((4, SEQ, D), np.float32)
    for c in range(8):
        b, half = c // 2, c % 2
        out[b, half * NO:(half + 1) * NO] = res.results[c]["out"].reshape(NO, D)
    return out
```

```python
import math
from contextlib import ExitStack

import numpy as np
import concourse.bass as bass
import concourse.mybir as mybir
from concourse.bass_utils import run_bass_kernel_spmd

F32 = mybir.dt.float32
BF16 = mybir.dt.bfloat16
AF = mybir.ActivationFunctionType
ALU = mybir.AluOpType
AX = mybir.AxisListType

D = 2048
SEQ = 2048
NB = 2048
NO = 1024
FF = 5632
NFC = FF // 128
ALPHA = 2.0 ** 0.25
EPS = 1e-5
NEG = -30000.0
EXTRA_DBG = []
SKIP = set()
GLA_H = 4
GLA_T = 16
NSA_BR = {'slc', 'win'}
SCALE = 128.0 ** -0.5

O_GQ, O_GK, O_GV, O_GO, O_GLR = 0, 512, 1024, 2048, 3072
O_NQ = 3088
O_KC, O_VC, O_KS, O_VS, O_KW, O_VW = 4112, 4368, 4624, 4880, 5136, 5392
O_GATE = 5648


class Buf:
    __slots__ = ("name", "w", "r", "excl")

    def __init__(self, name="", excl=False):
        self.name = name
        self.w = None
        self.r = {}
        self.excl = excl


class Sched:
    ENGS = ("pe", "act", "dve", "pool", "sp")
    NDS = 10

    def __init__(self, nc, es):
        self.nc = nc
        self.q = {e: [] for e in self.ENGS}
        self.cnt = {e: 0 for e in self.ENGS}
        self.waited = {e: {} for e in self.ENGS}
        self.dma_state = {qn: {"rr": 0, "n": [0] * self.NDS} for qn in ("sp", "pool", "act")}
        self.sems = {}
        for e in self.ENGS:
            self.sems["e_" + e] = es.enter_context(nc.semaphore("e_" + e))
        for qn in ("sp", "pool"):
            for k in range(self.NDS):
                sk = "d_%s_%d" % (qn, k)
                self.sems[sk] = es.enter_context(nc.semaphore(sk))

    def _collect(self, eng, reads, writes):
        deps = {}

        def add(tok, kind):
            if tok is None:
                return
            sk, val, e2 = tok
            if e2 == eng and eng == "pe":
                return
            if deps.get(sk, 0) < val:
                deps[sk] = val

        for b in reads:
            add(b.w, "raw")
            if b.excl:
                for sk, (val, e2) in b.r.items():
                    if e2 != eng:
                        add((sk, val, e2), "rar")
        for b in writes:
            add(b.w, "waw")
            for sk, (val, e2) in b.r.items():
                add((sk, val, e2), "war")
        waits = []
        wd = self.waited[eng]
        for sk, val in deps.items():
            if wd.get(sk, 0) >= val:
                continue
            wd[sk] = val
            waits.append((sk, val))
        return waits

    def _commit(self, tok, reads, writes):
        sk, val, e = tok
        for b in reads:
            old = b.r.get(sk)
            if old is None or old[0] < val:
                b.r[sk] = (val, e)
        for b in writes:
            b.w = tok
            b.r = {}

    LIMIT = None
    total = 0
    lines = []

    def _skip(self):
        import sys
        Sched.total += 1
        f = sys._getframe(2)
        Sched.lines.append(f.f_lineno)
        return Sched.LIMIT is not None and Sched.total > Sched.LIMIT

    def op(self, eng, fn, reads=(), writes=()):
        if self._skip():
            return None
        waits = self._collect(eng, reads, writes)
        self.cnt[eng] += 1
        tok = ("e_" + eng, self.cnt[eng], eng)
        self.q[eng].append((waits, fn, ("e_" + eng, 1)))
        self._commit(tok, reads, writes)
        return tok

    def dma(self, qn, fn, reads=(), writes=()):
        if self._skip():
            return None
        st = self.dma_state[qn]
        k = st["rr"]
        st["rr"] = (k + 1) % self.NDS
        sk = "d_%s_%d" % (qn, k)
        waits = self._collect(qn, reads, writes)
        prev = st["n"][k] * 16
        wd = self.waited[qn]
        if prev > 0 and wd.get(sk, 0) < prev:
            wd[sk] = prev
            waits.append((sk, prev))
        st["n"][k] += 1
        tok = (sk, st["n"][k] * 16, "dma_" + qn)
        self.q[qn].append((waits, fn, (sk, 16)))
        self._commit(tok, reads, writes)
        return tok

    def barrier(self):
        toks = []
        for e in self.ENGS:
            if self.cnt[e] > 0:
                toks.append(("e_" + e, self.cnt[e]))
        for qn, st in self.dma_state.items():
            for k, n in enumerate(st["n"]):
                if n > 0:
                    toks.append(("d_%s_%d" % (qn, k), n * 16))
        for e in self.ENGS:
            waits = []
            for sk, val in toks:
                if sk == "e_" + e:
                    continue
                if self.waited[e].get(sk, 0) < val:
                    self.waited[e][sk] = val
                    waits.append((sk, val))
            self.q[e].append((waits, None, None))

    def emit(self):
        nc = self.nc
        sems = self.sems
        with nc.Block() as block:
            def run(engname):
                def body(engine):
                    for waits, fn, inc in self.q[engname]:
                        for sk, val in waits:
                            engine.wait_ge(sems[sk], val)
                        if fn is not None:
                            fn(engine).then_inc(sems[inc[0]], inc[1])
                return body

            block.tensor(run("pe"))
            block.scalar(run("act"))
            block.vector(run("dve"))
            block.gpsimd(run("pool"))
            block.sync(run("sp"))
        self.q = {e: [] for e in self.ENGS}


def build_nc(dbg=()):
    nc = bass.Bass("TRN2", target_bir_lowering=False)

    def din(name, shape, dt=F32):
        return nc.dram_tensor(name, list(shape), dt, kind="ExternalInput").ap()

    xT_d = din("xT", [128, 16, NB])
    xo_d = din("xo", [8, 128, D])
    wg_d = din("wg", [4, 128, 16 * 768])
    wglr_d = din("wglr", [128, 16 * 16])
    w2aug_d = din("w2aug", [17, 512])
    normw_d = din("normw", [128, 256])
    wn_d = din("wn", [16, 128, 16 * 128])
    wv_d = din("wv", [2, 128, 16 * 256])
    wgate_d = din("wgate", [128, 16 * 24])
    w1c_d = din("w1c", [2, 128, 32 * 256])
    w2c_d = din("w2c", [2, 128, 2 * 128])
    posc_d = din("posc", [2, 128, 32])
    wout_d = din("wout", [4, 128, 16 * 512])
    w1t_d = din("w1t", [NFC, 128, 16 * 128])
    w3t_d = din("w3t", [NFC, 128, 16 * 128])
    w2t_d = din("w2t", [4, 11, 128, 4 * 512])
    ln_d = din("ln", [4, 128, D])
    ident_d = din("ident", [128, 128])
    umat_d = din("umat", [128, 128])
    lmat_d = din("lmat", [128, 128])
    cind_d = din("cind", [128, 2])
    pm_d = din("pm", [32, 32])
    cos_d = din("cosT", [32, NB])
    sin_d = din("sinT", [32, NB])
    wedge_d = din("wedge", [128, 128])
    wdiag_d = din("wdiag", [128, 128])
    wctx_d = din("wctx", [128, 128])
    cmpb_d = din("cmpb", [128, NO])
    ovl_d = din("ovl", [128, 32])
    tka_d = din("tka", [128, 8 * 32])
    tkb_d = din("tkb", [128, 8 * 32])
    tkv_d = din("tkv", [128, 8 * 32])
    ekt_d = din("ekt", [128, 16 * 128])
    out_d = nc.dram_tensor("out", [8, 128, D], F32, kind="ExternalOutput").ap()
    dbg_d = {}
    for name, shape in dbg:
        dbg_d[name] = nc.dram_tensor("dbg_" + name, list(shape), F32, kind="ExternalOutput").ap()

    with ExitStack() as es:
        S = Sched(nc, es)

        uid = [0]

        def sb(stack, name, shape, dt):
            uid[0] += 1
            return stack.enter_context(nc.sbuf_tensor("%s_u%d" % (name, uid[0]), list(shape), dt))

        psf = [(es.enter_context(nc.psum_tensor("psf%d" % i, [128, 512], F32)), Buf("psf%d" % i, True)) for i in range(6)]
        psb = [(es.enter_context(nc.psum_tensor("psb%d" % i, [128, 1024], BF16)), Buf("psb%d" % i, True)) for i in range(2)]
        rr = {"f": 0, "b": 0}

        def ps_f(fixed=None, lo=0):
            if fixed is not None:
                return psf[fixed]
            r = psf[lo + rr["f"] % (6 - lo)]
            rr["f"] += 1
            return r

        def ps_b():
            r = psb[rr["b"] % 2]
            rr["b"] += 1
            return r

        def const(name, src, shape, dt):
            t = sb(es, "c_" + name, shape, dt)
            b = Buf(name)
            q = "pool" if dt == BF16 else "sp"
            S.dma(q, lambda e: e.dma_start(out=t[:], in_=src), writes=[b])
            return t, b

        id16, b_id16 = const("id16", ident_d, [128, 128], BF16)
        id32, b_id32 = const("id32", ident_d, [128, 128], F32)
        umat, b_umat = const("umat", umat_d, [128, 128], F32)
        lmat, b_lmat = const("lmat", lmat_d, [128, 128], F32)
        cind, b_cind = const("cind", cind_d, [128, 2], F32)
        normw, b_normw = const("normw", normw_d, [128, 256], F32)
        mixT = sb(es, "mixT", [128, 16, NO], BF16)
        b_mixT = [[Buf("mixT%d_%d" % (c, t)) for t in range(8)] for c in range(16)]

        def dump(name, ap, bufs):
            if name in dbg_d:
                S.dma("sp", lambda e: e.dma_start(out=dbg_d[name], in_=ap), reads=bufs)

        with ExitStack() as pm:
            xT16 = sb(pm, "xT16", [128, 16, NB], BF16)
            b_xT = [[Buf("xT%d_%d" % (k, tb)) for tb in range(4)] for k in range(16)]

            def load_xT(tb):
                for kc in range(16):
                    S.dma("pool", lambda e, kc=kc: e.dma_start(out=xT16[:, kc, tb * 512:(tb + 1) * 512], in_=xT_d[:, kc, tb * 512:(tb + 1) * 512]),
                          writes=[b_xT[kc][tb]])

            load_xT(0)
            nsa_dd = dict(wn=wn_d, wv=wv_d, wgate=wgate_d, w1c=w1c_d, w2c=w2c_d, posc=posc_d, pm=pm_d, cos=cos_d, sin=sin_d,
                          wedge=wedge_d, wdiag=wdiag_d, wctx=wctx_d, cmpb=cmpb_d, ovl=ovl_d, tka=tka_d, tkb=tkb_d, tkv=tkv_d, ekt=ekt_d)
            nsa_deferred = []
            nsa_consts = NSA_consts(nc, S, pm, sb, nsa_dd, nsa_deferred)
            with ExitStack() as pg:
                wg16 = [sb(pg, "wg16_%d" % i, [128, 16, 768], BF16) for i in range(2)]
                b_wg = [Buf("wg0"), Buf("wg1")]
                wglr16 = sb(pg, "wglr16", [128, 16, 16], BF16)
                b_wglr = Buf("wglr")
                w2aug = sb(pg, "w2aug_s", [17, 512], F32)
                b_w2aug = Buf("w2aug")
                glrT = sb(pg, "glrT", [17, NB], F32)
                b_glrT4 = [Buf("glrT%d" % tb) for tb in range(4)]
                S.dma("pool", lambda e: e.dma_start(out=wglr16[:].rearrange("p a b -> p (a b)"), in_=wglr_d), writes=[b_wglr])
                S.dma("sp", lambda e: e.dma_start(out=w2aug[:], in_=w2aug_d), writes=[b_w2aug])

                def load_wg(h):
                    S.dma("pool", lambda e: e.dma_start(out=wg16[h % 2][:].rearrange("p a b -> p (a b)"), in_=wg_d[h]),
                          writes=[b_wg[h % 2]])

                load_wg(0)
                for tb in range(1, 4):
                    load_xT(tb)
                load_wg(1)
                for th_ in nsa_deferred:
                    th_()
                S.op("dve", lambda e: e.memset(glrT[:], 1.0), writes=b_glrT4)

                def emit_glr(tb, bank=None):
                    pt, pb = ps_f() if bank is None else psf[bank]
                    for kc in range(16):
                        S.op("pe", lambda e, kc=kc: e.matmul(
                            pt[0:16, :], lhsT=wglr16[:, kc, :], rhs=xT16[:, kc, tb * 512:(tb + 1) * 512],
                            start=(kc == 0), stop=(kc == 15)), reads=[b_wglr, b_xT[kc][tb]], writes=[pb])
                    S.op("act", lambda e: e.copy(out=glrT[0:16, tb * 512:(tb + 1) * 512], in_=pt[0:16, :]),
                         reads=[pb], writes=[b_glrT4[tb]])

                emit_glr(0)

                NBUF = 2
                def mk(name, shape, dt):
                    return [(sb(pg, "%s%d" % (name, i), shape, dt), Buf("%s%d" % (name, i))) for i in range(NBUF)]
                sp_b = mk("sp", [128, 128], F32)
                ez_b = mk("ez", [128, 128], F32)
                e1_b = mk("e1", [128, 128], F32)
                e2_b = mk("e2", [128, 128], F32)
                e3_b = mk("e3", [128, 128], F32)
                dec_b = mk("dec", [128, 2], F32)
                qd_b = mk("qd", [128, 128], BF16)
                ki_b = mk("ki", [128, 128], BF16)
                ks_b = mk("kst", [128, 128], BF16)
                v16_b = mk("v16", [128, 256], BF16)
                qdp_b = mk("qdp", [128, 2, 128], BF16)
                kiT_b = mk("kiT", [128, 128], BF16)
                at_b = mk("at", [128, 128], BF16)
                ssq_b = mk("ssq", [128, 2], F32)
                junk_b = mk("junk", [128, 256], F32)
                y_b = mk("y", [128, 256], F32)
                sg_b = mk("sg", [128, 256], F32)
                gO_b = mk("gO", [128, 256], F32)
                yg_b = mk("yg", [128, 256], BF16)
                st32 = sb(pg, "st32", [128, 256], F32)
                b_st32 = Buf("st32")
                st16 = mk("st16", [128, 256], BF16)
                for i in range(NBUF):
                    S.op("dve", lambda e, i=i: e.memset(qdp_b[i][0][:], 0.0), writes=[qdp_b[i][1]])

                free_banks = list(range(6))

                def palloc():
                    assert free_banks, "GLA: out of PSUM banks"
                    return free_banks.pop(0)

                def pfree(i):
                    free_banks.append(i)

                tile_ctr = [0]

                def make_proj(h, t):
                    own = t >= 8
                    wt, wb = wg16[h % 2], b_wg[h % 2]
                    i2 = tile_ctr[0] % NBUF
                    tile_ctr[0] += 1
                    ia, ib = palloc(), palloc()
                    pA, bA = psf[ia]
                    pB, bB = psf[ib]
                    tok = slice(t * 128, (t + 1) * 128)
                    c0 = 0 if own else 128
                    thunks = []
                    thunks.append(lambda: S.op("pe", lambda e: e.matmul(
                        pB[:, 256:384], lhsT=glrT[0:17, tok], rhs=w2aug[0:17, h * 128:(h + 1) * 128], start=True, stop=True),
                        reads=[b_glrT4[t // 4], b_w2aug], writes=[bB]))
                    for kc in range(16):
                        thunks.append(lambda kc=kc: S.op("pe", lambda e: e.matmul(
                            pA[:, c0:512], lhsT=xT16[:, kc, tok], rhs=wt[:, kc, c0:512], start=(kc == 0), stop=(kc == 15)),
                            reads=[b_xT[kc][t // 4], wb], writes=[bA]))
                    if own:
                        for kc in range(16):
                            thunks.append(lambda kc=kc: S.op("pe", lambda e: e.matmul(
                                pB[:, 0:256], lhsT=xT16[:, kc, tok], rhs=wt[:, kc, 512:768], start=(kc == 0), stop=(kc == 15)),
                                reads=[b_xT[kc][t // 4], wb], writes=[bB]))
                    ez, bez = ez_b[i2]
                    spt, bsp = sp_b[i2]
                    sg, bsg = sg_b[i2]

                    done = {"sp": False}

                    def post_sp():
                        if done["sp"]:
                            return
                        done["sp"] = True
                        S.op("act", lambda e: e.activation(out=ez[:], in_=pB[:, 256:384], func=AF.Exp, scale=-1.0), reads=[bB], writes=[bez])
                        S.op("act", lambda e: e.activation(out=spt[:], in_=ez[:], func=AF.Ln, bias=1.0), reads=[bez], writes=[bsp])

                    def post():
                        post_sp()
                        if own:
                            gO, bgO = gO_b[i2]
                            S.op("act", lambda e: e.activation(out=sg[:], in_=pB[:, 0:256], func=AF.Exp, scale=-1.0), reads=[bB], writes=[bsg])
                            S.op("act", lambda e: e.copy(out=gO[:], in_=pB[:, 0:256]), reads=[bB], writes=[bgO])
                            S.op("dve", lambda e: e.tensor_scalar(out=sg[:], in0=sg[:], scalar1=1.0, scalar2=None, op0=ALU.add), reads=[bsg], writes=[bsg])
                            S.op("dve", lambda e: e.reciprocal(out=sg[:], in_=sg[:]), reads=[bsg], writes=[bsg])
                            S.op("dve", lambda e: e.tensor_tensor(out=sg[:], in0=sg[:], in1=gO[:], op=ALU.mult), reads=[bsg, bgO], writes=[bsg])
                        pfree(ib)

                    return dict(h=h, t=t, own=own, i2=i2, ia=ia, pA=pA, bA=bA, thunks=thunks, post=post, post_sp=post_sp,
                                spt=spt, bsp=bsp, sg=sg, bsg=bsg)

                def fill(nxt, n):
                    if nxt is None:
                        return
                    for _ in range(n):
                        if nxt["thunks"]:
                            nxt["thunks"].pop(0)()

                def run_tile(cur, nxt, state):
                    h, t, own, i2 = cur["h"], cur["t"], cur["own"], cur["i2"]
                    pA, bA, spt, bsp = cur["pA"], cur["bA"], cur["spt"], cur["bsp"]
                    nfill = (len(nxt["thunks"]) + 3) // 4 if nxt is not None else 0
                    iu = palloc()
                    pU, bU = psf[iu]
                    if own:
                        S.op("pe", lambda e: e.matmul(pU[:, 0:128], lhsT=umat[:], rhs=spt[:], start=True, stop=True),
                             reads=[b_umat, bsp], writes=[bU])
                    S.op("pe", lambda e: e.matmul(pU[:, 128:256], lhsT=lmat[:], rhs=spt[:], start=True, stop=True),
                         reads=[b_lmat, bsp], writes=[bU])
                    S.op("pe", lambda e: e.matmul(pU[:, 256:258], lhsT=spt[:], rhs=cind[:], start=True, stop=True),
                         reads=[b_cind, bsp], writes=[bU])
                    e3, be3 = e3_b[i2]
                    dec, bdec = dec_b[i2]
                    if own:
                        e1, be1 = e1_b[i2]
                        e2, be2 = e2_b[i2]
                        S.op("act", lambda e: e.activation(out=e1[:], in_=pU[:, 0:128], func=AF.Exp, scale=-1.0 / 16), reads=[bU], writes=[be1])
                        S.op("act", lambda e: e.activation(out=e2[:], in_=pU[:, 0:128], func=AF.Exp, scale=1.0 / 16), reads=[bU], writes=[be2])
                        qd, bqd = qd_b[i2]
                        ki, bki = ki_b[i2]
                        S.op("dve", lambda e: e.scalar_tensor_tensor(out=qd[:], in0=pA[:, 0:128], scalar=SCALE, in1=e1[:], op0=ALU.mult, op1=ALU.mult),
                             reads=[bA, be1], writes=[bqd])
                        S.op("dve", lambda e: e.tensor_tensor(out=ki[:], in0=pA[:, 128:256], in1=e2[:], op=ALU.mult), reads=[bA, be2], writes=[bki])
                    S.op("act", lambda e: e.activation(out=e3[:], in_=pU[:, 128:256], func=AF.Exp, scale=-1.0 / 16), reads=[bU], writes=[be3])
                    S.op("act", lambda e: e.activation(out=dec[:], in_=pU[:, 256:258], func=AF.Exp, scale=-1.0 / 16), reads=[bU], writes=[bdec])
                    pfree(iu)
                    kst, bks = ks_b[i2]
                    v16, bv16 = v16_b[i2]
                    S.op("dve", lambda e: e.tensor_tensor(out=kst[:], in0=pA[:, 128:256], in1=e3[:], op=ALU.mult), reads=[bA, be3], writes=[bks])
                    S.op("dve", lambda e: e.tensor_copy(out=v16[:], in_=pA[:, 256:512]), reads=[bA], writes=[bv16])
                    pfree(cur["ia"])
                    fill(nxt, nfill + 6 if own else nfill)
                    if nxt is not None:
                        nxt["post_sp"]()
                    if state.get("deferred") is not None:
                        state["deferred"]()
                        state["deferred"] = None
                    if own:
                        pT, bT = ps_b()
                        S.op("pe", lambda e: e.transpose(pT[:, 0:128], qd[:], id16[:]), reads=[bqd, b_id16], writes=[bT])
                        S.op("pe", lambda e: e.transpose(pT[:, 128:256], ki[:], id16[:]), reads=[bki, b_id16], writes=[bT])
                        qdp, bqdp = qdp_b[i2]
                        kiT, bkiT = kiT_b[i2]
                        S.op("dve", lambda e: e.tensor_copy(out=qdp[:, 0, 0:64], in_=pT[:, 0:64]), reads=[bT], writes=[bqdp])
                        S.op("dve", lambda e: e.tensor_copy(out=qdp[:, 1, 64:128], in_=pT[:, 64:128]), reads=[bT], writes=[bqdp])
                        S.op("dve", lambda e: e.tensor_copy(out=kiT[:], in_=pT[:, 128:256]), reads=[bT], writes=[bkiT])
                        fill(nxt, nfill)
                        iat = palloc()
                        pAt, bAt = psf[iat]
                        for c in range(2):
                            S.op("pe", lambda e, c=c: e.matmul(pAt[:, c * 64:(c + 1) * 64], lhsT=kiT[:], rhs=qdp[:, c, c * 64:(c + 1) * 64],
                                                               start=True, stop=True), reads=[bkiT, bqdp], writes=[bAt])
                        at, bat = at_b[i2]
                        S.op("dve", lambda e: e.tensor_tensor(out=at[:], in0=pAt[:, 0:128], in1=umat[:], op=ALU.mult), reads=[bAt, b_umat], writes=[bat])
                        pfree(iat)
                        fill(nxt, nfill)
                        io = palloc()
                        pO, bO = psf[io]
                        S.op("pe", lambda e: e.matmul(pO[:, 0:256], lhsT=at[:], rhs=v16[:], start=True, stop=False), reads=[bat, bv16], writes=[bO])
                    for c in range(2):
                        if own:
                            s16, bs16 = st16[state["v"] % 2]
                            S.op("pe", lambda e, c=c, s16=s16: e.matmul(pO[:, 0:256], lhsT=qdp[:, c, :], rhs=s16[:], start=False, stop=(c == 1)),
                                 reads=[bqdp, bs16], writes=[bO])
                        if t == 15 and c == 1:
                            break
                        isb = palloc()
                        pS, bS = psf[isb]
                        S.op("pe", lambda e, c=c, pS=pS: e.matmul(pS[:, 0:256], lhsT=kst[c * 64:(c + 1) * 64, :], rhs=v16[c * 64:(c + 1) * 64, :],
                                                                  start=True, stop=True), reads=[bks, bv16], writes=[bS])
                        S.op("dve", lambda e, c=c, pS=pS: e.scalar_tensor_tensor(out=st32[:], in0=st32[:], scalar=dec[:, c:c + 1], in1=pS[:, 0:256],
                                                                               op0=ALU.mult, op1=ALU.add), reads=[b_st32, bdec, bS], writes=[b_st32])
                        pfree(isb)
                        state["v"] += 1
                        s16n, bs16n = st16[state["v"] % 2]
                        S.op("act", lambda e, s16n=s16n: e.copy(out=s16n[:], in_=st32[:]), reads=[b_st32], writes=[bs16n])
                        if c == 0:
                            fill(nxt, nfill if not own else nfill)
                    fill(nxt, 1000)
                    if nxt is not None:
                        nxt["post"]()
                    if own:
                        ssq, bssq = ssq_b[i2]
                        junk, bjunk = junk_b[i2]
                        S.op("dve", lambda e: e.memset(ssq[:], 0.0), writes=[bssq])
                        S.op("act", lambda e: e.activation(out=junk[:], in_=pO[:, 0:256], func=AF.Square, accum_out=ssq[:, 0:1]),
                             reads=[bO, bssq], writes=[bjunk, bssq])
                        S.op("act", lambda e: e.activation(out=ssq[:, 1:2], in_=ssq[:, 0:1], func=AF.Ln, scale=1.0 / 256, bias=EPS), reads=[bssq], writes=[bssq])
                        S.op("act", lambda e: e.activation(out=ssq[:, 1:2], in_=ssq[:, 1:2], func=AF.Exp, scale=-0.5), reads=[bssq], writes=[bssq])
                        y, by = y_b[i2]
                        S.op("dve", lambda e: e.scalar_tensor_tensor(out=y[:], in0=pO[:, 0:256], scalar=ssq[:, 1:2], in1=normw[:], op0=ALU.mult, op1=ALU.mult),
                             reads=[bO, bssq, b_normw], writes=[by])
                        pfree(io)
                        sg, bsg = cur["sg"], cur["bsg"]
                        yg, byg = yg_b[i2]
                        S.op("dve", lambda e: e.tensor_tensor(out=yg[:], in0=y[:], in1=sg[:], op=ALU.mult), reads=[by, bsg], writes=[byg])
                        def s6_pe():
                            pT2, bT2 = ps_b()
                            for ec in range(2):
                                S.op("pe", lambda e, ec=ec: e.transpose(pT2[:, ec * 128:(ec + 1) * 128], yg[:, ec * 128:(ec + 1) * 128], id16[:]),
                                     reads=[byg, b_id16], writes=[bT2])
                            qt = t - 8
                            S.op("act", lambda e: e.copy(out=mixT[:, 2 * h:2 * h + 2, qt * 128:(qt + 1) * 128],
                                                         in_=pT2[:, 0:256].rearrange("p (a b) -> p a b", a=2)),
                                 reads=[bT2], writes=[b_mixT[2 * h][qt], b_mixT[2 * h + 1][qt]])
                        state["deferred"] = s6_pe

                tiles = [(h, t) for h in range(GLA_H) for t in range(GLA_T)]
                cur = make_proj(*tiles[0])
                fill(cur, 1000)
                cur["post"]()
                state = {"v": 0}
                for idx, (h, t) in enumerate(tiles):
                    if t == 0:
                        S.op("dve", lambda e: e.memset(st32[:], 0.0), writes=[b_st32])
                        S.op("dve", lambda e: e.memset(st16[0][0][:], 0.0), writes=[st16[0][1]])
                        state["v"] = 0
                    if h == 0 and t % 4 == 2 and t // 4 + 1 < 4:
                        ib_ = palloc()
                        emit_glr(t // 4 + 1, bank=ib_)
                        pfree(ib_)
                    nxt = make_proj(*tiles[idx + 1]) if idx + 1 < len(tiles) else None
                    run_tile(cur, nxt, state)
                    cur = nxt
                    if t == GLA_T - 1 and h + 2 < 4:
                        load_wg(h + 2)
                if state.get("deferred") is not None:
                    state["deferred"]()
                    state["deferred"] = None
                S.barrier()
                S.emit()

            with ExitStack() as pn:
                NSA_phase(nc, S, pn, sb, ps_f, ps_b, xT16, b_xT, mixT, b_mixT, id16, b_id16, id32, b_id32, dump,
                          nsa_dd, nsa_consts)
            if True:
                S.barrier()
                S.emit()

        if "mixT" in dbg_d:
            with ExitStack() as pd:
                m32 = sb(pd, "m32", [128, 16, NO], F32)
                bm = Buf("m32")
                S.op("dve", lambda e: e.tensor_copy(out=m32[:], in_=mixT[:]), reads=[b for r in b_mixT for b in r], writes=[bm])
                S.dma("sp", lambda e: e.dma_start(out=dbg_d["mixT"], in_=m32[:]), reads=[bm])
                S.barrier()
                S.emit()

        with ExitStack() as pf:
          if "ffn" not in SKIP:
            FFN_phase(nc, S, pf, sb, ps_f, ps_b, mixT, b_mixT, id16, b_id16, xo_d, wout_d, w1t_d, w3t_d, w2t_d, ln_d, out_d, dump)
            S.barrier()
            S.emit()
    return nc


def FFN_phase(nc, S, pf, sb, ps_f, ps_b, mixT, b_mixT, id16, b_id16, xo_d, wout_d, w1t_d, w3t_d, w2t_d, ln_d, out_d, dump):
    r1 = sb(pf, "r1", [128, 4, D], F32)
    b_r1 = [[Buf("r1_%d_%d" % (t, c)) for c in range(4)] for t in range(4)]
    hT = sb(pf, "hT", [128, 16, 512], BF16)
    b_hT = [[Buf("hT%d_%d" % (k, t)) for t in range(4)] for k in range(16)]
    gT = sb(pf, "gT", [128, NFC, 512], BF16)
    b_gT = [Buf("gT%d" % j) for j in range(NFC)]
    lng = sb(pf, "lng", [128, D], F32)
    lnb = sb(pf, "lnb", [128, D], F32)
    b_lng, b_lnb = Buf("lng"), Buf("lnb")
    NW = 3
    wo16 = [(sb(pf, "wo16_%d" % i, [128, 16, 256], BF16), Buf("wo%d" % i)) for i in range(2)]
    w1b = [(sb(pf, "w1b_%d" % i, [128, 16, 128], BF16), Buf("w1b%d" % i)) for i in range(NW)]
    w3b = [(sb(pf, "w3b_%d" % i, [128, 16, 128], BF16), Buf("w3b%d" % i)) for i in range(NW)]
    w2b = [(sb(pf, "w2b_%d" % i, [128, 4, 512], BF16), Buf("w2b%d" % i)) for i in range(NW)]
    st = sb(pf, "lnst", [128, 4, 10], F32)
    b_st = [Buf("lnst%d" % t) for t in range(4)]
    sa = [(sb(pf, "sa%d" % i, [128, 512], F32), Buf("sa%d" % i)) for i in range(2)]

    def gview(c0):
        return gT[:, c0:c0 + 4, :].rearrange("p a b -> p (a b)"), b_gT[c0:c0 + 4]

    def layer_norm_all(after_tile=None):
        for tt in range(4):
            rb, src, s, bs = b_r1[tt], r1[:, tt, :], st[:, tt, :], b_st[tt]
            junk, bj = gview(4 * tt)
            S.op("dve", lambda e, s=s: e.memset(s[:, 0:2], 0.0), writes=[bs])
            S.op("act", lambda e, s=s, src=src, junk=junk: e.activation(out=junk, in_=src, func=AF.Identity, accum_out=s[:, 0:1]),
                 reads=rb + [bs], writes=bj + [bs])
            S.op("act", lambda e, s=s, src=src, junk=junk: e.activation(out=junk, in_=src, func=AF.Square, accum_out=s[:, 1:2]),
                 reads=rb + [bs], writes=bj + [bs])
        for tt in range(4):
            s, bs = st[:, tt, :], b_st[tt]
            S.op("dve", lambda e, s=s: e.tensor_scalar(out=s[:, 2:4], in0=s[:, 0:2], scalar1=1.0 / D, scalar2=None, op0=ALU.mult), reads=[bs], writes=[bs])
            S.op("dve", lambda e, s=s: e.tensor_tensor(out=s[:, 4:5], in0=s[:, 2:3], in1=s[:, 2:3], op=ALU.mult), reads=[bs], writes=[bs])
            S.op("dve", lambda e, s=s: e.tensor_tensor(out=s[:, 5:6], in0=s[:, 3:4], in1=s[:, 4:5], op=ALU.subtract), reads=[bs], writes=[bs])
            S.op("dve", lambda e, s=s: e.tensor_scalar(out=s[:, 5:6], in0=s[:, 5:6], scalar1=EPS, scalar2=None, op0=ALU.add), reads=[bs], writes=[bs])
        for tt in range(4):
            s, bs = st[:, tt, :], b_st[tt]
            S.op("act", lambda e, s=s: e.activation(out=s[:, 6:7], in_=s[:, 5:6], func=AF.Sqrt), reads=[bs], writes=[bs])
        for tt in range(4):
            s, bs = st[:, tt, :], b_st[tt]
            S.op("dve", lambda e, s=s: e.reciprocal(out=s[:, 7:8], in_=s[:, 6:7]), reads=[bs], writes=[bs])
            S.op("dve", lambda e, s=s: e.tensor_scalar(out=s[:, 8:9], in0=s[:, 2:3], scalar1=s[:, 7:8], scalar2=-1.0, op0=ALU.mult, op1=ALU.mult),
                 reads=[bs], writes=[bs])
        for tt in range(4):
            rb, src, s, bs = b_r1[tt], r1[:, tt, :], st[:, tt, :], b_st[tt]
            S.op("act", lambda e, s=s, src=src: e.activation(out=src, in_=src, func=AF.Identity, scale=s[:, 7:8], bias=s[:, 8:9]),
                 reads=rb + [bs], writes=rb)
        for tt in range(4):
            rb, src = b_r1[tt], r1[:, tt, :]
            S.op("dve", lambda e, src=src: e.tensor_tensor(out=src, in0=src, in1=lng[:], op=ALU.mult), reads=rb + [b_lng], writes=rb)
            S.op("dve", lambda e, src=src: e.tensor_tensor(out=src, in0=src, in1=lnb[:], op=ALU.add), reads=rb + [b_lnb], writes=rb)
            if after_tile is not None:
                after_tile(tt)

    for th in range(2):
        def load_wo(i, th=th):
            cb, hc = i // 2, i % 2
            wo, bwo = wo16[i % 2]
            src = wout_d[cb].rearrange("p (k c) -> p k c", k=16)[:, :, hc * 256:(hc + 1) * 256]
            S.dma("pool", lambda e: e.dma_start(out=wo[:], in_=src), writes=[bwo])
        load_wo(0)
        load_wo(1)
        for tt in range(4):
            S.dma("sp", lambda e, tt=tt, th=th: e.dma_start(out=r1[:, tt, :], in_=xo_d[th * 4 + tt]), writes=b_r1[tt])
        S.dma("sp", lambda e: e.dma_start(out=lng[:], in_=ln_d[0]), writes=[b_lng])
        S.dma("sp", lambda e: e.dma_start(out=lnb[:], in_=ln_d[1]), writes=[b_lnb])
        for i in range(8):
            cb, hc = i // 2, i % 2
            wo, bwo = wo16[i % 2]
            for tt in range(4):
                qt = th * 4 + tt
                pt, pb = ps_f()
                for kc in range(16):
                    S.op("pe", lambda e, kc=kc, pt=pt, qt=qt, wo=wo: e.matmul(
                        pt[:, 0:256], lhsT=mixT[:, kc, qt * 128:(qt + 1) * 128], rhs=wo[:, kc, :], start=(kc == 0), stop=(kc == 15)),
                        reads=[b_mixT[kc][qt], bwo], writes=[pb])
                dst = r1[:, tt, i * 256:(i + 1) * 256]
                S.op("dve", lambda e, dst=dst, pt=pt: e.scalar_tensor_tensor(out=dst, in0=dst, scalar=ALPHA, in1=pt[:, 0:256], op0=ALU.mult, op1=ALU.add),
                     reads=[pb, b_r1[tt][cb]], writes=[b_r1[tt][cb]])
            if i + 2 < 8:
                load_wo(i + 2)
        def ln1_tail(tt):
            h16, bh = gview(16 + 4 * (tt % 2))
            S.op("act", lambda e: e.copy(out=h16, in_=r1[:, tt, :]), reads=b_r1[tt], writes=bh)
            for k4 in range(4):
                pT, bT = ps_b()
                for k in range(4):
                    kc = k4 * 4 + k
                    S.op("pe", lambda e, k=k, kc=kc, pT=pT: e.transpose(pT[:, k * 128:(k + 1) * 128], h16[:, kc * 128:(kc + 1) * 128], id16[:]),
                         reads=bh + [b_id16], writes=[bT])
                eng = "act" if k4 % 2 == 0 else "dve"
                dstT = hT[:, k4 * 4:(k4 + 1) * 4, tt * 128:(tt + 1) * 128]
                srcT = pT[:, 0:512].rearrange("p (a b) -> p a b", a=4)
                wr = [b_hT[k4 * 4 + k][tt] for k in range(4)]
                if eng == "act":
                    S.op("act", lambda e, dstT=dstT, srcT=srcT: e.copy(out=dstT, in_=srcT), reads=[bT], writes=wr)
                else:
                    S.op("dve", lambda e, dstT=dstT, srcT=srcT: e.tensor_copy(out=dstT, in_=srcT), reads=[bT], writes=wr)
        layer_norm_all(after_tile=ln1_tail)
        if th == 0:
            dump("h", r1[:], [b for r in b_r1 for b in r])
        S.dma("sp", lambda e: e.dma_start(out=lng[:], in_=ln_d[2]), writes=[b_lng])
        S.dma("sp", lambda e: e.dma_start(out=lnb[:], in_=ln_d[3]), writes=[b_lnb])
        hT_all = [b for r in b_hT for b in r]

        def load_w13(j):
            w1, bw1 = w1b[j % NW]
            w3, bw3 = w3b[j % NW]
            S.dma("pool", lambda e: e.dma_start(out=w1[:].rearrange("p a b -> p (a b)"), in_=w1t_d[j]), writes=[bw1])
            S.dma("pool", lambda e: e.dma_start(out=w3[:].rearrange("p a b -> p (a b)"), in_=w3t_d[j]), writes=[bw3])

        for j in range(min(NW - 1, NFC)):
            load_w13(j)
        for j in range(NFC):
            if j + NW - 1 < NFC:
                load_w13(j + NW - 1)
            w1, bw1 = w1b[j % NW]
            w3, bw3 = w3b[j % NW]
            pa, pab = ps_f()
            pb_, pbb = ps_f()
            for kc in range(16):
                S.op("pe", lambda e, kc=kc, pa=pa, w1=w1: e.matmul(pa[:], lhsT=w1[:, kc, :], rhs=hT[:, kc, :], start=(kc == 0), stop=(kc == 15)),
                     reads=[bw1] + b_hT[kc], writes=[pab])
            for kc in range(16):
                S.op("pe", lambda e, kc=kc, pb_=pb_, w3=w3: e.matmul(pb_[:], lhsT=w3[:, kc, :], rhs=hT[:, kc, :], start=(kc == 0), stop=(kc == 15)),
                     reads=[bw3] + b_hT[kc], writes=[pbb])
            s, bs = sa[j % 2]
            S.op("act", lambda e, s=s, pa=pa: e.activation(out=s[:], in_=pa[:], func=AF.Silu), reads=[pab], writes=[bs])
            S.op("dve", lambda e, s=s, pb_=pb_, j=j: e.tensor_tensor(out=gT[:, j, :], in0=s[:], in1=pb_[:], op=ALU.mult),
                 reads=[bs, pbb], writes=[b_gT[j]])
        def load_w2(cb, fg):
            i = (cb * 11 + fg) % NW
            w2, bw2 = w2b[i]
            S.dma("pool", lambda e: e.dma_start(out=w2[:].rearrange("p a b -> p (a b)"), in_=w2t_d[cb, fg]), writes=[bw2])

        seq = [(cb, fg) for cb in range(4) for fg in range(11)]
        for i in range(NW - 1):
            load_w2(*seq[i])
        for idx, (cb, fg) in enumerate(seq):
            if idx + NW - 1 < len(seq):
                load_w2(*seq[idx + NW - 1])
            if fg == 0:
                acc = [ps_f() for _ in range(4)]
            w2, bw2 = w2b[idx % NW]
            for tt in range(4):
                pt, pb = acc[tt]
                for f in range(4):
                    fc = fg * 4 + f
                    S.op("pe", lambda e, pt=pt, fc=fc, tt=tt, w2=w2, f=f: e.matmul(
                        pt[:], lhsT=gT[:, fc, tt * 128:(tt + 1) * 128], rhs=w2[:, f, :], start=(fc == 0), stop=(fc == NFC - 1)),
                        reads=[b_gT[fc], bw2], writes=[pb])
            if fg == 10:
                for tt in range(4):
                    pt, pb = acc[tt]
                    dst = r1[:, tt, cb * 512:(cb + 1) * 512]
                    S.op("dve", lambda e, dst=dst, pt=pt: e.scalar_tensor_tensor(out=dst, in0=dst, scalar=ALPHA, in1=pt[:], op0=ALU.mult, op1=ALU.add),
                         reads=[pb, b_r1[tt][cb]], writes=[b_r1[tt][cb]])
        def ln2_tail(tt, th=th):
            S.dma("sp", lambda e: e.dma_start(out=out_d[th * 4 + tt], in_=r1[:, tt, :]), reads=b_r1[tt])
        layer_norm_all(after_tile=ln2_tail)


def NSA_consts(nc, S, pn, sb, dd, deferred):
    def cst(name, src, shape, dt):
        t = sb(pn, "n_" + name, shape, dt)
        b = Buf(name)
        q = "pool" if dt == BF16 else "sp"
        if len(shape) == 3:
            deferred.append(lambda: S.dma(q, lambda e: e.dma_start(out=t[:].rearrange("p a b -> p (a b)"), in_=src), writes=[b]))
        else:
            deferred.append(lambda: S.dma(q, lambda e: e.dma_start(out=t[:], in_=src), writes=[b]))
        return t, b

    pm32, b_pm = cst("pm", dd["pm"], [32, 32], F32)
    cosT, b_cos = cst("cos", dd["cos"], [32, NB], F32)
    sinT, b_sin = cst("sin", dd["sin"], [32, NB], F32)
    wedge, b_wedge = cst("wedge", dd["wedge"], [128, 128], BF16)
    wdiag, b_wdiag = cst("wdiag", dd["wdiag"], [128, 128], BF16)
    wctx, b_wctx = cst("wctx", dd["wctx"], [128, 128], BF16)
    cmpb, b_cmpb = cst("cmpb", dd["cmpb"], [128, NO], BF16)
    tka, b_tka = cst("tka", dd["tka"], [128, 8, 32], F32)
    tkb, b_tkb = cst("tkb", dd["tkb"], [128, 8, 32], F32)
    tkv, b_tkv = cst("tkv", dd["tkv"], [128, 8, 32], F32)
    ekt, b_ekt = cst("ekt", dd["ekt"], [128, 16, 128], BF16)
    wgate16, b_wgate = cst("wgate", dd["wgate"], [128, 16, 24], BF16)
    vcaug = [sb(pn, "vcaug%d" % g, [128, 161], BF16) for g in range(2)]
    b_vcaug = [Buf("vcaug0"), Buf("vcaug1")]
    kcmpT = [sb(pn, "kcmpT%d" % g, [128, 128], BF16) for g in range(2)]
    b_kcmpT = [Buf("kcmpT0"), Buf("kcmpT1")]
    sig = sb(pn, "sig", [128, 8, 24], F32)
    b_sig = Buf("sig")
    for g in range(2):
        S.op("dve", lambda e, g=g: e.memset(vcaug[g][:], 0.0), writes=[b_vcaug[g]])
        S.op("dve", lambda e, g=g: e.memset(vcaug[g][:, 128:129], 1.0), writes=[b_vcaug[g]])
        S.dma("pool", lambda e, g=g: e.dma_start(out=vcaug[g][:, 129:161], in_=dd["ovl"]), writes=[b_vcaug[g]])
        S.op("dve", lambda e, g=g: e.memset(kcmpT[g][:], 0.0), writes=[b_kcmpT[g]])

    return dict(pm32=pm32, b_pm=b_pm, cosT=cosT, b_cos=b_cos, sinT=sinT, b_sin=b_sin, wedge=wedge, b_wedge=b_wedge, wdiag=wdiag, b_wdiag=b_wdiag, wctx=wctx, b_wctx=b_wctx, cmpb=cmpb, b_cmpb=b_cmpb, tka=tka, b_tka=b_tka, tkb=tkb, b_tkb=b_tkb, tkv=tkv, b_tkv=b_tkv, ekt=ekt, b_ekt=b_ekt, wgate16=wgate16, b_wgate=b_wgate, vcaug=vcaug, b_vcaug=b_vcaug, kcmpT=kcmpT, b_kcmpT=b_kcmpT, sig=sig, b_sig=b_sig)


def NSA_phase(nc, S, pn, sb, ps_f, ps_b, xT16, b_xT, mixT, b_mixT, id16, b_id16, id32, b_id32, dump, dd, NC_):
    (pm32, b_pm, cosT, b_cos, sinT, b_sin, wedge, b_wedge, wdiag, b_wdiag, wctx, b_wctx, cmpb, b_cmpb, tka, b_tka, tkb, b_tkb, tkv, b_tkv, ekt, b_ekt, wgate16, b_wgate, vcaug, b_vcaug, kcmpT, b_kcmpT, sig, b_sig) = [NC_[k] for k in ('pm32', 'b_pm', 'cosT', 'b_cos', 'sinT', 'b_sin', 'wedge', 'b_wedge', 'wdiag', 'b_wdiag', 'wctx', 'b_wctx', 'cmpb', 'b_cmpb', 'tka', 'b_tka', 'tkb', 'b_tkb', 'tkv', 'b_tkv', 'ekt', 'b_ekt', 'wgate16', 'b_wgate', 'vcaug', 'b_vcaug', 'kcmpT', 'b_kcmpT', 'sig', 'b_sig')]
    def load_blk(ring, bi, src):
        t, b = ring[bi[0] % len(ring)]
        bi[0] += 1
        S.dma("pool", lambda e: e.dma_start(out=t[:].rearrange("p a b -> p (a b)"), in_=src), writes=[b])
        return t, b

    def proj_fm(wt, wb, tbs, evac):
        for tb in tbs:
            pt, pb = ps_f()
            for kc in range(16):
                S.op("pe", lambda e, kc=kc, pt=pt, tb=tb: e.matmul(pt[:], lhsT=wt[:, kc, :], rhs=xT16[:, kc, tb * 512:(tb + 1) * 512],
                                                                   start=(kc == 0), stop=(kc == 15)), reads=[wb, b_xT[kc][tb]], writes=[pb])
            evac(tb, pt, pb)

    for qt in range(8):
        pt, pb = ps_f()
        for kc in range(16):
            S.op("pe", lambda e, kc=kc, pt=pt, qt=qt: e.matmul(pt[:, 0:24], lhsT=xT16[:, kc, NO + qt * 128:NO + (qt + 1) * 128], rhs=wgate16[:, kc, :],
                                                               start=(kc == 0), stop=(kc == 15)), reads=[b_wgate, b_xT[kc][2 + qt // 4]], writes=[pb])
        S.op("act", lambda e, pt=pt, qt=qt: e.activation(out=sig[:, qt, :], in_=pt[:, 0:24], func=AF.Sigmoid), reads=[pb], writes=[b_sig])

    with ExitStack() as p1:
        kvT = [[sb(p1, "kvT%d_%d" % (ty, g), [128, NB], BF16) for g in range(2)] for ty in range(2)]
        b_kvT = [[Buf("kvT%d_%d" % (ty, g)) for g in range(2)] for ty in range(2)]
        w1c16 = [sb(p1, "w1c16_%d" % ty, [128, 32, 256], BF16) for ty in range(2)]
        b_w1c = [Buf("w1c0"), Buf("w1c1")]
        w2c16 = [sb(p1, "w2c16_%d" % ty, [128, 2, 128], BF16) for ty in range(2)]
        b_w2c = [Buf("w2c0"), Buf("w2c1")]
        posc16 = [sb(p1, "posc16_%d" % ty, [128, 32], BF16) for ty in range(2)]
        b_posc = [Buf("posc0"), Buf("posc1")]
        ring = [(sb(p1, "wnb1_%d" % i, [128, 16, 128], BF16), Buf("wnb1_%d" % i)) for i in range(2)]
        bi = [0]
        pbias = sb(p1, "pbias", [128, 4], F32)
        b_pbias = Buf("pbias")
        xh = sb(p1, "xh", [128, 127], F32); b_xh = Buf("xh")
        x2 = sb(p1, "x2", [128, 127], F32); b_x2 = Buf("x2")
        sg = sb(p1, "sgc", [128, 127], F32); b_sg = Buf("sgc")
        hid = sb(p1, "hid", [128, 2, 128], BF16); b_hid = Buf("hid")
        items1 = [(g, ty) for g in range(2) for ty in range(2)]
        loaded = [load_blk(ring, bi, dd["wn"][8 * g + 6 + ty]) for g, ty in items1[:2]]
        for ty in range(2):
            S.dma("pool", lambda e, ty=ty: e.dma_start(out=w1c16[ty][:].rearrange("p a b -> p (a b)"), in_=dd["w1c"][ty]), writes=[b_w1c[ty]])
            S.dma("pool", lambda e, ty=ty: e.dma_start(out=w2c16[ty][:].rearrange("p a b -> p (a b)"), in_=dd["w2c"][ty]), writes=[b_w2c[ty]])
            S.dma("pool", lambda e, ty=ty: e.dma_start(out=posc16[ty][:], in_=dd["posc"][ty]), writes=[b_posc[ty]])
        for i1, (g, ty) in enumerate(items1):
            wt, wb = loaded[i1]
            dst, bd = kvT[ty][g], b_kvT[ty][g]
            proj_fm(wt, wb, range(4), lambda tb, pt, pb, dst=dst, bd=bd: S.op(
                "act", lambda e: e.copy(out=dst[:, tb * 512:(tb + 1) * 512], in_=pt[:]), reads=[pb], writes=[bd]))
            if i1 + 2 < len(items1):
                g2, ty2 = items1[i1 + 2]
                loaded.append(load_blk(ring, bi, dd["wn"][8 * g2 + 6 + ty2]))
        pt, pb = ps_f()
        for ty in range(2):
            for cc in range(2):
                for l in range(32):
                    S.op("pe", lambda e, ty=ty, cc=cc, l=l, pt=pt: e.matmul(
                        pt[:, ty * 2 + cc:ty * 2 + cc + 1], lhsT=w1c16[ty][:, l, cc * 128:(cc + 1) * 128], rhs=posc16[ty][:, l:l + 1],
                        start=(l == 0), stop=(l == 31)), reads=[b_w1c[ty], b_posc[ty]], writes=[pb])
        S.op("act", lambda e, pt=pt: e.copy(out=pbias[:], in_=pt[:, 0:4]), reads=[pb], writes=[b_pbias])
        S.op("dve", lambda e: e.memset(hid[:], 0.0), writes=[b_hid])
        for g in range(2):
            for ty in range(2):
                src, bs = kvT[ty][g], b_kvT[ty][g]
                for cc in range(2):
                    pt, pb = ps_f()
                    for l in range(32):
                        S.op("pe", lambda e, ty=ty, cc=cc, l=l, pt=pt, src=src: e.matmul(
                            pt[:, 0:127], lhsT=w1c16[ty][:, l, cc * 128:(cc + 1) * 128], rhs=src[:, l:l + 16 * 126 + 1:16],
                            start=(l == 0), stop=(l == 31)), reads=[b_w1c[ty], bs], writes=[pb])
                    S.op("act", lambda e, pt=pt, ty=ty, cc=cc: e.activation(out=xh[:], in_=pt[:, 0:127], func=AF.Identity,
                                                                          bias=pbias[:, ty * 2 + cc:ty * 2 + cc + 1]),
                         reads=[pb, b_pbias], writes=[b_xh])
                    S.op("dve", lambda e: e.tensor_tensor(out=x2[:], in0=xh[:], in1=xh[:], op=ALU.mult), reads=[b_xh], writes=[b_x2])
                    S.op("dve", lambda e: e.tensor_scalar(out=x2[:], in0=x2[:], scalar1=0.044715, scalar2=1.0, op0=ALU.mult, op1=ALU.add),
                         reads=[b_x2], writes=[b_x2])
                    S.op("dve", lambda e: e.tensor_tensor(out=x2[:], in0=x2[:], in1=xh[:], op=ALU.mult), reads=[b_x2, b_xh], writes=[b_x2])
                    S.op("act", lambda e: e.activation(out=sg[:], in_=x2[:], func=AF.Sigmoid, scale=1.5957691216057308),
                         reads=[b_x2], writes=[b_sg])
                    S.op("dve", lambda e, cc=cc: e.tensor_tensor(out=hid[:, cc, 0:127], in0=xh[:], in1=sg[:], op=ALU.mult),
                         reads=[b_xh, b_sg], writes=[b_hid])
                pt, pb = ps_f()
                if ty == 0:
                    for cc in range(2):
                        S.op("pe", lambda e, cc=cc, pt=pt: e.matmul(pt[:, 0:127], lhsT=w2c16[0][:, cc, :], rhs=hid[:, cc, 0:127],
                                                                    start=(cc == 0), stop=(cc == 1)), reads=[b_w2c[0], b_hid], writes=[pb])
                    S.op("act", lambda e, pt=pt, g=g: e.copy(out=kcmpT[g][:, 0:127], in_=pt[:, 0:127]), reads=[pb], writes=[b_kcmpT[g]])
                else:
                    for cc in range(2):
                        S.op("pe", lambda e, cc=cc, pt=pt: e.matmul(pt[0:127, 0:128], lhsT=hid[:, cc, 0:127], rhs=w2c16[1][:, cc, :],
                                                                    start=(cc == 0), stop=(cc == 1)), reads=[b_w2c[1], b_hid], writes=[pb])
                    S.op("act", lambda e, pt=pt, g=g: e.copy(out=vcaug[g][0:127, 0:128], in_=pt[0:127, 0:128]), reads=[pb], writes=[b_vcaug[g]])
        S.barrier()
        S.emit()

    qraw = sb(pn, "qraw", [128, 4, NO], BF16); b_qraw = [Buf("qraw%d" % r) for r in range(4)]
    qrope = sb(pn, "qrope", [128, 4, NO], BF16); b_qrope = [Buf("qrope%d" % r) for r in range(4)]
    ksT = sb(pn, "ksT", [128, NB], BF16); b_ksT = Buf("ksT")
    kwT = sb(pn, "kwT", [128, NB], BF16); b_kwT = Buf("kwT")
    vsaug = sb(pn, "vsaug", [128, 16, 129], BF16); b_vs = Buf("vsaug")
    vwaug = sb(pn, "vwaug", [128, 16, 129], BF16); b_vw = Buf("vwaug")
    S.op("dve", lambda e: e.memset(vsaug[:, :, 128:129], 1.0), writes=[b_vs])
    S.op("dve", lambda e: e.memset(vwaug[:, :, 128:129], 1.0), writes=[b_vw])
    for g in range(2):
        with ExitStack() as pa:
            ring = [(sb(pa, "wnb2_%d" % i, [128, 16, 128], BF16), Buf("wnb2_%d" % i)) for i in range(3)]
            bi = [0]
            wv16 = sb(pa, "wv16", [128, 16, 256], BF16); b_wv = Buf("wv16")
            q32s = [(sb(pa, "q32_%d" % i, [32, 512], F32), Buf("q32_%d" % i)) for i in range(2)]
            t1 = sb(pa, "t1", [32, 512], F32); b_t1 = Buf("t1")
            t2 = sb(pa, "t2", [32, 512], F32); b_t2 = Buf("t2")
            rope_pending = []
            rope_ctr = [0]
            S.dma("pool", lambda e, g=g: e.dma_start(out=wv16[:].rearrange("p a b -> p (a b)"), in_=dd["wv"][g]), writes=[b_wv])

            def rope_evac(tb, pt, pb, dst, bd, raw=None):
                cs = slice(tb * 512, (tb + 1) * 512)
                q32, b_q32 = q32s[rope_ctr[0] % 2]
                rope_ctr[0] += 1
                while rope_pending:
                    rope_pending.pop(0)()
                if raw is not None:
                    rdst, rb = raw
                    S.op("act", lambda e: e.copy(out=rdst, in_=pt[:]), reads=[pb], writes=[rb])
                S.op("act", lambda e: e.copy(out=q32[:], in_=pt[0:32, :]), reads=[pb], writes=[b_q32])
                S.op("act", lambda e: e.copy(out=dst[32:64, :], in_=pt[32:64, :]), reads=[pb], writes=[bd])
                S.op("act", lambda e: e.copy(out=dst[64:128, :], in_=pt[64:128, :]), reads=[pb], writes=[bd])
                def part2():
                    pw, pwb = ps_f()
                    S.op("pe", lambda e: e.matmul(pw[0:32, :], lhsT=pm32[:], rhs=q32[:], start=True, stop=True), reads=[b_pm, b_q32], writes=[pwb])
                    S.op("dve", lambda e: e.tensor_tensor(out=t1[:], in0=q32[:], in1=cosT[:, cs], op=ALU.mult), reads=[b_q32, b_cos], writes=[b_t1])
                    S.op("dve", lambda e: e.tensor_tensor(out=t2[:], in0=pw[0:32, :], in1=sinT[:, cs], op=ALU.mult), reads=[pwb, b_sin], writes=[b_t2])
                    S.op("dve", lambda e: e.tensor_tensor(out=dst[0:32, :], in0=t1[:], in1=t2[:], op=ALU.add), reads=[b_t1, b_t2], writes=[bd])
                rope_pending.append(part2)

            for r in range(4):
                wt, wb = load_blk(ring, bi, dd["wn"][8 * g + r])
                proj_fm(wt, wb, (2, 3), lambda tb, pt, pb, r=r: rope_evac(
                    tb, pt, pb, qrope[:, r, (tb - 2) * 512:(tb - 1) * 512], b_qrope[r],
                    raw=(qraw[:, r, (tb - 2) * 512:(tb - 1) * 512], b_qraw[r])))
            wt, wb = load_blk(ring, bi, dd["wn"][8 * g + 4])
            proj_fm(wt, wb, range(4), lambda tb, pt, pb: rope_evac(tb, pt, pb, ksT[:, tb * 512:(tb + 1) * 512], b_ksT))
            wt, wb = load_blk(ring, bi, dd["wn"][8 * g + 5])
            proj_fm(wt, wb, range(4), lambda tb, pt, pb: rope_evac(tb, pt, pb, kwT[:, tb * 512:(tb + 1) * 512], b_kwT))
            for t in range(16):
                if t == 1:
                    while rope_pending:
                        rope_pending.pop(0)()
                pt, pb = ps_f()
                for kc in range(16):
                    S.op("pe", lambda e, kc=kc, pt=pt, t=t: e.matmul(pt[:, 0:256], lhsT=xT16[:, kc, t * 128:(t + 1) * 128], rhs=wv16[:, kc, :],
                                                                     start=(kc == 0), stop=(kc == 15)), reads=[b_wv, b_xT[kc][t // 4]], writes=[pb])
                S.op("act", lambda e, pt=pt, t=t: e.copy(out=vsaug[:, t, 0:128], in_=pt[:, 0:128]), reads=[pb], writes=[b_vs])
                S.op("act", lambda e, pt=pt, t=t: e.copy(out=vwaug[:, t, 0:128], in_=pt[:, 128:256]), reads=[pb], writes=[b_vw])
            S.barrier()
            S.emit()
        with ExitStack() as pb_:
            PT = [(sb(pb_, "PT%d" % i, [128, 512], BF16), Buf("PT%d" % i)) for i in range(3)]
            pti = [0]
            PTw = [(sb(pb_, "PTw%d" % i, [128, 5, 128], BF16), Buf("PTw%d" % i)) for i in range(2)]
            ptwi = [0]
            selbT = sb(pb_, "selbT", [128, NO], BF16); b_selbT = [Buf("selbT%d" % q) for q in range(8)]
            S.op("dve", lambda e: e.memset(selbT[:], 0.0), writes=b_selbT)
            onsa = sb(pb_, "onsa", [128, 8, 4, 128], F32); b_onsa = [[Buf("onsa%d_%d" % (q, r)) for r in range(4)] for q in range(8)]
            imp = sb(pb_, "imp", [128, 8, 32], F32); b_imp = [Buf("imp%d" % q) for q in range(8)]
            cmp3s = [(sb(pb_, "cmp3_%d" % i, [128, 32, 32], F32), Buf("cmp3_%d" % i)) for i in range(2)]
            rk = sb(pb_, "rk", [128, 8, 32], F32); b_rk = [Buf("rk%d" % q) for q in range(8)]
            sm = [(sb(pb_, "sm%d" % i, [128, 4], F32), Buf("sm%d" % i)) for i in range(4)]
            smi = [0]
            o16 = [(sb(pb_, "o16_%d" % i, [128, 4, 128], BF16), Buf("o16_%d" % i)) for i in range(2)]
            wstg = [(sb(pb_, "wstg%d" % i, [128, 129], F32), Buf("wstg%d" % i)) for i in range(12)]
            wsi = [0]
            zz = sb(pb_, "zz", [128, 258], BF16); b_zz = Buf("zz")
            S.op("dve", lambda e: e.memset(zz[:], 0.0), writes=[b_zz])

            def finalize(pacc, pab, col0, qt, r, branch, first):
                s, bs = sm[smi[0] % 4]
                smi[0] += 1
                S.op("dve", lambda e: e.reciprocal(out=s[:, 1:2], in_=pacc[:, col0 + 128:col0 + 129]), reads=[pab], writes=[bs])
                gcol = (4 * g + r) * 3 + branch
                S.op("dve", lambda e: e.tensor_tensor(out=s[:, 2:3], in0=s[:, 1:2], in1=sig[:, qt, gcol:gcol + 1], op=ALU.mult),
                     reads=[bs, b_sig], writes=[bs])
                dst = onsa[:, qt, r, :]
                if first:
                    S.op("act", lambda e: e.activation(out=dst, in_=pacc[:, col0:col0 + 128], func=AF.Copy, scale=s[:, 2:3]),
                         reads=[pab, bs], writes=[b_onsa[qt][r]])
                else:
                    S.op("dve", lambda e: e.scalar_tensor_tensor(out=dst, in0=pacc[:, col0:col0 + 128], scalar=s[:, 2:3], in1=dst,
                                                                 op0=ALU.mult, op1=ALU.add), reads=[pab, bs, b_onsa[qt][r]], writes=[b_onsa[qt][r]])
                return s, bs

            def cmp_scores(qb, r):
                qs = slice(qb * 512, (qb + 1) * 512)
                pt, pb = ps_f()
                S.op("pe", lambda e: e.matmul(pt[:], lhsT=kcmpT[g][:], rhs=qraw[:, r, qs], start=True, stop=False),
                     reads=[b_kcmpT[g], b_qraw[r]], writes=[pb])
                S.op("pe", lambda e: e.matmul(pt[:], lhsT=id16[:], rhs=cmpb[:, qs], start=False, stop=True),
                     reads=[b_id16, b_cmpb], writes=[pb])
                P, bP = PT[pti[0] % 3]
                pti[0] += 1
                S.op("act", lambda e: e.activation(out=P[:], in_=pt[:], func=AF.Exp, scale=SCALE), reads=[pb], writes=[bP])
                return P, bP

            def cmp_pv(qb, r, P, bP):
                for j in range(4):
                    qt = qb * 4 + j
                    po, pob = ps_f()
                    S.op("pe", lambda e, po=po, j=j: e.matmul(po[:, 0:161], lhsT=P[:, j * 128:(j + 1) * 128], rhs=vcaug[g][:],
                                                              start=True, stop=True), reads=[bP, b_vcaug[g]], writes=[pob])
                    s, bs = finalize(po, pob, 0, qt, r, 0, True)
                    if r == 0:
                        S.op("dve", lambda e, po=po, s=s, qt=qt: e.tensor_scalar(out=imp[:, qt, :], in0=po[:, 129:161], scalar1=s[:, 1:2],
                                                                                scalar2=None, op0=ALU.mult), reads=[pob, bs], writes=[b_imp[qt]])
                    else:
                        S.op("dve", lambda e, po=po, s=s, qt=qt: e.scalar_tensor_tensor(
                            out=imp[:, qt, :], in0=po[:, 129:161], scalar=s[:, 1:2], in1=imp[:, qt, :], op0=ALU.mult, op1=ALU.add),
                            reads=[pob, bs, b_imp[qt]], writes=[b_imp[qt]])

            def topk_dve(qts):
                for qt in qts:
                    iv = imp[:, qt, :]
                    rq = rk[:, qt, :]
                    cmp3, b_cmp3 = cmp3s[qt % 2]
                    S.op("dve", lambda e, iv=iv, qt=qt: e.tensor_tensor(out=iv, in0=iv, in1=tka[:, qt, :], op=ALU.mult), reads=[b_imp[qt], b_tka], writes=[b_imp[qt]])
                    S.op("dve", lambda e, iv=iv, qt=qt: e.tensor_tensor(out=iv, in0=iv, in1=tkb[:, qt, :], op=ALU.add), reads=[b_imp[qt], b_tkb], writes=[b_imp[qt]])
                    S.op("dve", lambda e, iv=iv, cmp3=cmp3: e.tensor_tensor(out=cmp3[:], in0=iv.unsqueeze(1).to_broadcast([128, 32, 32]),
                                                                 in1=iv.unsqueeze(2).to_broadcast([128, 32, 32]), op=ALU.is_gt),
                         reads=[b_imp[qt]], writes=[b_cmp3])
                    S.op("dve", lambda e, rq=rq, cmp3=cmp3: e.tensor_reduce(out=rq, in_=cmp3[:], axis=AX.X, op=ALU.add), reads=[b_cmp3], writes=[b_rk[qt]])
                    S.op("dve", lambda e, rq=rq: e.tensor_scalar(out=rq, in0=rq, scalar1=15.5, scalar2=None, op0=ALU.is_lt), reads=[b_rk[qt]], writes=[b_rk[qt]])
                    S.op("dve", lambda e, rq=rq, qt=qt: e.tensor_tensor(out=rq, in0=rq, in1=tkv[:, qt, :], op=ALU.mult), reads=[b_rk[qt], b_tkv], writes=[b_rk[qt]])
                    S.op("dve", lambda e, rq=rq: e.tensor_scalar(out=rq, in0=rq, scalar1=-NEG, scalar2=NEG, op0=ALU.mult, op1=ALU.add),
                         reads=[b_rk[qt]], writes=[b_rk[qt]])


            def topk_transposes(qts):
                for qt in qts:
                    pt, pb = ps_f()
                    S.op("pe", lambda e, pt=pt, qt=qt: e.transpose(pt[0:32, 0:128], rk[:, qt, :], id32[:]), reads=[b_rk[qt], b_id32], writes=[pb])
                    S.op("act", lambda e, pt=pt, qt=qt: e.copy(out=selbT[0:32, qt * 128:(qt + 1) * 128], in_=pt[0:32, 0:128]), reads=[pb], writes=[b_selbT[qt]])

            def win_scores(qt, r):
                pw1, pw1b = ps_f()
                pw2, pw2b = ps_f()
                for m in range(5):
                    kt = 4 + qt + m
                    tgt, tb_ = (pw1, pw1b) if m < 4 else (pw2, pw2b)
                    cs = slice((m % 4) * 128, (m % 4 + 1) * 128)
                    extra = []
                    if m == 0:
                        extra.append((wedge, b_wedge))
                    if m == 4:
                        extra.append((wdiag, b_wdiag))
                    if kt < 8:
                        extra.append((wctx, b_wctx))
                    S.op("pe", lambda e, tgt=tgt, cs=cs, kt=kt, ne=len(extra): e.matmul(
                        tgt[:, cs], lhsT=kwT[:, kt * 128:(kt + 1) * 128], rhs=qrope[:, r, qt * 128:(qt + 1) * 128], start=True, stop=(ne == 0)),
                        reads=[b_kwT, b_qrope[r]], writes=[tb_])
                    for xi, (xt, xb) in enumerate(extra):
                        S.op("pe", lambda e, tgt=tgt, cs=cs, xt=xt, xi=xi, ne=len(extra): e.matmul(
                            tgt[:, cs], lhsT=id16[:], rhs=xt[:], start=False, stop=(xi == ne - 1)), reads=[b_id16, xb], writes=[tb_])
                Pw, bPw = PTw[ptwi[0] % 2]
                ptwi[0] += 1
                S.op("act", lambda e: e.activation(out=Pw[:, 0:4, :], in_=pw1[:].rearrange("p (a b) -> p a b", a=4), func=AF.Exp, scale=SCALE),
                     reads=[pw1b], writes=[bPw])
                S.op("act", lambda e: e.activation(out=Pw[:, 4, :], in_=pw2[:, 0:128], func=AF.Exp, scale=SCALE),
                     reads=[pw2b], writes=[bPw])
                return Pw, bPw

            def win_pv(qt, r, Pw, bPw):
                po, pob = ps_f()
                for m in range(5):
                    kt = 4 + qt + m
                    S.op("pe", lambda e, m=m, kt=kt: e.matmul(po[:, 0:129], lhsT=Pw[:, m, :], rhs=vwaug[:, kt, :],
                                                              start=(m == 0), stop=(m == 4)), reads=[bPw, b_vw], writes=[pob])
                stg, bstg = wstg[wsi[0] % len(wstg)]
                wsi[0] += 1
                S.op("act", lambda e: e.copy(out=stg[:], in_=po[:, 0:129]), reads=[pob], writes=[bstg])
                finalize(stg, bstg, 0, qt, r, 2, False)

            cmp_list = [(qb, r) for qb in range(2) for r in range(4)]
            win_list = [(4 * qb + j, r) for qb in range(2) for r in range(4) for j in range(4)] if 'win' in NSA_BR else []
            wst = {"i": 0, "prev": None}

            def win_step():
                if wst["i"] >= len(win_list):
                    return
                qt, r = win_list[wst["i"]]
                wst["i"] += 1
                cur = (qt, r) + win_scores(qt, r)
                if wst["prev"] is not None:
                    win_pv(*wst["prev"])
                wst["prev"] = cur

            cprev = None
            for r in range(4):
                cur = (0, r) + cmp_scores(0, r)
                if cprev is not None:
                    cmp_pv(*cprev)
                cprev = cur
            cmp_pv(*cprev)
            topk_dve(range(4))
            cprev = None
            for r in range(4):
                cur = (1, r) + cmp_scores(1, r)
                if cprev is not None:
                    cmp_pv(*cprev)
                cprev = cur
                for _ in range(4):
                    win_step()
            cmp_pv(*cprev)
            if g == 0:
                dump("imp", imp[:], b_imp)
            while wst["i"] < len(win_list):
                win_step()
            if wst["prev"] is not None:
                win_pv(*wst["prev"])
            topk_transposes(range(4))
            topk_dve(range(4, 8))

            def slc_scores(qb, r, kt):
                qs = slice(qb * 512, (qb + 1) * 512)
                m = kt - (8 + 4 * qb)
                pt, pb = ps_f(lo=4)
                S.op("pe", lambda e: e.matmul(pt[:], lhsT=ksT[:, kt * 128:(kt + 1) * 128], rhs=qrope[:, r, qs],
                                              start=True, stop=False), reads=[b_ksT, b_qrope[r]], writes=[pb])
                if 0 <= m <= 3:
                    S.op("pe", lambda e: e.matmul(pt[:, m * 128:(m + 1) * 128], lhsT=id16[:], rhs=wdiag[:], start=False, stop=False),
                         reads=[b_id16, b_wdiag], writes=[pb])
                S.op("pe", lambda e: e.matmul(pt[:], lhsT=ekt[:, kt, :], rhs=selbT[:, qs], start=False, stop=True),
                     reads=[b_ekt] + b_selbT[qb * 4:qb * 4 + 4], writes=[pb])
                P, bP = PT[pti[0] % 3]
                pti[0] += 1
                S.op("act", lambda e: e.activation(out=P[:], in_=pt[:], func=AF.Exp, scale=SCALE), reads=[pb], writes=[bP])
                return P, bP

            def slc_pv(qb, kt, acc, P, bP):
                for j in range(4):
                    last = 8 + 4 * qb + j
                    if kt > last:
                        continue
                    pa_, pab = acc[j // 2]
                    c0 = (j % 2) * 129
                    S.op("pe", lambda e, pa_=pa_, c0=c0, j=j, last=last: e.matmul(
                        pa_[:, c0:c0 + 129], lhsT=P[:, j * 128:(j + 1) * 128], rhs=vsaug[:, kt, :], start=False, stop=(kt == last and j % 2 == 1)),
                        reads=[bP, b_vs], writes=[pab])

            it_ = 0
            for qb in (range(2) if 'slc' in NSA_BR else ()):
                nkt = 8 + 4 * qb + 4
                if qb == 1:
                    topk_transposes(range(4, 8))
                for r in range(4):
                    acc = [ps_f(fixed=2 * (it_ % 2)), ps_f(fixed=2 * (it_ % 2) + 1)]
                    it_ += 1
                    for pa_, pab in acc:
                        S.op("pe", lambda e, pa_=pa_: e.matmul(pa_[:, 0:258], lhsT=zz[:, 0:128], rhs=zz[:, 0:258], start=True, stop=False),
                             reads=[b_zz], writes=[pab])
                    prev = None
                    for kt in range(nkt):
                        cur = (kt,) + slc_scores(qb, r, kt)
                        if prev is not None:
                            slc_pv(qb, prev[0], acc, prev[1], prev[2])
                        prev = cur
                    slc_pv(qb, prev[0], acc, prev[1], prev[2])
                    for j in range(4):
                        pa_, pab = acc[j // 2]
                        finalize(pa_, pab, (j % 2) * 129, qb * 4 + j, r, 1, False)
            if g == 0:
                dump("onsa", onsa[:], [b for rr_ in b_onsa for b in rr_])
            for qt in range(8):
                o, bo = o16[qt % 2]
                S.op("act", lambda e, o=o, qt=qt: e.copy(out=o[:], in_=onsa[:, qt, :, :]), reads=b_onsa[qt], writes=[bo])
                pT, bT = ps_b()
                for r in range(4):
                    S.op("pe", lambda e, pT=pT, o=o, r=r: e.transpose(pT[:, r * 128:(r + 1) * 128], o[:, r, :], id16[:]), reads=[bo, b_id16], writes=[bT])
                S.op("dve", lambda e, pT=pT, qt=qt: e.tensor_copy(out=mixT[:, 8 + 4 * g:12 + 4 * g, qt * 128:(qt + 1) * 128],
                                                                  in_=pT[:, 0:512].rearrange("p (a b) -> p a b", a=4)),
                     reads=[bT], writes=[b_mixT[8 + 4 * g + r][qt] for r in range(4)])
            S.barrier()
            S.emit()


def _kc_layout(w):
    C = w.shape[1]
    return np.ascontiguousarray(w.reshape(16, 128, C).transpose(1, 0, 2)).reshape(128, 16 * C)


def prep_shared(inp):
    w_in = inp["w_in"][0]
    sh = {}
    wg = []
    for h in range(4):
        cols = np.concatenate([w_in[:, O_GQ + h * 128:O_GQ + (h + 1) * 128], w_in[:, O_GK + h * 128:O_GK + (h + 1) * 128],
                               w_in[:, O_GV + h * 256:O_GV + (h + 1) * 256], w_in[:, O_GO + h * 256:O_GO + (h + 1) * 256]], axis=1)
        wg.append(_kc_layout(cols))
    sh["wg"] = np.stack(wg)
    sh["wglr"] = _kc_layout(w_in[:, O_GLR:O_GLR + 16])
    sh["w2aug"] = np.concatenate([inp["gla_gate_w2"][0], inp["gla_gate_b2"][0][None, :]], axis=0).astype(np.float32)
    sh["normw"] = np.ascontiguousarray(np.broadcast_to(inp["gla_norm_w"][0][None, :], (128, 256))).astype(np.float32)
    blocks = []
    for g in range(2):
        for r in range(4):
            hh = 4 * g + r
            blocks.append(w_in[:, O_NQ + hh * 128:O_NQ + (hh + 1) * 128])
        blocks.append(w_in[:, O_KS + g * 128:O_KS + (g + 1) * 128])
        blocks.append(w_in[:, O_KW + g * 128:O_KW + (g + 1) * 128])
        blocks.append(w_in[:, O_KC + g * 128:O_KC + (g + 1) * 128])
        blocks.append(w_in[:, O_VC + g * 128:O_VC + (g + 1) * 128])
    sh["wn"] = np.stack([_kc_layout(b) for b in blocks])
    sh["wv"] = np.stack([_kc_layout(np.concatenate([w_in[:, O_VS + g * 128:O_VS + (g + 1) * 128],
                                                     w_in[:, O_VW + g * 128:O_VW + (g + 1) * 128]], axis=1)) for g in range(2)])
    sh["wgate"] = _kc_layout(w_in[:, O_GATE:O_GATE + 24])
    w1c = []
    for nm in ("cmp_k_w1", "cmp_v_w1"):
        w1 = inp[nm][0]
        w1c.append(np.ascontiguousarray(w1.reshape(32, 128, 256).transpose(1, 0, 2)).reshape(128, 32 * 256))
    sh["w1c"] = np.stack(w1c)
    w2c = []
    for nm in ("cmp_k_w2", "cmp_v_w2"):
        w2 = inp[nm][0]
        w2c.append(np.ascontiguousarray(w2.reshape(2, 128, 128).transpose(1, 0, 2)).reshape(128, 256))
    sh["w2c"] = np.stack(w2c)
    sh["posc"] = np.stack([np.ascontiguousarray(inp["cmp_k_pos"][0].T), np.ascontiguousarray(inp["cmp_v_pos"][0].T)])
    w_out = inp["w_out"][0]
    sh["wout"] = np.stack([_kc_layout(w_out[:, cb * 512:(cb + 1) * 512]) for cb in range(4)])
    w1 = inp["ffn_w1"][0]
    w3 = inp["ffn_w3"][0]
    sh["w1t"] = np.ascontiguousarray(w1.reshape(16, 128, NFC, 128).transpose(2, 1, 0, 3)).reshape(NFC, 128, 16 * 128)
    sh["w3t"] = np.ascontiguousarray(w3.reshape(16, 128, NFC, 128).transpose(2, 1, 0, 3)).reshape(NFC, 128, 16 * 128)
    w2 = inp["ffn_w2"][0]
    sh["w2t"] = np.ascontiguousarray(w2.reshape(11, 4, 128, 4, 512).transpose(3, 0, 2, 1, 4)).reshape(4, 11, 128, 4 * 512)
    ln = np.stack([inp["ln1_g"][0], inp["ln1_b"][0], inp["ln2_g"][0], inp["ln2_b"][0]])
    sh["ln"] = np.ascontiguousarray(np.broadcast_to(ln[:, None, :], (4, 128, D))).astype(np.float32)
    sh["ident"] = np.eye(128, dtype=np.float32)
    j = np.arange(128)[:, None]
    i = np.arange(128)[None, :]
    same = (j // 64) == (i // 64)
    sh["umat"] = ((j <= i) & same).astype(np.float32)
    sh["lmat"] = ((j > i) & same).astype(np.float32)
    sh["cind"] = np.stack([(np.arange(128) < 64), (np.arange(128) >= 64)], axis=1).astype(np.float32)
    pm = np.zeros((32, 32), np.float32)
    for m in range(32):
        pm[(m + 16) % 32, m] = 1.0
    sh["pm"] = pm
    sh["wedge"] = np.where(j > i, 0.0, NEG).astype(np.float32)
    sh["wdiag"] = np.where(j <= i, 0.0, NEG).astype(np.float32)
    n = np.arange(128)[:, None]
    blk = np.arange(32)[None, :]
    ovl = ((16 * n < 64 * blk + 64) & (64 * blk < 16 * n + 32)).astype(np.float32)
    ovl[127] = 0.0
    sh["ovl"] = ovl
    ekt = np.zeros((128, 16, 128), np.float32)
    for kt in range(16):
        for jj in range(128):
            ekt[2 * kt + jj // 64, kt, jj] = 1.0
    sh["ekt"] = ekt.reshape(128, 16 * 128)
    return sh


def prep_core(inp, b, half):
    x = inp["x"]
    own = x[b, half * NO:(half + 1) * NO]
    ctx = x[b, 0:NO] if half == 1 else np.zeros((NO, D), np.float32)
    xbuf = np.concatenate([ctx, own], axis=0)
    pc = {}
    pc["xT"] = np.ascontiguousarray(xbuf.T.reshape(16, 128, NB).transpose(1, 0, 2))
    pc["xo"] = np.ascontiguousarray(own.reshape(8, 128, D))
    off = 0 if half == 1 else -NO
    pos = (np.arange(NB) + off).astype(np.float32)
    inv = np.power(np.float32(500000.0), -np.arange(0, 32, 2, dtype=np.float32) / np.float32(32))
    ang = pos[None, :] * inv[:, None]
    c, s = np.cos(ang), np.sin(ang)
    pc["cosT"] = np.concatenate([c, c], axis=0).astype(np.float32)
    pc["sinT"] = np.concatenate([-s, s], axis=0).astype(np.float32)
    pc["wctx"] = np.full((128, 128), NEG if half == 0 else 0.0, np.float32)
    n = np.arange(128)[:, None]
    q = np.arange(NO)[None, :]
    t_true = half * NO + q
    n_true = n + (0 if half == 1 else -64)
    valid = (n_true >= 0) & (n < 127) & (16 * n_true + 31 <= t_true)
    pc["cmpb"] = np.where(valid, 0.0, NEG).astype(np.float32)
    pc["cmpb"][127, :] = -780.0
    qq = np.arange(NO)
    t_true = half * NO + qq
    cur = t_true // 64
    jb = np.arange(32)[None, :]
    jt = jb + (0 if half == 1 else -16)
    val = (jt >= 0) & (jt <= cur[:, None])
    forced = val & ((jt == 0) | (jt == cur[:, None]) | (jt == cur[:, None] - 1))
    A = (val & ~forced).astype(np.float32)
    Bt = forced.astype(np.float32) * 1e4 + (1.0 - val.astype(np.float32)) * (-1e4)
    def tl(a):
        return np.ascontiguousarray(a.reshape(8, 128, 32).transpose(1, 0, 2)).reshape(128, 8 * 32).astype(np.float32)
    pc["tka"], pc["tkb"], pc["tkv"] = tl(A), tl(Bt), tl(val.astype(np.float32))
    return pc


_NC_CACHE = {}


def kernel(**inputs):
    inp = {k: np.asarray(v) for k, v in inputs.items()}
    sh = prep_shared(inp)
    in_maps = []
    for c in range(8):
        m = dict(sh)
        m.update(prep_core(inp, c // 2, c % 2))
        in_maps.append(m)
    if "nc" not in _NC_CACHE:
        _NC_CACHE["nc"] = build_nc()
    nc = _NC_CACHE["nc"]
    res = run_bass_kernel_spmd(nc, in_maps, core_ids=list(range(8)))
    out = np.zeros((4, SEQ, D), np.float32)
    for c in range(8):
        b, half = c // 2, c % 2
        out[b, half * NO:(half + 1) * NO] = res.results[c]["out"].reshape(NO, D)
    return out
```

```python
import math
from contextlib import ExitStack

import numpy as np
import concourse.bass as bass
import concourse.mybir as mybir
from concourse.bass_utils import run_bass_kernel_spmd

F32 = mybir.dt.float32
BF16 = mybir.dt.bfloat16
AF = mybir.ActivationFunctionType
ALU = mybir.AluOpType
AX = mybir.AxisListType

D = 2048
SEQ = 2048
NB = 2048
NO = 1024
FF = 5632
NFC = FF // 128
ALPHA = 2.0 ** 0.25
EPS = 1e-5
NEG = -30000.0
EXTRA_DBG = []
SKIP = set()
GLA_H = 4
GLA_T = 16
NSA_BR = {'slc', 'win'}
SCALE = 128.0 ** -0.5

O_GQ, O_GK, O_GV, O_GO, O_GLR = 0, 512, 1024, 2048, 3072
O_NQ = 3088
O_KC, O_VC, O_KS, O_VS, O_KW, O_VW = 4112, 4368, 4624, 4880, 5136, 5392
O_GATE = 5648


class Buf:
    __slots__ = ("name", "w", "r", "excl")

    def __init__(self, name="", excl=False):
        self.name = name
        self.w = None
        self.r = {}
        self.excl = excl


class Sched:
    ENGS = ("pe", "act", "dve", "pool", "sp")
    NDS = 10

    def __init__(self, nc, es):
        self.nc = nc
        self.q = {e: [] for e in self.ENGS}
        self.cnt = {e: 0 for e in self.ENGS}
        self.waited = {e: {} for e in self.ENGS}
        self.dma_state = {qn: {"rr": 0, "n": [0] * self.NDS} for qn in ("sp", "pool", "act")}
        self.sems = {}
        for e in self.ENGS:
            self.sems["e_" + e] = es.enter_context(nc.semaphore("e_" + e))
        for qn in ("sp", "pool"):
            for k in range(self.NDS):
                sk = "d_%s_%d" % (qn, k)
                self.sems[sk] = es.enter_context(nc.semaphore(sk))

    def _collect(self, eng, reads, writes):
        deps = {}

        def add(tok, kind):
            if tok is None:
                return
            sk, val, e2 = tok
            if e2 == eng and eng == "pe":
                return
            if deps.get(sk, 0) < val:
                deps[sk] = val

        for b in reads:
            add(b.w, "raw")
            if b.excl:
                for sk, (val, e2) in b.r.items():
                    if e2 != eng:
                        add((sk, val, e2), "rar")
        for b in writes:
            add(b.w, "waw")
            for sk, (val, e2) in b.r.items():
                add((sk, val, e2), "war")
        waits = []
        wd = self.waited[eng]
        for sk, val in deps.items():
            if wd.get(sk, 0) >= val:
                continue
            wd[sk] = val
            waits.append((sk, val))
        return waits

    def _commit(self, tok, reads, writes):
        sk, val, e = tok
        for b in reads:
            old = b.r.get(sk)
            if old is None or old[0] < val:
                b.r[sk] = (val, e)
        for b in writes:
            b.w = tok
            b.r = {}

    LIMIT = None
    total = 0
    lines = []

    def _skip(self):
        import sys
        Sched.total += 1
        f = sys._getframe(2)
        Sched.lines.append(f.f_lineno)
        return Sched.LIMIT is not None and Sched.total > Sched.LIMIT

    def op(self, eng, fn, reads=(), writes=()):
        if self._skip():
            return None
        waits = self._collect(eng, reads, writes)
        self.cnt[eng] += 1
        tok = ("e_" + eng, self.cnt[eng], eng)
        self.q[eng].append((waits, fn, ("e_" + eng, 1)))
        self._commit(tok, reads, writes)
        return tok

    def dma(self, qn, fn, reads=(), writes=()):
        if self._skip():
            return None
        st = self.dma_state[qn]
        k = st["rr"]
        st["rr"] = (k + 1) % self.NDS
        sk = "d_%s_%d" % (qn, k)
        waits = self._collect(qn, reads, writes)
        prev = st["n"][k] * 16
        wd = self.waited[qn]
        if prev > 0 and wd.get(sk, 0) < prev:
            wd[sk] = prev
            waits.append((sk, prev))
        st["n"][k] += 1
        tok = (sk, st["n"][k] * 16, "dma_" + qn)
        self.q[qn].append((waits, fn, (sk, 16)))
        self._commit(tok, reads, writes)
        return tok

    def barrier(self):
        toks = []
        for e in self.ENGS:
            if self.cnt[e] > 0:
                toks.append(("e_" + e, self.cnt[e]))
        for qn, st in self.dma_state.items():
            for k, n in enumerate(st["n"]):
                if n > 0:
                    toks.append(("d_%s_%d" % (qn, k), n * 16))
        for e in self.ENGS:
            waits = []
            for sk, val in toks:
                if sk == "e_" + e:
                    continue
                if self.waited[e].get(sk, 0) < val:
                    self.waited[e][sk] = val
                    waits.append((sk, val))
            self.q[e].append((waits, None, None))

    def emit(self):
        nc = self.nc
        sems = self.sems
        with nc.Block() as block:
            def run(engname):
                def body(engine):
                    for waits, fn, inc in self.q[engname]:
                        for sk, val in waits:
                            engine.wait_ge(sems[sk], val)
                        if fn is not None:
                            fn(engine).then_inc(sems[inc[0]], inc[1])
                return body

            block.tensor(run("pe"))
            block.scalar(run("act"))
            block.vector(run("dve"))
            block.gpsimd(run("pool"))
            block.sync(run("sp"))
        self.q = {e: [] for e in self.ENGS}


def build_nc(dbg=()):
    nc = bass.Bass("TRN2", target_bir_lowering=False)

    def din(name, shape, dt=F32):
        return nc.dram_tensor(name, list(shape), dt, kind="ExternalInput").ap()

    xT_d = din("xT", [128, 16, NB])
    xo_d = din("xo", [8, 128, D])
    wg_d = din("wg", [4, 128, 16 * 768])
    wglr_d = din("wglr", [128, 16 * 16])
    w2aug_d = din("w2aug", [17, 512])
    normw_d = din("normw", [128, 256])
    wn_d = din("wn", [16, 128, 16 * 128])
    wv_d = din("wv", [2, 128, 16 * 256])
    wgate_d = din("wgate", [128, 16 * 24])
    w1c_d = din("w1c", [2, 128, 32 * 256])
    w2c_d = din("w2c", [2, 128, 2 * 128])
    posc_d = din("posc", [2, 128, 32])
    wout_d = din("wout", [4, 128, 16 * 512])
    w1t_d = din("w1t", [NFC, 128, 16 * 128])
    w3t_d = din("w3t", [NFC, 128, 16 * 128])
    w2t_d = din("w2t", [4, 11, 128, 4 * 512])
    ln_d = din("ln", [4, 128, D])
    ident_d = din("ident", [128, 128])
    umat_d = din("umat", [128, 128])
    lmat_d = din("lmat", [128, 128])
    cind_d = din("cind", [128, 2])
    pm_d = din("pm", [32, 32])
    cos_d = din("cosT", [32, NB])
    sin_d = din("sinT", [32, NB])
    wedge_d = din("wedge", [128, 128])
    wdiag_d = din("wdiag", [128, 128])
    wctx_d = din("wctx", [128, 128])
    cmpb_d = din("cmpb", [128, NO])
    ovl_d = din("ovl", [128, 32])
    tka_d = din("tka", [128, 8 * 32])
    tkb_d = din("tkb", [128, 8 * 32])
    tkv_d = din("tkv", [128, 8 * 32])
    ekt_d = din("ekt", [128, 16 * 128])
    out_d = nc.dram_tensor("out", [8, 128, D], F32, kind="ExternalOutput").ap()
    dbg_d = {}
    for name, shape in dbg:
        dbg_d[name] = nc.dram_tensor("dbg_" + name, list(shape), F32, kind="ExternalOutput").ap()

    with ExitStack() as es:
        S = Sched(nc, es)

        uid = [0]

        def sb(stack, name, shape, dt):
            uid[0] += 1
            return stack.enter_context(nc.sbuf_tensor("%s_u%d" % (name, uid[0]), list(shape), dt))

        psf = [(es.enter_context(nc.psum_tensor("psf%d" % i, [128, 512], F32)), Buf("psf%d" % i, True)) for i in range(6)]
        psb = [(es.enter_context(nc.psum_tensor("psb%d" % i, [128, 1024], BF16)), Buf("psb%d" % i, True)) for i in range(2)]
        rr = {"f": 0, "b": 0}

        def ps_f(fixed=None, lo=0):
            if fixed is not None:
                return psf[fixed]
            r = psf[lo + rr["f"] % (6 - lo)]
            rr["f"] += 1
            return r

        def ps_b():
            r = psb[rr["b"] % 2]
            rr["b"] += 1
            return r

        def const(name, src, shape, dt):
            t = sb(es, "c_" + name, shape, dt)
            b = Buf(name)
            q = "pool" if dt == BF16 else "sp"
            S.dma(q, lambda e: e.dma_start(out=t[:], in_=src), writes=[b])
            return t, b

        id16, b_id16 = const("id16", ident_d, [128, 128], BF16)
        id32, b_id32 = const("id32", ident_d, [128, 128], F32)
        umat, b_umat = const("umat", umat_d, [128, 128], F32)
        lmat, b_lmat = const("lmat", lmat_d, [128, 128], F32)
        cind, b_cind = const("cind", cind_d, [128, 2], F32)
        normw, b_normw = const("normw", normw_d, [128, 256], F32)
        mixT = sb(es, "mixT", [128, 16, NO], BF16)
        b_mixT = [[Buf("mixT%d_%d" % (c, t)) for t in range(8)] for c in range(16)]

        def dump(name, ap, bufs):
            if name in dbg_d:
                S.dma("sp", lambda e: e.dma_start(out=dbg_d[name], in_=ap), reads=bufs)

        with ExitStack() as pm:
            xT16 = sb(pm, "xT16", [128, 16, NB], BF16)
            b_xT = [[Buf("xT%d_%d" % (k, tb)) for tb in range(4)] for k in range(16)]

            def load_xT(tb):
                for kc in range(16):
                    S.dma("pool", lambda e, kc=kc: e.dma_start(out=xT16[:, kc, tb * 512:(tb + 1) * 512], in_=xT_d[:, kc, tb * 512:(tb + 1) * 512]),
                          writes=[b_xT[kc][tb]])

            load_xT(0)
            nsa_dd = dict(wn=wn_d, wv=wv_d, wgate=wgate_d, w1c=w1c_d, w2c=w2c_d, posc=posc_d, pm=pm_d, cos=cos_d, sin=sin_d,
                          wedge=wedge_d, wdiag=wdiag_d, wctx=wctx_d, cmpb=cmpb_d, ovl=ovl_d, tka=tka_d, tkb=tkb_d, tkv=tkv_d, ekt=ekt_d)
            nsa_deferred = []
            nsa_consts = NSA_consts(nc, S, pm, sb, nsa_dd, nsa_deferred)
            with ExitStack() as pg:
                wg16 = [sb(pg, "wg16_%d" % i, [128, 16, 768], BF16) for i in range(2)]
                b_wg = [Buf("wg0"), Buf("wg1")]
                wglr16 = sb(pg, "wglr16", [128, 16, 16], BF16)
                b_wglr = Buf("wglr")
                w2aug = sb(pg, "w2aug_s", [17, 512], F32)
                b_w2aug = Buf("w2aug")
                glrT = sb(pg, "glrT", [17, NB], F32)
                b_glrT4 = [Buf("glrT%d" % tb) for tb in range(4)]
                S.dma("pool", lambda e: e.dma_start(out=wglr16[:].rearrange("p a b -> p (a b)"), in_=wglr_d), writes=[b_wglr])
                S.dma("sp", lambda e: e.dma_start(out=w2aug[:], in_=w2aug_d), writes=[b_w2aug])

                def load_wg(h):
                    S.dma("pool", lambda e: e.dma_start(out=wg16[h % 2][:].rearrange("p a b -> p (a b)"), in_=wg_d[h]),
                          writes=[b_wg[h % 2]])

                load_wg(0)
                for tb in range(1, 4):
                    load_xT(tb)
                load_wg(1)
                for th_ in nsa_deferred:
                    th_()
                S.op("dve", lambda e: e.memset(glrT[:], 1.0), writes=b_glrT4)

                def emit_glr(tb, bank=None):
                    pt, pb = ps_f() if bank is None else psf[bank]
                    for kc in range(16):
                        S.op("pe", lambda e, kc=kc: e.matmul(
                            pt[0:16, :], lhsT=wglr16[:, kc, :], rhs=xT16[:, kc, tb * 512:(tb + 1) * 512],
                            start=(kc == 0), stop=(kc == 15)), reads=[b_wglr, b_xT[kc][tb]], writes=[pb])
                    S.op("act", lambda e: e.copy(out=glrT[0:16, tb * 512:(tb + 1) * 512], in_=pt[0:16, :]),
                         reads=[pb], writes=[b_glrT4[tb]])

                emit_glr(0)

                NBUF = 2
                def mk(name, shape, dt):
                    return [(sb(pg, "%s%d" % (name, i), shape, dt), Buf("%s%d" % (name, i))) for i in range(NBUF)]
                sp_b = mk("sp", [128, 128], F32)
                ez_b = mk("ez", [128, 128], F32)
                e1_b = mk("e1", [128, 128], F32)
                e2_b = mk("e2", [128, 128], F32)
                e3_b = mk("e3", [128, 128], F32)
                dec_b = mk("dec", [128, 2], F32)
                qd_b = mk("qd", [128, 128], BF16)
                ki_b = mk("ki", [128, 128], BF16)
                ks_b = mk("kst", [128, 128], BF16)
                v16_b = mk("v16", [128, 256], BF16)
                qdp_b = mk("qdp", [128, 2, 128], BF16)
                kiT_b = mk("kiT", [128, 128], BF16)
                at_b = mk("at", [128, 128], BF16)
                ssq_b = mk("ssq", [128, 2], F32)
                junk_b = mk("junk", [128, 256], F32)
                y_b = mk("y", [128, 256], F32)
                sg_b = mk("sg", [128, 256], F32)
                gO_b = mk("gO", [128, 256], F32)
                yg_b = mk("yg", [128, 256], BF16)
                st32 = sb(pg, "st32", [128, 256], F32)
                b_st32 = Buf("st32")
                st16 = mk("st16", [128, 256], BF16)
                for i in range(NBUF):
                    S.op("dve", lambda e, i=i: e.memset(qdp_b[i][0][:], 0.0), writes=[qdp_b[i][1]])

                free_banks = list(range(6))

                def palloc():
                    assert free_banks, "GLA: out of PSUM banks"
                    return free_banks.pop(0)

                def pfree(i):
                    free_banks.append(i)

                tile_ctr = [0]

                def make_proj(h, t):
                    own = t >= 8
                    wt, wb = wg16[h % 2], b_wg[h % 2]
                    i2 = tile_ctr[0] % NBUF
                    tile_ctr[0] += 1
                    ia, ib = palloc(), palloc()
                    pA, bA = psf[ia]
                    pB, bB = psf[ib]
                    tok = slice(t * 128, (t + 1) * 128)
                    c0 = 0 if own else 128
                    thunks = []
                    thunks.append(lambda: S.op("pe", lambda e: e.matmul(
                        pB[:, 256:384], lhsT=glrT[0:17, tok], rhs=w2aug[0:17, h * 128:(h + 1) * 128], start=True, stop=True),
                        reads=[b_glrT4[t // 4], b_w2aug], writes=[bB]))
                    for kc in range(16):
                        thunks.append(lambda kc=kc: S.op("pe", lambda e: e.matmul(
                            pA[:, c0:512], lhsT=xT16[:, kc, tok], rhs=wt[:, kc, c0:512], start=(kc == 0), stop=(kc == 15)),
                            reads=[b_xT[kc][t // 4], wb], writes=[bA]))
                    if own:
                        for kc in range(16):
                            thunks.append(lambda kc=kc: S.op("pe", lambda e: e.matmul(
                                pB[:, 0:256], lhsT=xT16[:, kc, tok], rhs=wt[:, kc, 512:768], start=(kc == 0), stop=(kc == 15)),
                                reads=[b_xT[kc][t // 4], wb], writes=[bB]))
                    ez, bez = ez_b[i2]
                    spt, bsp = sp_b[i2]
                    sg, bsg = sg_b[i2]

                    done = {"sp": False}

                    def post_sp():
                        if done["sp"]:
                            return
                        done["sp"] = True
                        S.op("act", lambda e: e.activation(out=ez[:], in_=pB[:, 256:384], func=AF.Exp, scale=-1.0), reads=[bB], writes=[bez])
                        S.op("act", lambda e: e.activation(out=spt[:], in_=ez[:], func=AF.Ln, bias=1.0), reads=[bez], writes=[bsp])

                    def post():
                        post_sp()
                        if own:
                            gO, bgO = gO_b[i2]
                            S.op("act", lambda e: e.activation(out=sg[:], in_=pB[:, 0:256], func=AF.Exp, scale=-1.0), reads=[bB], writes=[bsg])
                            S.op("act", lambda e: e.copy(out=gO[:], in_=pB[:, 0:256]), reads=[bB], writes=[bgO])
                            S.op("dve", lambda e: e.tensor_scalar(out=sg[:], in0=sg[:], scalar1=1.0, scalar2=None, op0=ALU.add), reads=[bsg], writes=[bsg])
                            S.op("dve", lambda e: e.reciprocal(out=sg[:], in_=sg[:]), reads=[bsg], writes=[bsg])
                            S.op("dve", lambda e: e.tensor_tensor(out=sg[:], in0=sg[:], in1=gO[:], op=ALU.mult), reads=[bsg, bgO], writes=[bsg])
                            S.op("dve", lambda e: e.tensor_tensor(out=sg[:], in0=sg[:], in1=normw[:], op=ALU.mult), reads=[bsg, b_normw], writes=[bsg])
                        pfree(ib)

                    return dict(h=h, t=t, own=own, i2=i2, ia=ia, pA=pA, bA=bA, thunks=thunks, post=post, post_sp=post_sp,
                                spt=spt, bsp=bsp, sg=sg, bsg=bsg)

                def fill(nxt, n):
                    if nxt is None:
                        return
                    for _ in range(n):
                        if nxt["thunks"]:
                            nxt["thunks"].pop(0)()

                def run_tile(cur, nxt, state):
                    h, t, own, i2 = cur["h"], cur["t"], cur["own"], cur["i2"]
                    pA, bA, spt, bsp = cur["pA"], cur["bA"], cur["spt"], cur["bsp"]
                    nfill = (len(nxt["thunks"]) + 3) // 4 if nxt is not None else 0
                    iu = palloc()
                    pU, bU = psf[iu]
                    if own:
                        S.op("pe", lambda e: e.matmul(pU[:, 0:128], lhsT=umat[:], rhs=spt[:], start=True, stop=True),
                             reads=[b_umat, bsp], writes=[bU])
                    S.op("pe", lambda e: e.matmul(pU[:, 128:256], lhsT=lmat[:], rhs=spt[:], start=True, stop=True),
                         reads=[b_lmat, bsp], writes=[bU])
                    S.op("pe", lambda e: e.matmul(pU[:, 256:258], lhsT=spt[:], rhs=cind[:], start=True, stop=True),
                         reads=[b_cind, bsp], writes=[bU])
                    e3, be3 = e3_b[i2]
                    dec, bdec = dec_b[i2]
                    if own:
                        e1, be1 = e1_b[i2]
                        e2, be2 = e2_b[i2]
                        S.op("act", lambda e: e.activation(out=e1[:], in_=pU[:, 0:128], func=AF.Exp, scale=-1.0 / 16), reads=[bU], writes=[be1])
                        S.op("act", lambda e: e.activation(out=e2[:], in_=pU[:, 0:128], func=AF.Exp, scale=1.0 / 16), reads=[bU], writes=[be2])
                        qd, bqd = qd_b[i2]
                        ki, bki = ki_b[i2]
                        S.op("dve", lambda e: e.scalar_tensor_tensor(out=qd[:], in0=pA[:, 0:128], scalar=SCALE, in1=e1[:], op0=ALU.mult, op1=ALU.mult),
                             reads=[bA, be1], writes=[bqd])
                        S.op("dve", lambda e: e.tensor_tensor(out=ki[:], in0=pA[:, 128:256], in1=e2[:], op=ALU.mult), reads=[bA, be2], writes=[bki])
                    S.op("act", lambda e: e.activation(out=e3[:], in_=pU[:, 128:256], func=AF.Exp, scale=-1.0 / 16), reads=[bU], writes=[be3])
                    S.op("act", lambda e: e.activation(out=dec[:], in_=pU[:, 256:258], func=AF.Exp, scale=-1.0 / 16), reads=[bU], writes=[bdec])
                    pfree(iu)
                    kst, bks = ks_b[i2]
                    v16, bv16 = v16_b[i2]
                    S.op("dve", lambda e: e.tensor_tensor(out=kst[:], in0=pA[:, 128:256], in1=e3[:], op=ALU.mult), reads=[bA, be3], writes=[bks])
                    S.op("dve", lambda e: e.tensor_copy(out=v16[:], in_=pA[:, 256:512]), reads=[bA], writes=[bv16])
                    pfree(cur["ia"])
                    fill(nxt, nfill + 6 if own else nfill)
                    if nxt is not None:
                        nxt["post_sp"]()
                    if state.get("deferred") is not None:
                        state["deferred"]()
                        state["deferred"] = None
                    if own:
                        pT, bT = ps_b()
                        S.op("pe", lambda e: e.transpose(pT[:, 0:128], qd[:], id16[:]), reads=[bqd, b_id16], writes=[bT])
                        S.op("pe", lambda e: e.transpose(pT[:, 128:256], ki[:], id16[:]), reads=[bki, b_id16], writes=[bT])
                        qdp, bqdp = qdp_b[i2]
                        kiT, bkiT = kiT_b[i2]
                        S.op("dve", lambda e: e.tensor_copy(out=qdp[:, 0, 0:64], in_=pT[:, 0:64]), reads=[bT], writes=[bqdp])
                        S.op("dve", lambda e: e.tensor_copy(out=qdp[:, 1, 64:128], in_=pT[:, 64:128]), reads=[bT], writes=[bqdp])
                        S.op("dve", lambda e: e.tensor_copy(out=kiT[:], in_=pT[:, 128:256]), reads=[bT], writes=[bkiT])
                        fill(nxt, nfill)
                        iat = palloc()
                        pAt, bAt = psf[iat]
                        for c in range(2):
                            S.op("pe", lambda e, c=c: e.matmul(pAt[:, c * 64:(c + 1) * 64], lhsT=kiT[:], rhs=qdp[:, c, c * 64:(c + 1) * 64],
                                                               start=True, stop=True), reads=[bkiT, bqdp], writes=[bAt])
                        at, bat = at_b[i2]
                        S.op("dve", lambda e: e.tensor_tensor(out=at[:], in0=pAt[:, 0:128], in1=umat[:], op=ALU.mult), reads=[bAt, b_umat], writes=[bat])
                        pfree(iat)
                        fill(nxt, nfill)
                        io = palloc()
                        pO, bO = psf[io]
                        S.op("pe", lambda e: e.matmul(pO[:, 0:256], lhsT=at[:], rhs=v16[:], start=True, stop=False), reads=[bat, bv16], writes=[bO])
                    for c in range(2):
                        if own:
                            s16, bs16 = st16[state["v"] % 2]
                            S.op("pe", lambda e, c=c, s16=s16: e.matmul(pO[:, 0:256], lhsT=qdp[:, c, :], rhs=s16[:], start=False, stop=(c == 1)),
                                 reads=[bqdp, bs16], writes=[bO])
                        if t == 15 and c == 1:
                            break
                        isb = palloc()
                        pS, bS = psf[isb]
                        S.op("pe", lambda e, c=c, pS=pS: e.matmul(pS[:, 0:256], lhsT=kst[c * 64:(c + 1) * 64, :], rhs=v16[c * 64:(c + 1) * 64, :],
                                                                  start=True, stop=True), reads=[bks, bv16], writes=[bS])
                        S.op("dve", lambda e, c=c, pS=pS: e.scalar_tensor_tensor(out=st32[:], in0=st32[:], scalar=dec[:, c:c + 1], in1=pS[:, 0:256],
                                                                               op0=ALU.mult, op1=ALU.add), reads=[b_st32, bdec, bS], writes=[b_st32])
                        pfree(isb)
                        state["v"] += 1
                        s16n, bs16n = st16[state["v"] % 2]
                        S.op("act", lambda e, s16n=s16n: e.copy(out=s16n[:], in_=st32[:]), reads=[b_st32], writes=[bs16n])
                        if c == 0:
                            fill(nxt, nfill if not own else nfill)
                    fill(nxt, 1000)
                    if nxt is not None:
                        nxt["post"]()
                    if own:
                        ssq, bssq = ssq_b[i2]
                        junk, bjunk = junk_b[i2]
                        S.op("dve", lambda e: e.memset(ssq[:], 0.0), writes=[bssq])
                        S.op("act", lambda e: e.activation(out=junk[:], in_=pO[:, 0:256], func=AF.Square, accum_out=ssq[:, 0:1]),
                             reads=[bO, bssq], writes=[bjunk, bssq])
                        S.op("act", lambda e: e.activation(out=ssq[:, 1:2], in_=ssq[:, 0:1], func=AF.Ln, scale=1.0 / 256, bias=EPS), reads=[bssq], writes=[bssq])
                        S.op("act", lambda e: e.activation(out=ssq[:, 1:2], in_=ssq[:, 1:2], func=AF.Exp, scale=-0.5), reads=[bssq], writes=[bssq])
                        y, by = y_b[i2]
                        S.op("act", lambda e: e.activation(out=y[:], in_=pO[:, 0:256], func=AF.Copy, scale=ssq[:, 1:2]), reads=[bO, bssq], writes=[by])
                        pfree(io)
                        sg, bsg = cur["sg"], cur["bsg"]
                        yg, byg = yg_b[i2]
                        S.op("dve", lambda e: e.tensor_tensor(out=yg[:], in0=y[:], in1=sg[:], op=ALU.mult), reads=[by, bsg], writes=[byg])
                        def s6_pe():
                            pT2, bT2 = ps_b()
                            for ec in range(2):
                                S.op("pe", lambda e, ec=ec: e.transpose(pT2[:, ec * 128:(ec + 1) * 128], yg[:, ec * 128:(ec + 1) * 128], id16[:]),
                                     reads=[byg, b_id16], writes=[bT2])
                            qt = t - 8
                            S.op("act", lambda e: e.copy(out=mixT[:, 2 * h:2 * h + 2, qt * 128:(qt + 1) * 128],
                                                         in_=pT2[:, 0:256].rearrange("p (a b) -> p a b", a=2)),
                                 reads=[bT2], writes=[b_mixT[2 * h][qt], b_mixT[2 * h + 1][qt]])
                        state["deferred"] = s6_pe

                tiles = [(h, t) for h in range(GLA_H) for t in range(GLA_T)]
                cur = make_proj(*tiles[0])
                fill(cur, 1000)
                cur["post"]()
                state = {"v": 0}
                for idx, (h, t) in enumerate(tiles):
                    if t == 0:
                        S.op("dve", lambda e: e.memset(st32[:], 0.0), writes=[b_st32])
                        S.op("dve", lambda e: e.memset(st16[0][0][:], 0.0), writes=[st16[0][1]])
                        state["v"] = 0
                    if h == 0 and t % 4 == 2 and t // 4 + 1 < 4:
                        ib_ = palloc()
                        emit_glr(t // 4 + 1, bank=ib_)
                        pfree(ib_)
                    nxt = make_proj(*tiles[idx + 1]) if idx + 1 < len(tiles) else None
                    run_tile(cur, nxt, state)
                    cur = nxt
                    if t == GLA_T - 1 and h + 2 < 4:
                        load_wg(h + 2)
                if state.get("deferred") is not None:
                    state["deferred"]()
                    state["deferred"] = None
                S.barrier()
                S.emit()

            with ExitStack() as pn:
                NSA_phase(nc, S, pn, sb, ps_f, ps_b, xT16, b_xT, mixT, b_mixT, id16, b_id16, id32, b_id32, dump,
                          nsa_dd, nsa_consts)
            if True:
                S.barrier()
                S.emit()

        if "mixT" in dbg_d:
            with ExitStack() as pd:
                m32 = sb(pd, "m32", [128, 16, NO], F32)
                bm = Buf("m32")
                S.op("dve", lambda e: e.tensor_copy(out=m32[:], in_=mixT[:]), reads=[b for r in b_mixT for b in r], writes=[bm])
                S.dma("sp", lambda e: e.dma_start(out=dbg_d["mixT"], in_=m32[:]), reads=[bm])
                S.barrier()
                S.emit()

        with ExitStack() as pf:
          if "ffn" not in SKIP:
            FFN_phase(nc, S, pf, sb, ps_f, ps_b, mixT, b_mixT, id16, b_id16, xo_d, wout_d, w1t_d, w3t_d, w2t_d, ln_d, out_d, dump)
            S.barrier()
            S.emit()
    return nc


def FFN_phase(nc, S, pf, sb, ps_f, ps_b, mixT, b_mixT, id16, b_id16, xo_d, wout_d, w1t_d, w3t_d, w2t_d, ln_d, out_d, dump):
    r1 = sb(pf, "r1", [128, 4, D], F32)
    b_r1 = [[Buf("r1_%d_%d" % (t, c)) for c in range(4)] for t in range(4)]
    hT = sb(pf, "hT", [128, 16, 512], BF16)
    b_hT = [[Buf("hT%d_%d" % (k, t)) for t in range(4)] for k in range(16)]
    gT = sb(pf, "gT", [128, NFC, 512], BF16)
    b_gT = [Buf("gT%d" % j) for j in range(NFC)]
    lng = sb(pf, "lng", [128, D], F32)
    lnb = sb(pf, "lnb", [128, D], F32)
    b_lng, b_lnb = Buf("lng"), Buf("lnb")
    NW = 3
    wo16 = [(sb(pf, "wo16_%d" % i, [128, 16, 256], BF16), Buf("wo%d" % i)) for i in range(2)]
    w1b = [(sb(pf, "w1b_%d" % i, [128, 16, 128], BF16), Buf("w1b%d" % i)) for i in range(NW)]
    w3b = [(sb(pf, "w3b_%d" % i, [128, 16, 128], BF16), Buf("w3b%d" % i)) for i in range(NW)]
    w2b = [(sb(pf, "w2b_%d" % i, [128, 4, 512], BF16), Buf("w2b%d" % i)) for i in range(NW)]
    st = sb(pf, "lnst", [128, 4, 10], F32)
    b_st = [Buf("lnst%d" % t) for t in range(4)]
    sa = [(sb(pf, "sa%d" % i, [128, 512], F32), Buf("sa%d" % i)) for i in range(2)]

    def gview(c0):
        return gT[:, c0:c0 + 4, :].rearrange("p a b -> p (a b)"), b_gT[c0:c0 + 4]

    def layer_norm_all(after_tile=None):
        for tt in range(4):
            rb, src, s, bs = b_r1[tt], r1[:, tt, :], st[:, tt, :], b_st[tt]
            junk, bj = gview(4 * tt)
            S.op("dve", lambda e, s=s: e.memset(s[:, 0:2], 0.0), writes=[bs])
            S.op("act", lambda e, s=s, src=src, junk=junk: e.activation(out=junk, in_=src, func=AF.Identity, accum_out=s[:, 0:1]),
                 reads=rb + [bs], writes=bj + [bs])
            S.op("act", lambda e, s=s, src=src, junk=junk: e.activation(out=junk, in_=src, func=AF.Square, accum_out=s[:, 1:2]),
                 reads=rb + [bs], writes=bj + [bs])
        for tt in range(4):
            s, bs = st[:, tt, :], b_st[tt]
            S.op("dve", lambda e, s=s: e.tensor_scalar(out=s[:, 2:4], in0=s[:, 0:2], scalar1=1.0 / D, scalar2=None, op0=ALU.mult), reads=[bs], writes=[bs])
            S.op("dve", lambda e, s=s: e.tensor_tensor(out=s[:, 4:5], in0=s[:, 2:3], in1=s[:, 2:3], op=ALU.mult), reads=[bs], writes=[bs])
            S.op("dve", lambda e, s=s: e.tensor_tensor(out=s[:, 5:6], in0=s[:, 3:4], in1=s[:, 4:5], op=ALU.subtract), reads=[bs], writes=[bs])
            S.op("dve", lambda e, s=s: e.tensor_scalar(out=s[:, 5:6], in0=s[:, 5:6], scalar1=EPS, scalar2=None, op0=ALU.add), reads=[bs], writes=[bs])
        for tt in range(4):
            s, bs = st[:, tt, :], b_st[tt]
            S.op("act", lambda e, s=s: e.activation(out=s[:, 6:7], in_=s[:, 5:6], func=AF.Sqrt), reads=[bs], writes=[bs])
        for tt in range(4):
            s, bs = st[:, tt, :], b_st[tt]
            S.op("dve", lambda e, s=s: e.reciprocal(out=s[:, 7:8], in_=s[:, 6:7]), reads=[bs], writes=[bs])
            S.op("dve", lambda e, s=s: e.tensor_scalar(out=s[:, 8:9], in0=s[:, 2:3], scalar1=s[:, 7:8], scalar2=-1.0, op0=ALU.mult, op1=ALU.mult),
                 reads=[bs], writes=[bs])
        for tt in range(4):
            rb, src, s, bs = b_r1[tt], r1[:, tt, :], st[:, tt, :], b_st[tt]
            S.op("act", lambda e, s=s, src=src: e.activation(out=src, in_=src, func=AF.Identity, scale=s[:, 7:8], bias=s[:, 8:9]),
                 reads=rb + [bs], writes=rb)
        for tt in range(4):
            rb, src = b_r1[tt], r1[:, tt, :]
            S.op("dve", lambda e, src=src: e.tensor_tensor(out=src, in0=src, in1=lng[:], op=ALU.mult), reads=rb + [b_lng], writes=rb)
            S.op("dve", lambda e, src=src: e.tensor_tensor(out=src, in0=src, in1=lnb[:], op=ALU.add), reads=rb + [b_lnb], writes=rb)
            if after_tile is not None:
                after_tile(tt)

    for th in range(2):
        def load_wo(i, th=th):
            cb, hc = i // 2, i % 2
            wo, bwo = wo16[i % 2]
            src = wout_d[cb].rearrange("p (k c) -> p k c", k=16)[:, :, hc * 256:(hc + 1) * 256]
            S.dma("pool", lambda e: e.dma_start(out=wo[:], in_=src), writes=[bwo])
        load_wo(0)
        load_wo(1)
        for tt in range(4):
            S.dma("sp", lambda e, tt=tt, th=th: e.dma_start(out=r1[:, tt, :], in_=xo_d[th * 4 + tt]), writes=b_r1[tt])
        S.dma("sp", lambda e: e.dma_start(out=lng[:], in_=ln_d[0]), writes=[b_lng])
        S.dma("sp", lambda e: e.dma_start(out=lnb[:], in_=ln_d[1]), writes=[b_lnb])
        for i in range(8):
            cb, hc = i // 2, i % 2
            wo, bwo = wo16[i % 2]
            for tt in range(4):
                qt = th * 4 + tt
                pt, pb = ps_f()
                for kc in range(16):
                    S.op("pe", lambda e, kc=kc, pt=pt, qt=qt, wo=wo: e.matmul(
                        pt[:, 0:256], lhsT=mixT[:, kc, qt * 128:(qt + 1) * 128], rhs=wo[:, kc, :], start=(kc == 0), stop=(kc == 15)),
                        reads=[b_mixT[kc][qt], bwo], writes=[pb])
                dst = r1[:, tt, i * 256:(i + 1) * 256]
                S.op("dve", lambda e, dst=dst, pt=pt: e.scalar_tensor_tensor(out=dst, in0=dst, scalar=ALPHA, in1=pt[:, 0:256], op0=ALU.mult, op1=ALU.add),
                     reads=[pb, b_r1[tt][cb]], writes=[b_r1[tt][cb]])
            if i + 2 < 8:
                load_wo(i + 2)
        def ln1_tail(tt):
            h16, bh = gview(16 + 4 * (tt % 2))
            S.op("act", lambda e: e.copy(out=h16, in_=r1[:, tt, :]), reads=b_r1[tt], writes=bh)
            for k4 in range(4):
                pT, bT = ps_b()
                for k in range(4):
                    kc = k4 * 4 + k
                    S.op("pe", lambda e, k=k, kc=kc, pT=pT: e.transpose(pT[:, k * 128:(k + 1) * 128], h16[:, kc * 128:(kc + 1) * 128], id16[:]),
                         reads=bh + [b_id16], writes=[bT])
                eng = "act" if k4 % 2 == 0 else "dve"
                dstT = hT[:, k4 * 4:(k4 + 1) * 4, tt * 128:(tt + 1) * 128]
                srcT = pT[:, 0:512].rearrange("p (a b) -> p a b", a=4)
                wr = [b_hT[k4 * 4 + k][tt] for k in range(4)]
                if eng == "act":
                    S.op("act", lambda e, dstT=dstT, srcT=srcT: e.copy(out=dstT, in_=srcT), reads=[bT], writes=wr)
                else:
                    S.op("dve", lambda e, dstT=dstT, srcT=srcT: e.tensor_copy(out=dstT, in_=srcT), reads=[bT], writes=wr)
        layer_norm_all(after_tile=ln1_tail)
        if th == 0:
            dump("h", r1[:], [b for r in b_r1 for b in r])
        S.dma("sp", lambda e: e.dma_start(out=lng[:], in_=ln_d[2]), writes=[b_lng])
        S.dma("sp", lambda e: e.dma_start(out=lnb[:], in_=ln_d[3]), writes=[b_lnb])
        hT_all = [b for r in b_hT for b in r]

        def load_w13(j):
            w1, bw1 = w1b[j % NW]
            w3, bw3 = w3b[j % NW]
            S.dma("pool", lambda e: e.dma_start(out=w1[:].rearrange("p a b -> p (a b)"), in_=w1t_d[j]), writes=[bw1])
            S.dma("pool", lambda e: e.dma_start(out=w3[:].rearrange("p a b -> p (a b)"), in_=w3t_d[j]), writes=[bw3])

        for j in range(min(NW - 1, NFC)):
            load_w13(j)
        for j in range(NFC):
            if j + NW - 1 < NFC:
                load_w13(j + NW - 1)
            w1, bw1 = w1b[j % NW]
            w3, bw3 = w3b[j % NW]
            pa, pab = ps_f()
            pb_, pbb = ps_f()
            for kc in range(16):
                S.op("pe", lambda e, kc=kc, pa=pa, w1=w1: e.matmul(pa[:], lhsT=w1[:, kc, :], rhs=hT[:, kc, :], start=(kc == 0), stop=(kc == 15)),
                     reads=[bw1] + b_hT[kc], writes=[pab])
            for kc in range(16):
                S.op("pe", lambda e, kc=kc, pb_=pb_, w3=w3: e.matmul(pb_[:], lhsT=w3[:, kc, :], rhs=hT[:, kc, :], start=(kc == 0), stop=(kc == 15)),
                     reads=[bw3] + b_hT[kc], writes=[pbb])
            s, bs = sa[j % 2]
            S.op("act", lambda e, s=s, pa=pa: e.activation(out=s[:], in_=pa[:], func=AF.Silu), reads=[pab], writes=[bs])
            S.op("dve", lambda e, s=s, pb_=pb_, j=j: e.tensor_tensor(out=gT[:, j, :], in0=s[:], in1=pb_[:], op=ALU.mult),
                 reads=[bs, pbb], writes=[b_gT[j]])
        def load_w2(cb, fg):
            i = (cb * 11 + fg) % NW
            w2, bw2 = w2b[i]
            S.dma("pool", lambda e: e.dma_start(out=w2[:].rearrange("p a b -> p (a b)"), in_=w2t_d[cb, fg]), writes=[bw2])

        seq = [(cb, fg) for cb in range(4) for fg in range(11)]
        for i in range(NW - 1):
            load_w2(*seq[i])
        for idx, (cb, fg) in enumerate(seq):
            if idx + NW - 1 < len(seq):
                load_w2(*seq[idx + NW - 1])
            if fg == 0:
                acc = [ps_f() for _ in range(4)]
            w2, bw2 = w2b[idx % NW]
            for tt in range(4):
                pt, pb = acc[tt]
                for f in range(4):
                    fc = fg * 4 + f
                    S.op("pe", lambda e, pt=pt, fc=fc, tt=tt, w2=w2, f=f: e.matmul(
                        pt[:], lhsT=gT[:, fc, tt * 128:(tt + 1) * 128], rhs=w2[:, f, :], start=(fc == 0), stop=(fc == NFC - 1)),
                        reads=[b_gT[fc], bw2], writes=[pb])
            if fg == 10:
                for tt in range(4):
                    pt, pb = acc[tt]
                    dst = r1[:, tt, cb * 512:(cb + 1) * 512]
                    S.op("dve", lambda e, dst=dst, pt=pt: e.scalar_tensor_tensor(out=dst, in0=dst, scalar=ALPHA, in1=pt[:], op0=ALU.mult, op1=ALU.add),
                         reads=[pb, b_r1[tt][cb]], writes=[b_r1[tt][cb]])
        def ln2_tail(tt, th=th):
            S.dma("sp", lambda e: e.dma_start(out=out_d[th * 4 + tt], in_=r1[:, tt, :]), reads=b_r1[tt])
        layer_norm_all(after_tile=ln2_tail)


def NSA_consts(nc, S, pn, sb, dd, deferred):
    def cst(name, src, shape, dt):
        t = sb(pn, "n_" + name, shape, dt)
        b = Buf(name)
        q = "pool" if dt == BF16 else "sp"
        if len(shape) == 3:
            deferred.append(lambda: S.dma(q, lambda e: e.dma_start(out=t[:].rearrange("p a b -> p (a b)"), in_=src), writes=[b]))
        else:
            deferred.append(lambda: S.dma(q, lambda e: e.dma_start(out=t[:], in_=src), writes=[b]))
        return t, b

    pm32, b_pm = cst("pm", dd["pm"], [32, 32], F32)
    cosT, b_cos = cst("cos", dd["cos"], [32, NB], F32)
    sinT, b_sin = cst("sin", dd["sin"], [32, NB], F32)
    wedge, b_wedge = cst("wedge", dd["wedge"], [128, 128], BF16)
    wdiag, b_wdiag = cst("wdiag", dd["wdiag"], [128, 128], BF16)
    wctx, b_wctx = cst("wctx", dd["wctx"], [128, 128], BF16)
    cmpb, b_cmpb = cst("cmpb", dd["cmpb"], [128, NO], BF16)
    tka, b_tka = cst("tka", dd["tka"], [128, 8, 32], F32)
    tkb, b_tkb = cst("tkb", dd["tkb"], [128, 8, 32], F32)
    tkv, b_tkv = cst("tkv", dd["tkv"], [128, 8, 32], F32)
    ekt, b_ekt = cst("ekt", dd["ekt"], [128, 16, 128], BF16)
    wgate16, b_wgate = cst("wgate", dd["wgate"], [128, 16, 24], BF16)
    vcaug = [sb(pn, "vcaug%d" % g, [128, 161], BF16) for g in range(2)]
    b_vcaug = [Buf("vcaug0"), Buf("vcaug1")]
    kcmpT = [sb(pn, "kcmpT%d" % g, [128, 128], BF16) for g in range(2)]
    b_kcmpT = [Buf("kcmpT0"), Buf("kcmpT1")]
    sig = sb(pn, "sig", [128, 8, 24], F32)
    b_sig = Buf("sig")
    for g in range(2):
        S.op("dve", lambda e, g=g: e.memset(vcaug[g][:], 0.0), writes=[b_vcaug[g]])
        S.op("dve", lambda e, g=g: e.memset(vcaug[g][:, 128:129], 1.0), writes=[b_vcaug[g]])
        S.dma("pool", lambda e, g=g: e.dma_start(out=vcaug[g][:, 129:161], in_=dd["ovl"]), writes=[b_vcaug[g]])
        S.op("dve", lambda e, g=g: e.memset(kcmpT[g][:], 0.0), writes=[b_kcmpT[g]])

    return dict(pm32=pm32, b_pm=b_pm, cosT=cosT, b_cos=b_cos, sinT=sinT, b_sin=b_sin, wedge=wedge, b_wedge=b_wedge, wdiag=wdiag, b_wdiag=b_wdiag, wctx=wctx, b_wctx=b_wctx, cmpb=cmpb, b_cmpb=b_cmpb, tka=tka, b_tka=b_tka, tkb=tkb, b_tkb=b_tkb, tkv=tkv, b_tkv=b_tkv, ekt=ekt, b_ekt=b_ekt, wgate16=wgate16, b_wgate=b_wgate, vcaug=vcaug, b_vcaug=b_vcaug, kcmpT=kcmpT, b_kcmpT=b_kcmpT, sig=sig, b_sig=b_sig)


def NSA_phase(nc, S, pn, sb, ps_f, ps_b, xT16, b_xT, mixT, b_mixT, id16, b_id16, id32, b_id32, dump, dd, NC_):
    (pm32, b_pm, cosT, b_cos, sinT, b_sin, wedge, b_wedge, wdiag, b_wdiag, wctx, b_wctx, cmpb, b_cmpb, tka, b_tka, tkb, b_tkb, tkv, b_tkv, ekt, b_ekt, wgate16, b_wgate, vcaug, b_vcaug, kcmpT, b_kcmpT, sig, b_sig) = [NC_[k] for k in ('pm32', 'b_pm', 'cosT', 'b_cos', 'sinT', 'b_sin', 'wedge', 'b_wedge', 'wdiag', 'b_wdiag', 'wctx', 'b_wctx', 'cmpb', 'b_cmpb', 'tka', 'b_tka', 'tkb', 'b_tkb', 'tkv', 'b_tkv', 'ekt', 'b_ekt', 'wgate16', 'b_wgate', 'vcaug', 'b_vcaug', 'kcmpT', 'b_kcmpT', 'sig', 'b_sig')]
    def load_blk(ring, bi, src):
        t, b = ring[bi[0] % len(ring)]
        bi[0] += 1
        S.dma("pool", lambda e: e.dma_start(out=t[:].rearrange("p a b -> p (a b)"), in_=src), writes=[b])
        return t, b

    def proj_fm(wt, wb, tbs, evac):
        for tb in tbs:
            pt, pb = ps_f()
            for kc in range(16):
                S.op("pe", lambda e, kc=kc, pt=pt, tb=tb: e.matmul(pt[:], lhsT=wt[:, kc, :], rhs=xT16[:, kc, tb * 512:(tb + 1) * 512],
                                                                   start=(kc == 0), stop=(kc == 15)), reads=[wb, b_xT[kc][tb]], writes=[pb])
            evac(tb, pt, pb)

    for qt in range(8):
        pt, pb = ps_f()
        for kc in range(16):
            S.op("pe", lambda e, kc=kc, pt=pt, qt=qt: e.matmul(pt[:, 0:24], lhsT=xT16[:, kc, NO + qt * 128:NO + (qt + 1) * 128], rhs=wgate16[:, kc, :],
                                                               start=(kc == 0), stop=(kc == 15)), reads=[b_wgate, b_xT[kc][2 + qt // 4]], writes=[pb])
        S.op("act", lambda e, pt=pt, qt=qt: e.activation(out=sig[:, qt, :], in_=pt[:, 0:24], func=AF.Sigmoid), reads=[pb], writes=[b_sig])

    with ExitStack() as p1:
        kvT = [[sb(p1, "kvT%d_%d" % (ty, g), [128, NB], BF16) for g in range(2)] for ty in range(2)]
        b_kvT = [[Buf("kvT%d_%d" % (ty, g)) for g in range(2)] for ty in range(2)]
        w1c16 = [sb(p1, "w1c16_%d" % ty, [128, 32, 256], BF16) for ty in range(2)]
        b_w1c = [Buf("w1c0"), Buf("w1c1")]
        w2c16 = [sb(p1, "w2c16_%d" % ty, [128, 2, 128], BF16) for ty in range(2)]
        b_w2c = [Buf("w2c0"), Buf("w2c1")]
        posc16 = [sb(p1, "posc16_%d" % ty, [128, 32], BF16) for ty in range(2)]
        b_posc = [Buf("posc0"), Buf("posc1")]
        ring = [(sb(p1, "wnb1_%d" % i, [128, 16, 128], BF16), Buf("wnb1_%d" % i)) for i in range(2)]
        bi = [0]
        pbias = sb(p1, "pbias", [128, 4], F32)
        b_pbias = Buf("pbias")
        xh = sb(p1, "xh", [128, 127], F32); b_xh = Buf("xh")
        x2 = sb(p1, "x2", [128, 127], F32); b_x2 = Buf("x2")
        sg = sb(p1, "sgc", [128, 127], F32); b_sg = Buf("sgc")
        hid = sb(p1, "hid", [128, 2, 128], BF16); b_hid = Buf("hid")
        items1 = [(g, ty) for g in range(2) for ty in range(2)]
        loaded = [load_blk(ring, bi, dd["wn"][8 * g + 6 + ty]) for g, ty in items1[:2]]
        for ty in range(2):
            S.dma("pool", lambda e, ty=ty: e.dma_start(out=w1c16[ty][:].rearrange("p a b -> p (a b)"), in_=dd["w1c"][ty]), writes=[b_w1c[ty]])
            S.dma("pool", lambda e, ty=ty: e.dma_start(out=w2c16[ty][:].rearrange("p a b -> p (a b)"), in_=dd["w2c"][ty]), writes=[b_w2c[ty]])
            S.dma("pool", lambda e, ty=ty: e.dma_start(out=posc16[ty][:], in_=dd["posc"][ty]), writes=[b_posc[ty]])
        for i1, (g, ty) in enumerate(items1):
            wt, wb = loaded[i1]
            dst, bd = kvT[ty][g], b_kvT[ty][g]
            proj_fm(wt, wb, range(4), lambda tb, pt, pb, dst=dst, bd=bd: S.op(
                "act", lambda e: e.copy(out=dst[:, tb * 512:(tb + 1) * 512], in_=pt[:]), reads=[pb], writes=[bd]))
            if i1 + 2 < len(items1):
                g2, ty2 = items1[i1 + 2]
                loaded.append(load_blk(ring, bi, dd["wn"][8 * g2 + 6 + ty2]))
        pt, pb = ps_f()
        for ty in range(2):
            for cc in range(2):
                for l in range(32):
                    S.op("pe", lambda e, ty=ty, cc=cc, l=l, pt=pt: e.matmul(
                        pt[:, ty * 2 + cc:ty * 2 + cc + 1], lhsT=w1c16[ty][:, l, cc * 128:(cc + 1) * 128], rhs=posc16[ty][:, l:l + 1],
                        start=(l == 0), stop=(l == 31)), reads=[b_w1c[ty], b_posc[ty]], writes=[pb])
        S.op("act", lambda e, pt=pt: e.copy(out=pbias[:], in_=pt[:, 0:4]), reads=[pb], writes=[b_pbias])
        S.op("dve", lambda e: e.memset(hid[:], 0.0), writes=[b_hid])
        for g in range(2):
            for ty in range(2):
                src, bs = kvT[ty][g], b_kvT[ty][g]
                for cc in range(2):
                    pt, pb = ps_f()
                    for l in range(32):
                        S.op("pe", lambda e, ty=ty, cc=cc, l=l, pt=pt, src=src: e.matmul(
                            pt[:, 0:127], lhsT=w1c16[ty][:, l, cc * 128:(cc + 1) * 128], rhs=src[:, l:l + 16 * 126 + 1:16],
                            start=(l == 0), stop=(l == 31)), reads=[b_w1c[ty], bs], writes=[pb])
                    S.op("act", lambda e, pt=pt, ty=ty, cc=cc: e.activation(out=xh[:], in_=pt[:, 0:127], func=AF.Identity,
                                                                          bias=pbias[:, ty * 2 + cc:ty * 2 + cc + 1]),
                         reads=[pb, b_pbias], writes=[b_xh])
                    S.op("dve", lambda e: e.tensor_tensor(out=x2[:], in0=xh[:], in1=xh[:], op=ALU.mult), reads=[b_xh], writes=[b_x2])
                    S.op("dve", lambda e: e.tensor_scalar(out=x2[:], in0=x2[:], scalar1=0.044715, scalar2=1.0, op0=ALU.mult, op1=ALU.add),
                         reads=[b_x2], writes=[b_x2])
                    S.op("dve", lambda e: e.tensor_tensor(out=x2[:], in0=x2[:], in1=xh[:], op=ALU.mult), reads=[b_x2, b_xh], writes=[b_x2])
                    S.op("act", lambda e: e.activation(out=sg[:], in_=x2[:], func=AF.Sigmoid, scale=1.5957691216057308),
                         reads=[b_x2], writes=[b_sg])
                    S.op("dve", lambda e, cc=cc: e.tensor_tensor(out=hid[:, cc, 0:127], in0=xh[:], in1=sg[:], op=ALU.mult),
                         reads=[b_xh, b_sg], writes=[b_hid])
                pt, pb = ps_f()
                if ty == 0:
                    for cc in range(2):
                        S.op("pe", lambda e, cc=cc, pt=pt: e.matmul(pt[:, 0:127], lhsT=w2c16[0][:, cc, :], rhs=hid[:, cc, 0:127],
                                                                    start=(cc == 0), stop=(cc == 1)), reads=[b_w2c[0], b_hid], writes=[pb])
                    S.op("act", lambda e, pt=pt, g=g: e.copy(out=kcmpT[g][:, 0:127], in_=pt[:, 0:127]), reads=[pb], writes=[b_kcmpT[g]])
                else:
                    for cc in range(2):
                        S.op("pe", lambda e, cc=cc, pt=pt: e.matmul(pt[0:127, 0:128], lhsT=hid[:, cc, 0:127], rhs=w2c16[1][:, cc, :],
                                                                    start=(cc == 0), stop=(cc == 1)), reads=[b_w2c[1], b_hid], writes=[pb])
                    S.op("act", lambda e, pt=pt, g=g: e.copy(out=vcaug[g][0:127, 0:128], in_=pt[0:127, 0:128]), reads=[pb], writes=[b_vcaug[g]])
        S.barrier()
        S.emit()

    qraw = sb(pn, "qraw", [128, 4, NO], BF16); b_qraw = [Buf("qraw%d" % r) for r in range(4)]
    qrope = sb(pn, "qrope", [128, 4, NO], BF16); b_qrope = [Buf("qrope%d" % r) for r in range(4)]
    ksT = sb(pn, "ksT", [128, NB], BF16); b_ksT = Buf("ksT")
    kwT = sb(pn, "kwT", [128, NB], BF16); b_kwT = Buf("kwT")
    vsaug = sb(pn, "vsaug", [128, 16, 129], BF16); b_vs = Buf("vsaug")
    vwaug = sb(pn, "vwaug", [128, 16, 129], BF16); b_vw = Buf("vwaug")
    S.op("dve", lambda e: e.memset(vsaug[:, :, 128:129], 1.0), writes=[b_vs])
    S.op("dve", lambda e: e.memset(vwaug[:, :, 128:129], 1.0), writes=[b_vw])
    for g in range(2):
        with ExitStack() as pa:
            ring = [(sb(pa, "wnb2_%d" % i, [128, 16, 128], BF16), Buf("wnb2_%d" % i)) for i in range(3)]
            bi = [0]
            wv16 = sb(pa, "wv16", [128, 16, 256], BF16); b_wv = Buf("wv16")
            q32s = [(sb(pa, "q32_%d" % i, [32, 512], F32), Buf("q32_%d" % i)) for i in range(2)]
            t1 = sb(pa, "t1", [32, 512], F32); b_t1 = Buf("t1")
            t2 = sb(pa, "t2", [32, 512], F32); b_t2 = Buf("t2")
            rope_pending = []
            rope_ctr = [0]
            S.dma("pool", lambda e, g=g: e.dma_start(out=wv16[:].rearrange("p a b -> p (a b)"), in_=dd["wv"][g]), writes=[b_wv])

            def rope_evac(tb, pt, pb, dst, bd, raw=None):
                cs = slice(tb * 512, (tb + 1) * 512)
                q32, b_q32 = q32s[rope_ctr[0] % 2]
                rope_ctr[0] += 1
                while rope_pending:
                    rope_pending.pop(0)()
                if raw is not None:
                    rdst, rb = raw
                    S.op("act", lambda e: e.copy(out=rdst, in_=pt[:]), reads=[pb], writes=[rb])
                S.op("act", lambda e: e.copy(out=q32[:], in_=pt[0:32, :]), reads=[pb], writes=[b_q32])
                S.op("act", lambda e: e.copy(out=dst[32:64, :], in_=pt[32:64, :]), reads=[pb], writes=[bd])
                S.op("act", lambda e: e.copy(out=dst[64:128, :], in_=pt[64:128, :]), reads=[pb], writes=[bd])
                def part2():
                    pw, pwb = ps_f()
                    S.op("pe", lambda e: e.matmul(pw[0:32, :], lhsT=pm32[:], rhs=q32[:], start=True, stop=True), reads=[b_pm, b_q32], writes=[pwb])
                    S.op("dve", lambda e: e.tensor_tensor(out=t1[:], in0=q32[:], in1=cosT[:, cs], op=ALU.mult), reads=[b_q32, b_cos], writes=[b_t1])
                    S.op("dve", lambda e: e.tensor_tensor(out=t2[:], in0=pw[0:32, :], in1=sinT[:, cs], op=ALU.mult), reads=[pwb, b_sin], writes=[b_t2])
                    S.op("dve", lambda e: e.tensor_tensor(out=dst[0:32, :], in0=t1[:], in1=t2[:], op=ALU.add), reads=[b_t1, b_t2], writes=[bd])
                rope_pending.append(part2)

            for r in range(4):
                wt, wb = load_blk(ring, bi, dd["wn"][8 * g + r])
                proj_fm(wt, wb, (2, 3), lambda tb, pt, pb, r=r: rope_evac(
                    tb, pt, pb, qrope[:, r, (tb - 2) * 512:(tb - 1) * 512], b_qrope[r],
                    raw=(qraw[:, r, (tb - 2) * 512:(tb - 1) * 512], b_qraw[r])))
            wt, wb = load_blk(ring, bi, dd["wn"][8 * g + 4])
            proj_fm(wt, wb, range(4), lambda tb, pt, pb: rope_evac(tb, pt, pb, ksT[:, tb * 512:(tb + 1) * 512], b_ksT))
            wt, wb = load_blk(ring, bi, dd["wn"][8 * g + 5])
            proj_fm(wt, wb, range(4), lambda tb, pt, pb: rope_evac(tb, pt, pb, kwT[:, tb * 512:(tb + 1) * 512], b_kwT))
            for t in range(16):
                if t == 1:
                    while rope_pending:
                        rope_pending.pop(0)()
                pt, pb = ps_f()
                for kc in range(16):
                    S.op("pe", lambda e, kc=kc, pt=pt, t=t: e.matmul(pt[:, 0:256], lhsT=xT16[:, kc, t * 128:(t + 1) * 128], rhs=wv16[:, kc, :],
                                                                     start=(kc == 0), stop=(kc == 15)), reads=[b_wv, b_xT[kc][t // 4]], writes=[pb])
                S.op("act", lambda e, pt=pt, t=t: e.copy(out=vsaug[:, t, 0:128], in_=pt[:, 0:128]), reads=[pb], writes=[b_vs])
                S.op("act", lambda e, pt=pt, t=t: e.copy(out=vwaug[:, t, 0:128], in_=pt[:, 128:256]), reads=[pb], writes=[b_vw])
            S.barrier()
            S.emit()
        with ExitStack() as pb_:
            PT = [(sb(pb_, "PT%d" % i, [128, 512], BF16), Buf("PT%d" % i)) for i in range(3)]
            pti = [0]
            PTw = [(sb(pb_, "PTw%d" % i, [128, 5, 128], BF16), Buf("PTw%d" % i)) for i in range(2)]
            ptwi = [0]
            selbT = sb(pb_, "selbT", [128, NO], BF16); b_selbT = [Buf("selbT%d" % q) for q in range(8)]
            S.op("dve", lambda e: e.memset(selbT[:], 0.0), writes=b_selbT)
            onsa = sb(pb_, "onsa", [128, 8, 4, 128], F32); b_onsa = [[Buf("onsa%d_%d" % (q, r)) for r in range(4)] for q in range(8)]
            imp = sb(pb_, "imp", [128, 8, 32], F32); b_imp = [Buf("imp%d" % q) for q in range(8)]
            cmp3s = [(sb(pb_, "cmp3_%d" % i, [128, 32, 32], F32), Buf("cmp3_%d" % i)) for i in range(2)]
            rk = sb(pb_, "rk", [128, 8, 32], F32); b_rk = [Buf("rk%d" % q) for q in range(8)]
            sm = [(sb(pb_, "sm%d" % i, [128, 4], F32), Buf("sm%d" % i)) for i in range(4)]
            smi = [0]
            o16 = [(sb(pb_, "o16_%d" % i, [128, 4, 128], BF16), Buf("o16_%d" % i)) for i in range(2)]
            wstg = [(sb(pb_, "wstg%d" % i, [128, 129], F32), Buf("wstg%d" % i)) for i in range(12)]
            wsi = [0]
            zz = sb(pb_, "zz", [128, 258], BF16); b_zz = Buf("zz")
            S.op("dve", lambda e: e.memset(zz[:], 0.0), writes=[b_zz])

            def finalize(pacc, pab, col0, qt, r, branch, first):
                s, bs = sm[smi[0] % 4]
                smi[0] += 1
                S.op("dve", lambda e: e.reciprocal(out=s[:, 1:2], in_=pacc[:, col0 + 128:col0 + 129]), reads=[pab], writes=[bs])
                gcol = (4 * g + r) * 3 + branch
                S.op("dve", lambda e: e.tensor_tensor(out=s[:, 2:3], in0=s[:, 1:2], in1=sig[:, qt, gcol:gcol + 1], op=ALU.mult),
                     reads=[bs, b_sig], writes=[bs])
                dst = onsa[:, qt, r, :]
                if first:
                    S.op("dve", lambda e: e.tensor_scalar(out=dst, in0=pacc[:, col0:col0 + 128], scalar1=s[:, 2:3], scalar2=None, op0=ALU.mult),
                         reads=[pab, bs], writes=[b_onsa[qt][r]])
                else:
                    S.op("dve", lambda e: e.scalar_tensor_tensor(out=dst, in0=pacc[:, col0:col0 + 128], scalar=s[:, 2:3], in1=dst,
                                                                 op0=ALU.mult, op1=ALU.add), reads=[pab, bs, b_onsa[qt][r]], writes=[b_onsa[qt][r]])
                return s, bs

            def cmp_scores(qb, r):
                qs = slice(qb * 512, (qb + 1) * 512)
                pt, pb = ps_f()
                S.op("pe", lambda e: e.matmul(pt[:], lhsT=kcmpT[g][:], rhs=qraw[:, r, qs], start=True, stop=False),
                     reads=[b_kcmpT[g], b_qraw[r]], writes=[pb])
                S.op("pe", lambda e: e.matmul(pt[:], lhsT=id16[:], rhs=cmpb[:, qs], start=False, stop=True),
                     reads=[b_id16, b_cmpb], writes=[pb])
                P, bP = PT[pti[0] % 3]
                pti[0] += 1
                S.op("act", lambda e: e.activation(out=P[:], in_=pt[:], func=AF.Exp, scale=SCALE), reads=[pb], writes=[bP])
                return P, bP

            def cmp_pv(qb, r, P, bP):
                for j in range(4):
                    qt = qb * 4 + j
                    po, pob = ps_f()
                    S.op("pe", lambda e, po=po, j=j: e.matmul(po[:, 0:161], lhsT=P[:, j * 128:(j + 1) * 128], rhs=vcaug[g][:],
                                                              start=True, stop=True), reads=[bP, b_vcaug[g]], writes=[pob])
                    s, bs = finalize(po, pob, 0, qt, r, 0, True)
                    if r == 0:
                        S.op("dve", lambda e, po=po, s=s, qt=qt: e.tensor_scalar(out=imp[:, qt, :], in0=po[:, 129:161], scalar1=s[:, 1:2],
                                                                                scalar2=None, op0=ALU.mult), reads=[pob, bs], writes=[b_imp[qt]])
                    else:
                        S.op("dve", lambda e, po=po, s=s, qt=qt: e.scalar_tensor_tensor(
                            out=imp[:, qt, :], in0=po[:, 129:161], scalar=s[:, 1:2], in1=imp[:, qt, :], op0=ALU.mult, op1=ALU.add),
                            reads=[pob, bs, b_imp[qt]], writes=[b_imp[qt]])

            def topk_dve(qts):
                for qt in qts:
                    iv = imp[:, qt, :]
                    rq = rk[:, qt, :]
                    cmp3, b_cmp3 = cmp3s[qt % 2]
                    S.op("dve", lambda e, iv=iv, qt=qt: e.tensor_tensor(out=iv, in0=iv, in1=tka[:, qt, :], op=ALU.mult), reads=[b_imp[qt], b_tka], writes=[b_imp[qt]])
                    S.op("dve", lambda e, iv=iv, qt=qt: e.tensor_tensor(out=iv, in0=iv, in1=tkb[:, qt, :], op=ALU.add), reads=[b_imp[qt], b_tkb], writes=[b_imp[qt]])
                    S.op("dve", lambda e, iv=iv, cmp3=cmp3: e.tensor_tensor(out=cmp3[:], in0=iv.unsqueeze(1).to_broadcast([128, 32, 32]),
                                                                 in1=iv.unsqueeze(2).to_broadcast([128, 32, 32]), op=ALU.is_gt),
                         reads=[b_imp[qt]], writes=[b_cmp3])
                    S.op("dve", lambda e, rq=rq, cmp3=cmp3: e.tensor_reduce(out=rq, in_=cmp3[:], axis=AX.X, op=ALU.add), reads=[b_cmp3], writes=[b_rk[qt]])
                    S.op("dve", lambda e, rq=rq: e.tensor_scalar(out=rq, in0=rq, scalar1=15.5, scalar2=None, op0=ALU.is_lt), reads=[b_rk[qt]], writes=[b_rk[qt]])
                    S.op("dve", lambda e, rq=rq, qt=qt: e.tensor_tensor(out=rq, in0=rq, in1=tkv[:, qt, :], op=ALU.mult), reads=[b_rk[qt], b_tkv], writes=[b_rk[qt]])
                    S.op("dve", lambda e, rq=rq: e.tensor_scalar(out=rq, in0=rq, scalar1=-NEG, scalar2=NEG, op0=ALU.mult, op1=ALU.add),
                         reads=[b_rk[qt]], writes=[b_rk[qt]])


            def topk_transposes(qts):
                for qt in qts:
                    pt, pb = ps_f()
                    S.op("pe", lambda e, pt=pt, qt=qt: e.transpose(pt[0:32, 0:128], rk[:, qt, :], id32[:]), reads=[b_rk[qt], b_id32], writes=[pb])
                    S.op("act", lambda e, pt=pt, qt=qt: e.copy(out=selbT[0:32, qt * 128:(qt + 1) * 128], in_=pt[0:32, 0:128]), reads=[pb], writes=[b_selbT[qt]])

            def win_scores(qt, r):
                pw1, pw1b = ps_f()
                pw2, pw2b = ps_f()
                for m in range(5):
                    kt = 4 + qt + m
                    tgt, tb_ = (pw1, pw1b) if m < 4 else (pw2, pw2b)
                    cs = slice((m % 4) * 128, (m % 4 + 1) * 128)
                    extra = []
                    if m == 0:
                        extra.append((wedge, b_wedge))
                    if m == 4:
                        extra.append((wdiag, b_wdiag))
                    if kt < 8:
                        extra.append((wctx, b_wctx))
                    S.op("pe", lambda e, tgt=tgt, cs=cs, kt=kt, ne=len(extra): e.matmul(
                        tgt[:, cs], lhsT=kwT[:, kt * 128:(kt + 1) * 128], rhs=qrope[:, r, qt * 128:(qt + 1) * 128], start=True, stop=(ne == 0)),
                        reads=[b_kwT, b_qrope[r]], writes=[tb_])
                    for xi, (xt, xb) in enumerate(extra):
                        S.op("pe", lambda e, tgt=tgt, cs=cs, xt=xt, xi=xi, ne=len(extra): e.matmul(
                            tgt[:, cs], lhsT=id16[:], rhs=xt[:], start=False, stop=(xi == ne - 1)), reads=[b_id16, xb], writes=[tb_])
                Pw, bPw = PTw[ptwi[0] % 2]
                ptwi[0] += 1
                S.op("act", lambda e: e.activation(out=Pw[:, 0:4, :], in_=pw1[:].rearrange("p (a b) -> p a b", a=4), func=AF.Exp, scale=SCALE),
                     reads=[pw1b], writes=[bPw])
                S.op("act", lambda e: e.activation(out=Pw[:, 4, :], in_=pw2[:, 0:128], func=AF.Exp, scale=SCALE),
                     reads=[pw2b], writes=[bPw])
                return Pw, bPw

            def win_pv(qt, r, Pw, bPw):
                po, pob = ps_f()
                for m in range(5):
                    kt = 4 + qt + m
                    S.op("pe", lambda e, m=m, kt=kt: e.matmul(po[:, 0:129], lhsT=Pw[:, m, :], rhs=vwaug[:, kt, :],
                                                              start=(m == 0), stop=(m == 4)), reads=[bPw, b_vw], writes=[pob])
                stg, bstg = wstg[wsi[0] % len(wstg)]
                wsi[0] += 1
                S.op("act", lambda e: e.copy(out=stg[:], in_=po[:, 0:129]), reads=[pob], writes=[bstg])
                finalize(stg, bstg, 0, qt, r, 2, False)

            cmp_list = [(qb, r) for qb in range(2) for r in range(4)]
            win_list = [(4 * qb + j, r) for qb in range(2) for r in range(4) for j in range(4)] if 'win' in NSA_BR else []
            wst = {"i": 0, "prev": None}

            def win_step():
                if wst["i"] >= len(win_list):
                    return
                qt, r = win_list[wst["i"]]
                wst["i"] += 1
                cur = (qt, r) + win_scores(qt, r)
                if wst["prev"] is not None:
                    win_pv(*wst["prev"])
                wst["prev"] = cur

            cprev = None
            for r in range(4):
                cur = (0, r) + cmp_scores(0, r)
                if cprev is not None:
                    cmp_pv(*cprev)
                cprev = cur
            cmp_pv(*cprev)
            topk_dve(range(4))
            cprev = None
            for r in range(4):
                cur = (1, r) + cmp_scores(1, r)
                if cprev is not None:
                    cmp_pv(*cprev)
                cprev = cur
                for _ in range(4):
                    win_step()
            cmp_pv(*cprev)
            if g == 0:
                dump("imp", imp[:], b_imp)
            while wst["i"] < len(win_list):
                win_step()
            if wst["prev"] is not None:
                win_pv(*wst["prev"])
            topk_transposes(range(4))
            topk_dve(range(4, 8))

            def slc_scores(qb, r, kt):
                qs = slice(qb * 512, (qb + 1) * 512)
                m = kt - (8 + 4 * qb)
                pt, pb = ps_f(lo=4)
                S.op("pe", lambda e: e.matmul(pt[:], lhsT=ksT[:, kt * 128:(kt + 1) * 128], rhs=qrope[:, r, qs],
                                              start=True, stop=False), reads=[b_ksT, b_qrope[r]], writes=[pb])
                if 0 <= m <= 3:
                    S.op("pe", lambda e: e.matmul(pt[:, m * 128:(m + 1) * 128], lhsT=id16[:], rhs=wdiag[:], start=False, stop=False),
                         reads=[b_id16, b_wdiag], writes=[pb])
                S.op("pe", lambda e: e.matmul(pt[:], lhsT=ekt[:, kt, :], rhs=selbT[:, qs], start=False, stop=True),
                     reads=[b_ekt] + b_selbT[qb * 4:qb * 4 + 4], writes=[pb])
                P, bP = PT[pti[0] % 3]
                pti[0] += 1
                S.op("act", lambda e: e.activation(out=P[:], in_=pt[:], func=AF.Exp, scale=SCALE), reads=[pb], writes=[bP])
                return P, bP

            def slc_pv(qb, kt, acc, P, bP):
                for j in range(4):
                    last = 8 + 4 * qb + j
                    if kt > last:
                        continue
                    pa_, pab = acc[j // 2]
                    c0 = (j % 2) * 129
                    S.op("pe", lambda e, pa_=pa_, c0=c0, j=j, last=last: e.matmul(
                        pa_[:, c0:c0 + 129], lhsT=P[:, j * 128:(j + 1) * 128], rhs=vsaug[:, kt, :], start=False, stop=(kt == last and j % 2 == 1)),
                        reads=[bP, b_vs], writes=[pab])

            it_ = 0
            for qb in (range(2) if 'slc' in NSA_BR else ()):
                nkt = 8 + 4 * qb + 4
                if qb == 1:
                    topk_transposes(range(4, 8))
                for r in range(4):
                    acc = [ps_f(fixed=2 * (it_ % 2)), ps_f(fixed=2 * (it_ % 2) + 1)]
                    it_ += 1
                    for pa_, pab in acc:
                        S.op("pe", lambda e, pa_=pa_: e.matmul(pa_[:, 0:258], lhsT=zz[:, 0:128], rhs=zz[:, 0:258], start=True, stop=False),
                             reads=[b_zz], writes=[pab])
                    prev = None
                    for kt in range(nkt):
                        cur = (kt,) + slc_scores(qb, r, kt)
                        if prev is not None:
                            slc_pv(qb, prev[0], acc, prev[1], prev[2])
                        prev = cur
                    slc_pv(qb, prev[0], acc, prev[1], prev[2])
                    for j in range(4):
                        pa_, pab = acc[j // 2]
                        finalize(pa_, pab, (j % 2) * 129, qb * 4 + j, r, 1, False)
            if g == 0:
                dump("onsa", onsa[:], [b for rr_ in b_onsa for b in rr_])
            for qt in range(8):
                o, bo = o16[qt % 2]
                S.op("act", lambda e, o=o, qt=qt: e.copy(out=o[:], in_=onsa[:, qt, :, :]), reads=b_onsa[qt], writes=[bo])
                pT, bT = ps_b()
                for r in range(4):
                    S.op("pe", lambda e, pT=pT, o=o, r=r: e.transpose(pT[:, r * 128:(r + 1) * 128], o[:, r, :], id16[:]), reads=[bo, b_id16], writes=[bT])
                S.op("dve", lambda e, pT=pT, qt=qt: e.tensor_copy(out=mixT[:, 8 + 4 * g:12 + 4 * g, qt * 128:(qt + 1) * 128],
                                                                  in_=pT[:, 0:512].rearrange("p (a b) -> p a b", a=4)),
                     reads=[bT], writes=[b_mixT[8 + 4 * g + r][qt] for r in range(4)])
            S.barrier()
            S.emit()


def _kc_layout(w):
    C = w.shape[1]
    return np.ascontiguousarray(w.reshape(16, 128, C).transpose(1, 0, 2)).reshape(128, 16 * C)


def prep_shared(inp):
    w_in = inp["w_in"][0]
    sh = {}
    wg = []
    for h in range(4):
        cols = np.concatenate([w_in[:, O_GQ + h * 128:O_GQ + (h + 1) * 128], w_in[:, O_GK + h * 128:O_GK + (h + 1) * 128],
                               w_in[:, O_GV + h * 256:O_GV + (h + 1) * 256], w_in[:, O_GO + h * 256:O_GO + (h + 1) * 256]], axis=1)
        wg.append(_kc_layout(cols))
    sh["wg"] = np.stack(wg)
    sh["wglr"] = _kc_layout(w_in[:, O_GLR:O_GLR + 16])
    sh["w2aug"] = np.concatenate([inp["gla_gate_w2"][0], inp["gla_gate_b2"][0][None, :]], axis=0).astype(np.float32)
    sh["normw"] = np.ascontiguousarray(np.broadcast_to(inp["gla_norm_w"][0][None, :], (128, 256))).astype(np.float32)
    blocks = []
    for g in range(2):
        for r in range(4):
            hh = 4 * g + r
            blocks.append(w_in[:, O_NQ + hh * 128:O_NQ + (hh + 1) * 128])
        blocks.append(w_in[:, O_KS + g * 128:O_KS + (g + 1) * 128])
        blocks.append(w_in[:, O_KW + g * 128:O_KW + (g + 1) * 128])
        blocks.append(w_in[:, O_KC + g * 128:O_KC + (g + 1) * 128])
        blocks.append(w_in[:, O_VC + g * 128:O_VC + (g + 1) * 128])
    sh["wn"] = np.stack([_kc_layout(b) for b in blocks])
    sh["wv"] = np.stack([_kc_layout(np.concatenate([w_in[:, O_VS + g * 128:O_VS + (g + 1) * 128],
                                                     w_in[:, O_VW + g * 128:O_VW + (g + 1) * 128]], axis=1)) for g in range(2)])
    sh["wgate"] = _kc_layout(w_in[:, O_GATE:O_GATE + 24])
    w1c = []
    for nm in ("cmp_k_w1", "cmp_v_w1"):
        w1 = inp[nm][0]
        w1c.append(np.ascontiguousarray(w1.reshape(32, 128, 256).transpose(1, 0, 2)).reshape(128, 32 * 256))
    sh["w1c"] = np.stack(w1c)
    w2c = []
    for nm in ("cmp_k_w2", "cmp_v_w2"):
        w2 = inp[nm][0]
        w2c.append(np.ascontiguousarray(w2.reshape(2, 128, 128).transpose(1, 0, 2)).reshape(128, 256))
    sh["w2c"] = np.stack(w2c)
    sh["posc"] = np.stack([np.ascontiguousarray(inp["cmp_k_pos"][0].T), np.ascontiguousarray(inp["cmp_v_pos"][0].T)])
    w_out = inp["w_out"][0]
    sh["wout"] = np.stack([_kc_layout(w_out[:, cb * 512:(cb + 1) * 512]) for cb in range(4)])
    w1 = inp["ffn_w1"][0]
    w3 = inp["ffn_w3"][0]
    sh["w1t"] = np.ascontiguousarray(w1.reshape(16, 128, NFC, 128).transpose(2, 1, 0, 3)).reshape(NFC, 128, 16 * 128)
    sh["w3t"] = np.ascontiguousarray(w3.reshape(16, 128, NFC, 128).transpose(2, 1, 0, 3)).reshape(NFC, 128, 16 * 128)
    w2 = inp["ffn_w2"][0]
    sh["w2t"] = np.ascontiguousarray(w2.reshape(11, 4, 128, 4, 512).transpose(3, 0, 2, 1, 4)).reshape(4, 11, 128, 4 * 512)
    ln = np.stack([inp["ln1_g"][0], inp["ln1_b"][0], inp["ln2_g"][0], inp["ln2_b"][0]])
    sh["ln"] = np.ascontiguousarray(np.broadcast_to(ln[:, None, :], (4, 128, D))).astype(np.float32)
    sh["ident"] = np.eye(128, dtype=np.float32)
    j = np.arange(128)[:, None]
    i = np.arange(128)[None, :]
    same = (j // 64) == (i // 64)
    sh["umat"] = ((j <= i) & same).astype(np.float32)
    sh["lmat"] = ((j > i) & same).astype(np.float32)
    sh["cind"] = np.stack([(np.arange(128) < 64), (np.arange(128) >= 64)], axis=1).astype(np.float32)
    pm = np.zeros((32, 32), np.float32)
    for m in range(32):
        pm[(m + 16) % 32, m] = 1.0
    sh["pm"] = pm
    sh["wedge"] = np.where(j > i, 0.0, NEG).astype(np.float32)
    sh["wdiag"] = np.where(j <= i, 0.0, NEG).astype(np.float32)
    n = np.arange(128)[:, None]
    blk = np.arange(32)[None, :]
    ovl = ((16 * n < 64 * blk + 64) & (64 * blk < 16 * n + 32)).astype(np.float32)
    ovl[127] = 0.0
    sh["ovl"] = ovl
    ekt = np.zeros((128, 16, 128), np.float32)
    for kt in range(16):
        for jj in range(128):
            ekt[2 * kt + jj // 64, kt, jj] = 1.0
    sh["ekt"] = ekt.reshape(128, 16 * 128)
    return sh


def prep_core(inp, b, half):
    x = inp["x"]
    own = x[b, half * NO:(half + 1) * NO]
    ctx = x[b, 0:NO] if half == 1 else np.zeros((NO, D), np.float32)
    xbuf = np.concatenate([ctx, own], axis=0)
    pc = {}
    pc["xT"] = np.ascontiguousarray(xbuf.T.reshape(16, 128, NB).transpose(1, 0, 2))
    pc["xo"] = np.ascontiguousarray(own.reshape(8, 128, D))
    off = 0 if half == 1 else -NO
    pos = (np.arange(NB) + off).astype(np.float32)
    inv = np.power(np.float32(500000.0), -np.arange(0, 32, 2, dtype=np.float32) / np.float32(32))
    ang = pos[None, :] * inv[:, None]
    c, s = np.cos(ang), np.sin(ang)
    pc["cosT"] = np.concatenate([c, c], axis=0).astype(np.float32)
    pc["sinT"] = np.concatenate([-s, s], axis=0).astype(np.float32)
    pc["wctx"] = np.full((128, 128), NEG if half == 0 else 0.0, np.float32)
    n = np.arange(128)[:, None]
    q = np.arange(NO)[None, :]
    t_true = half * NO + q
    n_true = n + (0 if half == 1 else -64)
    valid = (n_true >= 0) & (n < 127) & (16 * n_true + 31 <= t_true)
    pc["cmpb"] = np.where(valid, 0.0, NEG).astype(np.float32)
    pc["cmpb"][127, :] = -780.0
    qq = np.arange(NO)
    t_true = half * NO + qq
    cur = t_true // 64
    jb = np.arange(32)[None, :]
    jt = jb + (0 if half == 1 else -16)
    val = (jt >= 0) & (jt <= cur[:, None])
    forced = val & ((jt == 0) | (jt == cur[:, None]) | (jt == cur[:, None] - 1))
    A = (val & ~forced).astype(np.float32)
    Bt = forced.astype(np.float32) * 1e4 + (1.0 - val.astype(np.float32)) * (-1e4)
    def tl(a):
        return np.ascontiguousarray(a.reshape(8, 128, 32).transpose(1, 0, 2)).reshape(128, 8 * 32).astype(np.float32)
    pc["tka"], pc["tkb"], pc["tkv"] = tl(A), tl(Bt), tl(val.astype(np.float32))
    return pc


_NC_CACHE = {}


def kernel(**inputs):
    inp = {k: np.asarray(v) for k, v in inputs.items()}
    sh = prep_shared(inp)
    in_maps = []
    for c in range(8):
        m = dict(sh)
        m.update(prep_core(inp, c // 2, c % 2))
        in_maps.append(m)
    if "nc" not in _NC_CACHE:
        _NC_CACHE["nc"] = build_nc()
    nc = _NC_CACHE["nc"]
    res = run_bass_kernel_spmd(nc, in_maps, core_ids=list(range(8)))
    out = np.zeros((4, SEQ, D), np.float32)
    for c in range(8):
        b, half = c // 2, c % 2
        out[b, half * NO:(half + 1) * NO] = res.results[c]["out"].reshape(NO, D)
    return out
```
